# Optimizing a Trainium2 kernel written in Bass

```python
import math, functools
import jax, jax.numpy as jnp
from jax import lax
import numpy as np

D_MODEL = 1024
BATCH = 8
SEQ = 2048
DEPTH = 4
DEC_BATCH = 32
DEC_SEQ = 4
PAST_LEN = 8192
PAGE_SIZE = 128

D_MIX = D_MODEL
D_GROUP = D_MIX // 4
D_A = D_GROUP
CONV_W = 31
D_B = D_GROUP
CHUNK = 128
SGU_HEADS = 4
SGU_HD = D_B // SGU_HEADS
H_C = 4
DK_C = D_GROUP // (2 * H_C)
DV_C = 2 * DK_C
D_C = H_C * DV_C
Q_BLOCK = 128
D_D = D_GROUP
POOL_WINDOWS = (2, 4, 8, 16)
POOL_GROUPS = len(POOL_WINDOWS)
POOL_GD = D_D // POOL_GROUPS
POOL_MAX = max(POOL_WINDOWS)
D_FF = -(-8 * D_MODEL // (3 * 256)) * 256
QK_C = H_C * 2 * DK_C
IN_WIDTHS = (D_A, D_A, D_B, D_B, QK_C, QK_C, D_C, D_D)
D_IN = sum(IN_WIDTHS)
IN_SPLITS = tuple(int(s) for s in np.cumsum(IN_WIDTHS)[:-1])
RMS_EPS = 1e-6
LN_EPS = 1e-5

kernel_name = "hybrid_parallel_groups_decoder_step"


def rms_norm(x, g, eps=RMS_EPS):
    xf = x.astype(jnp.float32)
    y = xf * lax.rsqrt(jnp.mean(xf * xf, axis=-1, keepdims=True) + eps)
    return (y * g.astype(jnp.float32)).astype(x.dtype)


def layer_norm(x, g, b, eps=LN_EPS):
    xf = x.astype(jnp.float32)
    mu = jnp.mean(xf, axis=-1, keepdims=True)
    var = jnp.mean(jnp.square(xf - mu), axis=-1, keepdims=True)
    y = (xf - mu) * lax.rsqrt(var + eps) * g.astype(jnp.float32) + b.astype(jnp.float32)
    return y.astype(x.dtype)


def conformer_conv(a, gate, prev, w_dw, b_dw, ln_g, ln_b):
    h = a * jax.nn.sigmoid(gate)
    ext = jnp.concatenate([prev.astype(h.dtype), h], axis=1)
    y = lax.conv_general_dilated(ext, w_dw[:, None, :].astype(h.dtype), window_strides=(1,),
                                 padding='VALID', dimension_numbers=('NWC', 'WIO', 'NWC'),
                                 feature_group_count=D_A) + b_dw
    y = jax.nn.silu(layer_norm(y, ln_g, ln_b))
    return y, ext[:, -(CONV_W - 1):]


def chunk_sgu(u, v, w_s, b_s, ln_g, ln_b):
    v = layer_norm(v, ln_g, ln_b)
    B_, T, _ = v.shape
    L = min(T, CHUNK)
    n = T // L
    vc = v.reshape(B_, n, L, SGU_HEADS, SGU_HD)
    causal = jnp.tril(jnp.ones((L, L), dtype=bool))
    w = jnp.where(causal[None], w_s[:, :L, :L], 0.0)
    mixed = jnp.einsum('gts,bnsgc->bntgc', w, vc) + b_s[:, :L].T[None, None, :, :, None]
    return u * mixed.reshape(B_, T, D_B), v


def diff_lambda(lq1, lk1, lq2, lk2, lam_init):
    f32 = jnp.float32
    return (jnp.exp(jnp.sum(lq1.astype(f32) * lk1.astype(f32)))
            - jnp.exp(jnp.sum(lq2.astype(f32) * lk2.astype(f32))) + lam_init)


def diff_attend(q, k, v, q_pos, k_pos, lam):
    s = jnp.einsum('bqhcd,bkhcd->bhcqk', q, k).astype(jnp.float32) * (DK_C ** -0.5)
    dist = (q_pos[:, None] - k_pos[None, :]).astype(jnp.float32)
    slopes = 2.0 ** (-8.0 * jnp.arange(1, H_C + 1, dtype=jnp.float32) / H_C)
    bias = -slopes[:, None, None, None] * dist
    s = jnp.where(dist >= 0, s + bias, -jnp.inf)
    p = jax.nn.softmax(s, axis=-1)
    p = p[:, :, 0] - lam * p[:, :, 1]
    return jnp.einsum('bhqk,bkhe->bqhe', p.astype(v.dtype), v)


def diff_attn_prompt(q, k, v, lam):
    B_, S = q.shape[0], q.shape[1]
    nb = S // Q_BLOCK
    qb = jnp.moveaxis(q.reshape(B_, nb, Q_BLOCK, H_C, 2, DK_C), 1, 0)
    pos = jnp.arange(S)
    qpos = pos.reshape(nb, Q_BLOCK)
    out = lax.map(lambda a: diff_attend(a[0], k, v, a[1], pos, lam), (qb, qpos))
    return jnp.moveaxis(out, 0, 1).reshape(B_, S, H_C, DV_C)


def diff_attn_sample(q, k, v, lam, k_past, v_past, past_len):
    T = q.shape[1]
    kk = jnp.concatenate([k_past.astype(k.dtype), k], axis=1)
    vv = jnp.concatenate([v_past.astype(v.dtype), v], axis=1)
    q_pos = past_len + jnp.arange(T)
    k_pos = jnp.arange(past_len + T)
    return diff_attend(q, kk, vv, q_pos, k_pos, lam)


def pool_mix(xd, prev, pos, w_pool, scale):
    B_, T, _ = xd.shape
    P = POOL_MAX - 1
    ext_raw = jnp.concatenate([prev.astype(xd.dtype), xd], axis=1)
    ext = ext_raw.astype(jnp.float32)
    cs = jnp.pad(jnp.cumsum(ext, axis=1), ((0, 0), (1, 0), (0, 0)))
    outs = []
    for gi, w in enumerate(POOL_WINDOWS):
        sl = slice(gi * POOL_GD, (gi + 1) * POOL_GD)
        win = cs[:, P + 1:P + 1 + T, sl] - cs[:, P + 1 - w:P + 1 - w + T, sl]
        cnt = jnp.minimum(pos + 1, w).astype(jnp.float32)[None, :, None]
        outs.append(win / cnt - ext[:, P:, sl])
    d = jnp.stack(outs, axis=2)
    y = jnp.einsum('btgc,gce->btge', d, w_pool.astype(jnp.float32)).reshape(B_, T, D_D)
    return (y * scale.astype(jnp.float32)).astype(xd.dtype), ext_raw[:, -P:]


def hybrid_layer(x, pos, conv_prev, pool_prev, attend, lam_init, p):
    B_, T, _ = x.shape
    h = rms_norm(x, p['g_mix_pre'])
    z = h @ p['w_in']
    a, ga, u, vb, q, k, v, xd = jnp.split(z, IN_SPLITS, axis=-1)
    ya, conv_new = conformer_conv(a, ga, conv_prev, p['conv_w'], p['conv_b'], p['conv_ln_g'], p['conv_ln_b'])
    yb, v_rows = chunk_sgu(jax.nn.gelu(u, approximate=False), jax.nn.gelu(vb, approximate=False),
                           p['sgu_w'], p['sgu_b'], p['sgu_ln_g'], p['sgu_ln_b'])
    lam = diff_lambda(p['lambda_q1'], p['lambda_k1'], p['lambda_q2'], p['lambda_k2'], lam_init)
    q = q.reshape(B_, T, H_C, 2, DK_C)
    k = k.reshape(B_, T, H_C, 2, DK_C)
    v = v.reshape(B_, T, H_C, DV_C)
    o = attend(q, k, v, lam)
    yc = (rms_norm(o, p['attn_sub_g'], LN_EPS) * (1.0 - lam_init)).reshape(B_, T, D_C)
    yd, pool_new = pool_mix(xd, pool_prev, pos, p['pool_w'], p['pool_scale'])
    mix = jnp.concatenate([ya, yb, yc, yd], axis=-1) @ p['w_out']
    x = x + rms_norm(mix, p['g_mix_post'])
    gate, up = jnp.split(rms_norm(x, p['g_ffn_pre']) @ p['w_gu'], 2, axis=-1)
    x = x + rms_norm((jax.nn.silu(gate) * up) @ p['w_down'], p['g_ffn_post'])
    return x, k, v, conv_new, pool_new, v_rows


def setup_inputs(seed: int = 0) -> dict:
    key = jax.random.key(seed)
    ks = iter(jax.random.split(key, 40))
    f32 = jnp.float32
    nrm = lambda shape, s: jax.random.normal(next(ks), shape, f32) * s
    n_pages = PAST_LEN // PAGE_SIZE
    used = DEC_BATCH * n_pages
    n_pool = used + max(1, used // 4)
    page_table = jax.random.permutation(next(ks), n_pool)[:used].reshape(DEC_BATCH, n_pages).astype(jnp.int32)
    return {
        'x_prompt': nrm((BATCH, SEQ, D_MODEL), 1.0),
        'x_sample': nrm((DEC_BATCH, DEC_SEQ, D_MODEL), 1.0),
        'cache_k': nrm((DEPTH, n_pool, PAGE_SIZE, H_C, 2, DK_C), 1.0),
        'cache_v': nrm((DEPTH, n_pool, PAGE_SIZE, H_C, DV_C), 1.0),
        'state_conv': nrm((DEPTH, DEC_BATCH, CONV_W - 1, D_A), 0.5),
        'state_pool': nrm((DEPTH, DEC_BATCH, POOL_MAX - 1, D_D), 1.0),
        'page_table': page_table,
        'w_in': nrm((DEPTH, D_MODEL, D_IN), D_MODEL ** -0.5),
        'w_out': nrm((DEPTH, D_MIX, D_MODEL), D_MIX ** -0.5),
        'conv_w': nrm((DEPTH, CONV_W, D_A), CONV_W ** -0.5),
        'conv_b': nrm((DEPTH, D_A), 0.01),
        'conv_ln_g': 1.0 + nrm((DEPTH, D_A), 0.02),
        'conv_ln_b': nrm((DEPTH, D_A), 0.01),
        'sgu_w': nrm((DEPTH, SGU_HEADS, CHUNK, CHUNK), CHUNK ** -0.5),
        'sgu_b': 1.0 + nrm((DEPTH, SGU_HEADS, CHUNK), 0.02),
        'sgu_ln_g': 1.0 + nrm((DEPTH, D_B), 0.02),
        'sgu_ln_b': nrm((DEPTH, D_B), 0.01),
        'lambda_q1': nrm((DEPTH, DK_C), 0.1),
        'lambda_k1': nrm((DEPTH, DK_C), 0.1),
        'lambda_q2': nrm((DEPTH, DK_C), 0.1),
        'lambda_k2': nrm((DEPTH, DK_C), 0.1),
        'attn_sub_g': 1.0 + nrm((DEPTH, DV_C), 0.02),
        'pool_w': nrm((DEPTH, POOL_GROUPS, POOL_GD, POOL_GD), POOL_GD ** -0.5),
        'pool_scale': 1.0 + nrm((DEPTH, D_D), 0.1),
        'g_mix_pre': 1.0 + nrm((DEPTH, D_MODEL), 0.02),
        'g_mix_post': 1.0 + nrm((DEPTH, D_MODEL), 0.02),
        'g_ffn_pre': 1.0 + nrm((DEPTH, D_MODEL), 0.02),
        'g_ffn_post': 1.0 + nrm((DEPTH, D_MODEL), 0.02),
        'w_gu': nrm((DEPTH, D_MODEL, 2 * D_FF), D_MODEL ** -0.5),
        'w_down': nrm((DEPTH, D_FF, D_MODEL), D_FF ** -0.5),
    }


def reference(x_prompt, x_sample, cache_k, cache_v, state_conv, state_pool, page_table,
              w_in, w_out, conv_w, conv_b, conv_ln_g, conv_ln_b, sgu_w, sgu_b, sgu_ln_g, sgu_ln_b,
              lambda_q1, lambda_k1, lambda_q2, lambda_k2, attn_sub_g, pool_w, pool_scale,
              g_mix_pre, g_mix_post, g_ffn_pre, g_ffn_post, w_gu, w_down):
    B_, S, _ = x_prompt.shape
    DB, T, _ = x_sample.shape
    past_len = page_table.shape[1] * cache_k.shape[2]
    pos_p = jnp.arange(S)
    pos_s = past_len + jnp.arange(T)
    conv_zero = jnp.zeros((B_, CONV_W - 1, D_A), x_prompt.dtype)
    pool_zero = jnp.zeros((B_, POOL_MAX - 1, D_D), x_prompt.dtype)
    xp, xs = x_prompt, x_sample
    kp_l, vp_l, cp_l, pp_l = [], [], [], []
    ks_l, vs_l, cs_l, ps_l, gs_l = [], [], [], [], []
    for l in range(DEPTH):
        lam_init = 0.8 - 0.6 * math.exp(-0.3 * l)
        prm = {
            'w_in': w_in[l], 'w_out': w_out[l], 'conv_w': conv_w[l], 'conv_b': conv_b[l],
            'conv_ln_g': conv_ln_g[l], 'conv_ln_b': conv_ln_b[l], 'sgu_w': sgu_w[l], 'sgu_b': sgu_b[l],
            'sgu_ln_g': sgu_ln_g[l], 'sgu_ln_b': sgu_ln_b[l], 'lambda_q1': lambda_q1[l],
            'lambda_k1': lambda_k1[l], 'lambda_q2': lambda_q2[l], 'lambda_k2': lambda_k2[l],
            'attn_sub_g': attn_sub_g[l], 'pool_w': pool_w[l], 'pool_scale': pool_scale[l],
            'g_mix_pre': g_mix_pre[l], 'g_mix_post': g_mix_post[l], 'g_ffn_pre': g_ffn_pre[l],
            'g_ffn_post': g_ffn_post[l], 'w_gu': w_gu[l], 'w_down': w_down[l],
        }
        xp, kp, vp, cp, pp, _ = hybrid_layer(xp, pos_p, conv_zero, pool_zero, diff_attn_prompt, lam_init, prm)
        k_past = cache_k[l][page_table].reshape(DB, past_len, H_C, 2, DK_C)
        v_past = cache_v[l][page_table].reshape(DB, past_len, H_C, DV_C)
        attend_s = functools.partial(diff_attn_sample, k_past=k_past, v_past=v_past, past_len=past_len)
        xs, ks_, vs_, cs_, ps_, gs_ = hybrid_layer(xs, pos_s, state_conv[l], state_pool[l], attend_s, lam_init, prm)
        kp_l.append(kp); vp_l.append(vp); cp_l.append(cp); pp_l.append(pp)
        ks_l.append(ks_); vs_l.append(vs_); cs_l.append(cs_); ps_l.append(ps_); gs_l.append(gs_)
    return (xp, xs,
            jnp.stack(kp_l), jnp.stack(vp_l), jnp.stack(cp_l), jnp.stack(pp_l),
            jnp.stack(ks_l), jnp.stack(vs_l), jnp.stack(cs_l), jnp.stack(ps_l), jnp.stack(gs_l))
```

```python
import contextlib
import math
import numpy as np
import concourse.bass as bass
import concourse.mybir as mybir
from concourse.bass_utils import run_bass_kernel_spmd

F32 = mybir.dt.float32
BF16 = mybir.dt.bfloat16
I32 = mybir.dt.int32
AF = mybir.ActivationFunctionType
ALU = mybir.AluOpType
AX = mybir.AxisListType

NCORES = 8
D = 1024
DFF = 2816
NFC = 22
RMS_EPS = 1e-6
LN_EPS = 1e-5
SLOPES = [2.0 ** (-8.0 * (h + 1) / 4) for h in range(4)]
ISQ = 32 ** -0.5
NEG = -30000.0
ENGS = ["pe", "act", "dve", "pool", "sp"]

C_ID, C_TRI, C_BLK, C_BT, C_INVW, C_ICNT, C_MC, C_MH2C, C_I16, C_HM, C_E8, C_BIAS = 0, 128, 256, 384, 448, 450, 482, 484, 488, 489, 745, 873
B_ID, B_MA, B_MB, B_ONE, B_SEL, B_BN, NCB = 0, 128, 640, 1152, 1280, 1408, 2432
VB_LNG, VB_LNB, VB_GSUB, VB_LQ1, VB_LK1, VB_LQ2, VB_LK2, NVB = 0, 256, 512, 576, 608, 640, 672, 704


class Prog:
    def __init__(self, nc, stack):
        self.nc = nc
        self.stack = stack
        self.q = {e: [] for e in ENGS}
        self.esem = {e: stack.enter_context(nc.semaphore("es_" + e)) for e in ENGS}
        self.ecnt = {e: 0 for e in ENGS}
        self.seen = {e: {} for e in ENGS}
        self.buf = {}
        self.dsem = {}
        self.dcnt = {}

    def _st(self, k):
        if k not in self.buf:
            self.buf[k] = {"w": None, "r": []}
        return self.buf[k]

    def _waits(self, eng, reads, writes):
        evs = []
        for k in reads:
            s = self._st(k)
            if s["w"] is not None:
                evs.append(s["w"])
        for k in writes:
            s = self._st(k)
            if s["w"] is not None:
                evs.append(s["w"])
            evs.extend(s["r"])
        need = {}
        for kind, sid, val in evs:
            if kind == "E" and sid == eng and eng == "pe":
                continue
            key = (kind, sid)
            if self.seen[eng].get(key, 0) >= val:
                continue
            need[key] = max(need.get(key, 0), val)
        out = []
        for key, val in need.items():
            self.seen[eng][key] = val
            sem = self.esem[key[1]] if key[0] == "E" else self.dsem[key[1]]
            out.append((sem, val))
        return out

    def _record(self, ev, reads, writes):
        for k in reads:
            self._st(k)["r"].append(ev)
        for k in writes:
            s = self._st(k)
            s["w"] = ev
            s["r"] = []

    def op(self, eng, fn, reads=(), writes=(), signal=True):
        waits = self._waits(eng, reads, writes)
        if signal:
            self.ecnt[eng] += 1
            ev = ("E", eng, self.ecnt[eng])
            inc = (self.esem[eng], 1)
        else:
            ev = ("E", eng, self.ecnt[eng] + 1)
            inc = None
        self.q[eng].append((waits, fn, inc))
        self._record(ev, reads, writes)

    def dma(self, eng, fn, reads=(), writes=(), sem=None, inc=16):
        if sem is None:
            sem = writes[0]
        if sem not in self.dsem:
            self.dsem[sem] = self.stack.enter_context(self.nc.semaphore("ds_" + sem))
            self.dcnt[sem] = 0
        waits = self._waits(eng, reads, writes)
        self.dcnt[sem] += inc
        ev = ("D", sem, self.dcnt[sem])
        self.q[eng].append((waits, fn, (self.dsem[sem], inc)))
        self._record(ev, reads, writes)

    def barrier(self, keys):
        self.op("sp", lambda e: e.nop(), reads=(), writes=list(keys))

    def finish(self, out_keys):
        waits = self._waits("sp", out_keys, [])
        self.q["sp"].append((waits, None, None))
        nc = self.nc
        with nc.Block() as block:
            def run(engobj, lst):
                for waits, fn, inc in lst:
                    for sem, val in waits:
                        engobj.wait_ge(sem, val)
                    if fn is None:
                        continue
                    ins = fn(engobj)
                    if inc is not None:
                        ins.then_inc(inc[0], inc[1])

            @block.tensor
            def _(e):
                run(e, self.q["pe"])

            @block.scalar
            def _(e):
                run(e, self.q["act"])

            @block.vector
            def _(e):
                run(e, self.q["dve"])

            @block.gpsimd
            def _(e):
                run(e, self.q["pool"])

            @block.sync
            def _(e):
                run(e, self.q["sp"])


class Arena:
    def __init__(self, ap, ncols):
        self.ap = ap
        self.n = ncols
        self.off = 0
        self.keys = set()
        self.peak = 0

    def reset(self):
        self.off = 0

    def get(self, key, free, dtype):
        n = int(np.prod(free))
        cols = n * (2 if dtype in (F32, I32) else 1)
        cols = (cols + 1) // 2 * 2
        assert self.off + cols <= self.n, ("arena overflow", key, self.off + cols, self.n)
        a = self.ap[:, self.off:self.off + cols]
        self.off += cols
        self.peak = max(self.peak, self.off)
        self.keys.add(key)
        if dtype in (F32, I32):
            a = a.bitcast(dtype)
        a = a[:, 0:n]
        if len(free) == 2:
            a = a.rearrange("p (a b) -> p a b", a=free[0])
        elif len(free) == 3:
            a = a.rearrange("p (a b c) -> p a b c", a=free[0], b=free[1])
        return a


def make_cfg(L=4, S=2048, NP=64, NPOOL=2560):
    return dict(L=L, S=S, NP=NP, NPOOL=NPOOL, NT=S // 128, NB=S // 256, NF=S // 512, T8=NP // 8, PAST=NP * 128)


def host_consts(cfg, core):
    T8, PAST = cfg["T8"], cfg["PAST"]
    p = np.arange(128)
    cst = np.zeros((128, C_BIAS + T8 * 32), np.float32)
    cst[:, C_ID:C_ID + 128] = np.eye(128)
    cst[:, C_TRI:C_TRI + 128] = (p[:, None] <= p[None, :])
    cst[:, C_BLK:C_BLK + 128] = ((p[:, None] % 32) == (p[None, :] % 32))
    for h in range(4):
        for dd in range(16):
            cst[:, C_BT + h * 16 + dd] = SLOPES[h] * (p - 127 - dd * 128)
    wtab = np.zeros((128, 2))
    wtab[:64, 0], wtab[64:, 0], wtab[:64, 1], wtab[64:, 1] = 2, 4, 8, 16
    cst[:, C_INVW:C_INVW + 2] = 1.0 / wtab
    for ch in range(2):
        for pos in range(16):
            cst[:, C_ICNT + ch * 16 + pos] = 1.0 / np.minimum(pos + 1, wtab[:, ch])
    for c in range(2):
        cst[:, C_MC + c] = ((p // 32) % 2 == c)
    for h2 in range(2):
        for c in range(2):
            cst[:, C_MH2C + h2 * 2 + c] = ((p // 64 == h2) & ((p // 32) % 2 == c))
    cst[:, C_I16] = p % 16
    r = np.arange(32)
    cols = np.arange(256)
    cst[:32, C_HM:C_HM + 256] = (((r % 8) // 2)[:, None] == (cols // 64)[None, :])
    cst[:8, C_E8:C_E8 + 128] = (np.arange(8)[:, None] == (p // 16)[None, :])
    col = np.arange(32)
    hcol = (col % 8) // 2
    sl = np.array(SLOPES)[hcol]
    for t in range(T8):
        kpos = (8 * t + p // 16) * 128 + 16 * core + p % 16
        cst[:, C_BIAS + t * 32:C_BIAS + (t + 1) * 32] = sl[None, :] * (kpos[:, None] - (PAST + 3))
    cb = np.zeros((128, NCB), np.float32)
    cb[:, B_ID:B_ID + 128] = np.eye(128)
    tri = (p[:, None] <= p[None, :]).astype(np.float32)
    for c in range(2):
        cb[:, B_MA + c * 256:B_MA + c * 256 + 128] = tri
        cb[:, B_MA + c * 256 + 128:B_MA + c * 256 + 256] = 1.0
        cb[:, B_MB + c * 256 + 128:B_MB + c * 256 + 256] = tri
    cb[:, B_ONE:B_ONE + 128] = 1.0 / 256
    cb[:4, B_SEL:B_SEL + 128] = (np.arange(4)[:, None] == (p // 32)[None, :])
    tq = col // 8
    tk, bk = p // 32, p % 32
    for b in range(32):
        ok = (bk[:, None] == b) & (tk[:, None] <= tq[None, :])
        cb[:, B_BN + b * 32:B_BN + (b + 1) * 32] = np.where(ok, sl[None, :] * (tk[:, None] - 3.0), NEG)
    return cst, cb


def host_static(inp, cfg):
    L, S, NP, NPOOL, T8 = cfg["L"], cfg["S"], cfg["NP"], cfg["NPOOL"], cfg["T8"]
    f = lambda a: np.ascontiguousarray(a, dtype=np.float32)
    st = {}
    st["pt8"] = np.ascontiguousarray(np.asarray(inp["page_table"]).reshape(32, T8, 8).transpose(2, 0, 1).reshape(8, 32 * T8).astype(np.int32))
    vec = np.zeros((L, 128, 2, 35), np.float32)
    fm2 = lambda a: np.asarray(a).reshape(L, 2, 128).transpose(0, 2, 1)
    vec[:, :, :, 0] = fm2(inp["conv_b"])
    vec[:, :, :, 1] = fm2(inp["conv_ln_g"])
    vec[:, :, :, 2] = fm2(inp["conv_ln_b"])
    vec[:, :, :, 3] = fm2(inp["pool_scale"])
    vec[:, :, :, 4:35] = np.asarray(inp["conv_w"]).reshape(L, 31, 2, 128).transpose(0, 3, 2, 1)
    st["vecfm"] = vec
    gfm = np.zeros((L, 128, 2, 8), np.float32)
    gfm[:, :, 0, :] = np.asarray(inp["g_mix_pre"]).reshape(L, 8, 128).transpose(0, 2, 1)
    gfm[:, :, 1, :] = np.asarray(inp["g_ffn_pre"]).reshape(L, 8, 128).transpose(0, 2, 1)
    st["gfm"] = gfm
    st["vbc"] = f(np.concatenate([np.asarray(inp[k]).reshape(L, -1) for k in
                                  ("sgu_ln_g", "sgu_ln_b", "attn_sub_g", "lambda_q1", "lambda_k1", "lambda_q2", "lambda_k2")], axis=1).reshape(L, 1, NVB))
    st["gpost"] = f(np.stack([np.asarray(inp["g_mix_post"]), np.asarray(inp["g_ffn_post"])], axis=1).reshape(L, 2, 1, D))
    st["consts"] = [host_consts(cfg, c) for c in range(NCORES)]
    return st


def stage_maps(inp, cfg, stage, st, state):
    L, S, NP, NPOOL, T8 = cfg["L"], cfg["S"], cfg["NP"], cfg["NPOOL"], cfg["T8"]
    f = lambda a: np.ascontiguousarray(a, dtype=np.float32)
    doA, doB = stage < L, stage >= 1
    shared = {"xs": f(state["xs"])}
    if doB:
        lb = stage - 1
        sl = slice(lb, lb + 1)
        sc = np.asarray(inp["state_conv"])[sl]
        sp_ = np.asarray(inp["state_pool"])[sl]
        shared["sconv_fm"] = f(sc.transpose(0, 3, 1, 2))
        shared["spool_fm"] = f(sp_.transpose(0, 3, 1, 2))
        shared["sconv_raw"] = f(sc)
        shared["spool_raw"] = f(sp_)
        shared["w_in"] = f(np.asarray(inp["w_in"])[sl])
        shared["w_out"] = f(np.asarray(inp["w_out"])[sl])
        wgu = np.asarray(inp["w_gu"])[sl].reshape(1, 8, 128, 2, NFC, 128)
        shared["w_gu_t"] = f(wgu.transpose(0, 4, 2, 1, 3, 5).reshape(1, NFC, 128, 8, 256))
        shared["w_down"] = f(np.asarray(inp["w_down"])[sl])
        for k in ("vecfm", "gfm", "vbc", "gpost"):
            shared[k] = f(st[k][sl])
        shared["sgu_w"] = f(np.asarray(inp["sgu_w"])[sl])
        shared["sgu_b"] = f(np.asarray(inp["sgu_b"])[sl])
        shared["pool_w"] = f(np.asarray(inp["pool_w"])[sl])
        lam_init = 0.8 - 0.6 * math.exp(-0.3 * lb)
        lamc = np.zeros((128, 2), np.float32)
        lamc[:, 0] = -lam_init
        lamc[:, 1] = 1.0 - lam_init
        shared["lamc"] = lamc
        shared["part"] = f(state["part"])
    if doA:
        la = stage
        shared["pt8"] = st["pt8"]
        shared["w_in_a"] = f(np.asarray(inp["w_in"])[la][:, 1024:1792])
        shared["gfm_a"] = f(st["gfm"][la])
        ck = np.asarray(inp["cache_k"])[la].reshape(NPOOL, 128, 256)
        cv = np.asarray(inp["cache_v"])[la].reshape(NPOOL, 128, 256)
    maps = []
    for c in range(NCORES):
        m = dict(shared)
        m["cst"], m["cstb"] = st["consts"][c]
        if doB:
            m["xp"] = f(state["xp"][c])
        if doA:
            kv = np.empty((NPOOL, 16, 512), np.float32)
            kv[..., 0:256] = ck[:, 16 * c:16 * c + 16]
            kv[..., 256:512] = cv[:, 16 * c:16 * c + 16]
            m["kv"] = kv.reshape(NPOOL * 16, 512)
        maps.append(m)
    return maps


def build(cfg, stage):
    L, S, NP, NPOOL, NT, NB, NF, T8, PAST = (cfg[k] for k in ("L", "S", "NP", "NPOOL", "NT", "NB", "NF", "T8", "PAST"))
    doA = stage < L
    doB = stage >= 1
    nc = bass.Bass("TRN2", target_bir_lowering=False)
    NC_ = C_BIAS + T8 * 32

    def din(name, shape, dt=F32):
        return nc.dram_tensor(name, list(shape), dt, kind="ExternalInput").ap()

    def dout(name, shape):
        return nc.dram_tensor(name, list(shape), F32, kind="ExternalOutput").ap()

    xs_d = din("xs", [128, D])
    cst_d = din("cst", [128, NC_]); cstb_d = din("cstb", [128, NCB])
    out_keys = []
    if doB:
        xp_d = din("xp", [S, D])
        sconv_fm_d = din("sconv_fm", [1, 256, 32, 30]); spool_fm_d = din("spool_fm", [1, 256, 32, 15])
        sconv_raw_d = din("sconv_raw", [1, 32, 30, 256]); spool_raw_d = din("spool_raw", [1, 32, 15, 256])
        w_in_d = din("w_in", [1, D, 2048]); w_out_d = din("w_out", [1, D, D])
        w_gu_d = din("w_gu_t", [1, NFC, 128, 8, 256]); w_down_d = din("w_down", [1, DFF, D])
        vecfm_d = din("vecfm", [1, 128, 2, 35]); gfm_d = din("gfm", [1, 128, 2, 8]); vbc_d = din("vbc", [1, 1, NVB])
        gpost_d = din("gpost", [1, 2, 1, D]); sguw_d = din("sgu_w", [1, 4, 128, 128]); sgub_d = din("sgu_b", [1, 4, 128])
        poolw_d = din("pool_w", [1, 4, 64, 64]); lamc_d = din("lamc", [128, 2]); part_d = din("part", [NCORES, 1024, 65])
        o_yp = dout("o_yp", [S, D]); o_ys = dout("o_ys", [128, D])
        o_kp = dout("o_kp", [1, S, 256]); o_vp = dout("o_vp", [1, S, 256])
        o_cp = dout("o_cp", [1, 30, 256]); o_pp = dout("o_pp", [1, 15, 256])
        o_cs = dout("o_cs", [1, 32, 30, 256]); o_ps = dout("o_ps", [1, 32, 15, 256]); o_gs = dout("o_gs", [1, 128, 256])
        out_keys += ["o_yp", "o_ys", "o_kp", "o_vp", "o_cp", "o_pp", "o_cs", "o_ps", "o_gs"]
    if doA:
        kv_d = din("kv", [NPOOL * 16, 512])
        pt8_d = din("pt8", [8, 32 * T8], I32)
        w_in_a_d = din("w_in_a", [D, 768]); gfm_a_d = din("gfm_a", [128, 2, 8])
        o_ks = dout("o_ks", [128, 256]); o_vs = dout("o_vs", [128, 256]); o_part = dout("o_part", [1024, 65])
        out_keys += ["o_ks", "o_vs", "o_part"]
    DBG = False

    with contextlib.ExitStack() as st:
        P = Prog(nc, st)
        SB = lambda n, s, d: st.enter_context(nc.sbuf_tensor(n, list(s), d))
        x = SB("x", [128, NT, D], F32)
        xs = SB("xs_sb", [128, D], F32)
        cst = SB("cst_sb", [128, NC_], F32)
        cstb = SB("cstb_sb", [128, NCB], BF16)
        Wt = SB("Wt", [128, 24576], BF16)
        TCOLS = 32896
        Tt = SB("Tt", [128, TCOLS], BF16)
        hn = SB("hn", [128, D], BF16)
        tmpn = SB("tmpn", [128, 512], F32)
        gpost = SB("gpost_sb", [128, D], F32)
        vbc = SB("vbc_sb", [128, NVB], F32)
        vecfm = SB("vecfm_sb", [128, 2, 35], F32)
        gfm = SB("gfm_sb", [128, 2, 8], F32)
        stt = SB("stt", [128, 48], F32)
        idx = SB("idx", [128, 32 * T8], I32)
        WsT = SB("WsT", [128, 4, 128], BF16)
        WsS = SB("WsS", [128, 4, 128], BF16)
        BsT = SB("BsT", [128, 2, 128], F32)
        Wp = SB("Wp", [128, 2, 128], BF16)
        Wpst = SB("Wpst", [128, 2, 64], F32)
        zer = SB("zer", [128, 512], BF16)
        nlam = SB("nlam", [128, 4], F32)
        lamc = SB("lamc_sb", [128, 2], F32)
        ps = [st.enter_context(nc.psum_tensor(f"ps{i}", [128, 512], F32)) for i in range(8)]
        psb = [p_[:].bitcast(BF16) for p_ in ps]
        TA = Arena(Tt[:], TCOLS)

        idf = cst[:, C_ID:C_ID + 128]
        idb = cstb[:, B_ID:B_ID + 128]
        w_in_sb = Wt[:, 0:16384].rearrange("p (k n) -> p k n", k=8)
        w_out_sb = Wt[:, 16384:24576].rearrange("p (k n) -> p k n", k=8)
        w_down_sb = Wt[:, 0:22528].rearrange("p (k n) -> p k n", k=NFC)

        V = lambda fn, r, w: P.op("dve", fn, r, w)
        A = lambda fn, r, w: P.op("act", fn, r, w)
        G = lambda fn, r, w: P.op("pool", fn, r, w)

        def MM(out, lhsT, rhs, start, stop, r, w, signal):
            P.op("pe", lambda e: e.matmul(out, lhsT=lhsT, rhs=rhs, start=start, stop=stop, skip_group_check=True), r, w, signal)

        def TR(out, in_, ident, r, w, signal):
            P.op("pe", lambda e: e.transpose(out=out, in_=in_, identity=ident), r, w, signal)

        gbc = [0]

        def gb():
            gbc[0] = (gbc[0] + 1) % 6
            return gbc[0]

        sctr = [0]

        def sc(n=1):
            if sctr[0] + n > 48:
                sctr[0] = 0
            a = sctr[0]
            sctr[0] += n
            return a

        P.dma("sp", (lambda e, _kw=dict(out=cst[:], in_=cst_d): e.dma_start(**_kw)), writes=["cst"])
        P.dma("pool", (lambda e, _kw=dict(out=cstb[:], in_=cstb_d): e.dma_start(**_kw)), writes=["cstb"])
        G((lambda e, _kw=dict(ap=zer[:], constant=0.0): e.memset(**_kw)), [], ["zer"])
        G((lambda e, _kw=dict(ap=Wp[:], constant=0.0): e.memset(**_kw)), [], ["Wp"])
        if doB:
            xv = xp_d.rearrange("(t p) d -> p t d", p=128)
            for t0 in range(0, NT, 4):
                P.dma("sp", (lambda t0: (lambda e, _kw=dict(out=x[:, t0:t0 + 4, :], in_=xv[:, t0:t0 + 4, :]): e.dma_start(**_kw)))(t0),
                      writes=[f"x{t}" for t in range(t0, t0 + 4)], sem=f"xl{t0 // 4 % 4}")
            P.dma("sp", (lambda e, _kw=dict(out=lamc[:], in_=lamc_d): e.dma_start(**_kw)), writes=["lamc"])
        P.dma("sp", (lambda e, _kw=dict(out=xs[:], in_=xs_d): e.dma_start(**_kw)), writes=["xs"])
        if doA:
            TA.reset()
            pt8i = TA.get("pt8i", (32 * T8,), I32)
            pt8f = TA.get("pt8f", (32 * T8,), F32)
            idxf = TA.get("idxf", (32 * T8,), F32)
            P.dma("sp", (lambda e, _kw=dict(out=pt8i[0:8, :], in_=pt8_d): e.dma_start(**_kw)), writes=["pt8i"])
            V((lambda e, _kw=dict(out=pt8f[0:8, :], in_=pt8i[0:8, :]): e.tensor_copy(**_kw)), ["pt8i"], ["pt8f"])
            b0 = gb()
            MM(ps[b0][:, 0:32 * T8], cst[0:8, C_E8:C_E8 + 128], pt8f[0:8, :], True, True, ["cst", "pt8f"], [f"ps{b0}"], True)
            V((lambda e, _kw=dict(out=idxf, in0=ps[b0][:, 0:32 * T8], scalar1=16.0, scalar2=cst[:, C_I16:C_I16 + 1], op0=ALU.mult, op1=ALU.add): e.tensor_scalar(**_kw)),
              [f"ps{b0}", "cst"], ["idxf"])
            V((lambda e, _kw=dict(out=idx[:], in_=idxf): e.tensor_copy(**_kw)), ["idxf"], ["idx"])

        def prenorm_tile(xap, xkey, gsel, hT, hkey, tcol):
            c0 = sc(3)
            hnj = hn[:]
            A((lambda e, _kw=dict(out=hnj, in_=xap, func=AF.Square, accum_out=stt[:, c0:c0 + 1]): e.activation(**_kw)), [xkey], ["hn", "stt"])
            A((lambda e, _kw=dict(out=stt[:, c0 + 1:c0 + 2], in_=stt[:, c0:c0 + 1], func=AF.Sqrt, scale=1.0 / D, bias=RMS_EPS): e.activation(**_kw)), ["stt"], ["stt"])
            V((lambda e, _kw=dict(out=stt[:, c0 + 2:c0 + 3], in_=stt[:, c0 + 1:c0 + 2]): e.reciprocal(**_kw)), ["stt"], ["stt"])
            V((lambda e, _kw=dict(out=hnj, in0=xap, scalar1=stt[:, c0 + 2:c0 + 3], scalar2=None, op0=ALU.mult): e.tensor_scalar(**_kw)), [xkey, "stt"], ["hn"])
            b = gb()
            for kc in range(8):
                TR(psb[b][:, kc * 128:(kc + 1) * 128], hn[:, kc * 128:(kc + 1) * 128], idb, ["hn", "cstb"], [f"ps{b}"], kc == 7)
            V((lambda e, _kw=dict(out=hT[:, :, tcol:tcol + 128], in0=psb[b].rearrange("p (k t) -> p k t", k=8),
                                        in1=gfm[:, gsel, :].unsqueeze(2).to_broadcast([128, 8, 128]), op=ALU.mult): e.tensor_tensor(**_kw)),
              [f"ps{b}", "gfm"], [hkey])

        def postnorm_add(bA, bB, xap, xkey):
            c0 = sc(5)
            tj = tmpn[:].bitcast(BF16)
            A((lambda e, _kw=dict(out=tj[:, 0:512], in_=ps[bA][:], func=AF.Square, accum_out=stt[:, c0:c0 + 1]): e.activation(**_kw)), [f"ps{bA}"], ["tmpn", "stt"])
            A((lambda e, _kw=dict(out=tj[:, 512:1024], in_=ps[bB][:], func=AF.Square, accum_out=stt[:, c0 + 1:c0 + 2]): e.activation(**_kw)), [f"ps{bB}"], ["tmpn", "stt"])
            V((lambda e, _kw=dict(out=stt[:, c0 + 2:c0 + 3], in0=stt[:, c0:c0 + 1], in1=stt[:, c0 + 1:c0 + 2], op=ALU.add): e.tensor_tensor(**_kw)), ["stt"], ["stt"])
            A((lambda e, _kw=dict(out=stt[:, c0 + 3:c0 + 4], in_=stt[:, c0 + 2:c0 + 3], func=AF.Sqrt, scale=1.0 / D, bias=RMS_EPS): e.activation(**_kw)), ["stt"], ["stt"])
            V((lambda e, _kw=dict(out=stt[:, c0 + 4:c0 + 5], in_=stt[:, c0 + 3:c0 + 4]): e.reciprocal(**_kw)), ["stt"], ["stt"])
            for n, bk in enumerate((bA, bB)):
                V((lambda e, _kw=dict(out=tmpn[:], in0=ps[bk][:], scalar=stt[:, c0 + 4:c0 + 5], in1=gpost[:, n * 512:(n + 1) * 512],
                                                            op0=ALU.mult, op1=ALU.mult): e.scalar_tensor_tensor(**_kw)), [f"ps{bk}", "stt", "gpost"], ["tmpn"])
                G((lambda e, _kw=dict(out=xap[:, n * 512:(n + 1) * 512], in0=xap[:, n * 512:(n + 1) * 512], in1=tmpn[:], op=ALU.add): e.tensor_tensor(**_kw)),
                  ["tmpn", xkey], [xkey])

        def fm_chunk(col, hT, hkey, W, evac):
            b = gb()
            for kc in range(8):
                MM(ps[b][:, 0:W], w_in_sb[:, kc, col:col + 128], hT[:, kc, 0:W], kc == 0, kc == 7, ["w_in", hkey], [f"ps{b}"], kc == 7)
            evac(b)

        def tm_cols(col, N, hT, hkey, tcol):
            b = gb()
            for kc in range(8):
                MM(ps[b][:, 0:N], hT[:, kc, tcol:tcol + 128], w_in_sb[:, kc, col:col + N], kc == 0, kc == 7, ["w_in", hkey], [f"ps{b}"], kc == 7)
            return b

        def ln_rows(vg, vgk, vn32, vnk):
            c0 = sc(10)
            V((lambda e, _kw=dict(out=stt[:, c0:c0 + 6], in_=vg): e.bn_stats(**_kw)), [vgk], ["stt"])
            V((lambda e, _kw=dict(out=stt[:, c0 + 6:c0 + 8], in_=stt[:, c0:c0 + 6]): e.bn_aggr(**_kw)), ["stt"], ["stt"])
            A((lambda e, _kw=dict(out=stt[:, c0 + 8:c0 + 9], in_=stt[:, c0 + 7:c0 + 8], func=AF.Sqrt, scale=1.0, bias=LN_EPS): e.activation(**_kw)), ["stt"], ["stt"])
            V((lambda e, _kw=dict(out=stt[:, c0 + 9:c0 + 10], in_=stt[:, c0 + 8:c0 + 9]): e.reciprocal(**_kw)), ["stt"], ["stt"])
            V((lambda e, _kw=dict(out=vn32, in0=vg, scalar1=stt[:, c0 + 6:c0 + 7], scalar2=stt[:, c0 + 9:c0 + 10], op0=ALU.subtract, op1=ALU.mult): e.tensor_scalar(**_kw)),
              [vgk, "stt"], [vnk])
            V((lambda e, _kw=dict(out=vn32, in0=vn32, in1=vbc[:, VB_LNG:VB_LNG + 256], op=ALU.mult): e.tensor_tensor(**_kw)), [vnk, "vbc"], [vnk])
            V((lambda e, _kw=dict(out=vn32, in0=vn32, in1=vbc[:, VB_LNB:VB_LNB + 256], op=ALU.add): e.tensor_tensor(**_kw)), [vnk, "vbc"], [vnk])

        def to_vnz(vn32, vnk, vnz):
            v4 = vn32.rearrange("p (a two c) -> p a two c", two=2, c=64)
            z4 = vnz.rearrange("p (a two) c -> p a two c", two=2)
            G((lambda e, _kw=dict(out=z4[:, :, 0, 0:64], in_=v4[:, :, 0, :]): e.tensor_copy(**_kw)), [vnk], ["vnz"])
            G((lambda e, _kw=dict(out=z4[:, :, 1, 64:128], in_=v4[:, :, 1, :]): e.tensor_copy(**_kw)), [vnk], ["vnz"])

        def conv_ln_silu(cy, W, ybf, ysq, mean, m2, rstd, mixT, mkey, mcol):
            G((lambda e, _kw=dict(out=ybf, in_=cy): e.tensor_copy(**_kw)), ["cy"], ["ybf"])
            G((lambda e, _kw=dict(out=ysq, in0=cy, in1=cy, op=ALU.mult): e.tensor_tensor(**_kw)), ["cy"], ["ysq"])
            b = gb()
            one = cstb[:, B_ONE:B_ONE + 128]
            for ch in range(2):
                MM(ps[b][:, 0:W], one, ybf[:, ch, :], ch == 0, ch == 1, ["cstb", "ybf"], [f"ps{b}"], False)
            for ch in range(2):
                MM(ps[b][:, 256:256 + W], one, ysq[:, ch, :], ch == 0, ch == 1, ["cstb", "ysq"], [f"ps{b}"], ch == 1)
            A((lambda e, _kw=dict(out=mean, in_=ps[b][:, 0:W]): e.copy(**_kw)), [f"ps{b}"], ["cmean"])
            G((lambda e, _kw=dict(out=m2, in0=mean, in1=mean, op=ALU.mult): e.tensor_tensor(**_kw)), ["cmean"], ["cm2"])
            V((lambda e, _kw=dict(out=m2, in0=ps[b][:, 256:256 + W], in1=m2, op=ALU.subtract): e.tensor_tensor(**_kw)), [f"ps{b}", "cm2"], ["cm2"])
            A((lambda e, _kw=dict(out=rstd, in_=m2, func=AF.Sqrt, scale=1.0, bias=LN_EPS): e.activation(**_kw)), ["cm2"], ["crstd"])
            V((lambda e, _kw=dict(out=rstd, in_=rstd): e.reciprocal(**_kw)), ["crstd"], ["crstd"])
            V((lambda e, _kw=dict(out=cy, in0=cy, in1=mean.unsqueeze(1).to_broadcast([128, 2, W]), op=ALU.subtract): e.tensor_tensor(**_kw)), ["cy", "cmean"], ["cy"])
            V((lambda e, _kw=dict(out=cy, in0=cy, in1=rstd.unsqueeze(1).to_broadcast([128, 2, W]), op=ALU.mult): e.tensor_tensor(**_kw)), ["cy", "crstd"], ["cy"])
            for ch in range(2):
                A((lambda e, _kw=dict(out=mixT[:, ch, mcol:mcol + W], in_=cy[:, ch, :], func=AF.Silu, scale=vecfm[:, ch, 1:2], bias=vecfm[:, ch, 2:3]): e.activation(**_kw)),
                  ["cy", "vecfm"], [mkey])

        def attn_finish(o0, l0, o1, l1, rk, obuf_h, okey):
            c0 = sc(3)
            V((lambda e, _kw=dict(out=stt[:, c0:c0 + 1], in_=l0): e.reciprocal(**_kw)), rk, ["stt"])
            V((lambda e, _kw=dict(out=stt[:, c0 + 1:c0 + 2], in_=l1): e.reciprocal(**_kw)), rk, ["stt"])
            V((lambda e, _kw=dict(out=stt[:, c0 + 2:c0 + 3], in0=stt[:, c0 + 1:c0 + 2], in1=nlam[:, 0:1], op=ALU.mult): e.tensor_tensor(**_kw)), ["stt", "nlam"], ["stt"])
            V((lambda e, _kw=dict(out=obuf_h, in0=o0, scalar1=stt[:, c0:c0 + 1], scalar2=None, op0=ALU.mult): e.tensor_scalar(**_kw)), rk + ["stt"], [okey])
            V((lambda e, _kw=dict(out=obuf_h, in0=o1, scalar=stt[:, c0 + 2:c0 + 3], in1=obuf_h, op0=ALU.mult, op1=ALU.add): e.scalar_tensor_tensor(**_kw)), rk + ["stt", okey], [okey])

        def head_norm_to_mix(ob, okey, osq, ycb, lam_init, mixT, mkey, mcol):
            c0 = sc(8)
            o3 = ob.rearrange("p (h e) -> p h e", h=4)
            G((lambda e, _kw=dict(out=osq, in0=ob, in1=ob, op=ALU.mult): e.tensor_tensor(**_kw)), [okey], ["osq"])
            V((lambda e, _kw=dict(out=stt[:, c0:c0 + 4], in_=osq.rearrange("p (h e) -> p h e", h=4), axis=AX.X, op=ALU.add): e.tensor_reduce(**_kw)), ["osq"], ["stt"])
            A((lambda e, _kw=dict(out=stt[:, c0 + 4:c0 + 8], in_=stt[:, c0:c0 + 4], func=AF.Sqrt, scale=1.0 / 64, bias=LN_EPS): e.activation(**_kw)), ["stt"], ["stt"])
            V((lambda e, _kw=dict(out=stt[:, c0:c0 + 4], in_=stt[:, c0 + 4:c0 + 8]): e.reciprocal(**_kw)), ["stt"], ["stt"])
            V((lambda e, _kw=dict(out=o3, in0=o3, scalar=lamc[:, 1:2], in1=stt[:, c0:c0 + 4].unsqueeze(2).to_broadcast([128, 4, 64]),
                                               op0=ALU.mult, op1=ALU.mult): e.scalar_tensor_tensor(**_kw)), [okey, "stt", "lamc"], [okey])
            V((lambda e, _kw=dict(out=ycb.rearrange("p (h e) -> p h e", h=4), in0=o3,
                                        in1=vbc[:, VB_GSUB:VB_GSUB + 64].unsqueeze(1).to_broadcast([128, 4, 64]), op=ALU.mult): e.tensor_tensor(**_kw)), [okey, "vbc"], ["ycb"])
            b = gb()
            for ch in range(2):
                TR(psb[b][:, ch * 128:(ch + 1) * 128], ycb[:, ch * 128:(ch + 1) * 128], idb, ["ycb", "cstb"], [f"ps{b}"], ch == 1)
            A((lambda e, _kw=dict(out=mixT[:, 4:6, mcol:mcol + 128], in_=psb[b][:, 0:256].rearrange("p (c t) -> p c t", c=2)): e.copy(**_kw)), [f"ps{b}"], [mkey])

        def out_proj_tile(mixT, mkeys, mcol, xap, xkey):
            bA, bB = gb(), gb()
            for n, bk in enumerate((bA, bB)):
                for kc in range(8):
                    MM(ps[bk][:], mixT[:, kc, mcol:mcol + 128], w_out_sb[:, kc, n * 512:(n + 1) * 512], kc == 0, kc == 7, mkeys + ["w_out"], [f"ps{bk}"], kc == 7)
            postnorm_add(bA, bB, xap, xkey)

        ARK = set()

        def phase_barrier(extra=()):
            P.barrier(sorted(set(P.buf.keys()) | ARK | set(extra)))

        for l in ([0] if doB else []):
            lam_init = None
            phase_barrier(["w_in", "w_out", "w_down"])
            TA.reset()
            w_in_v = w_in_d[l].rearrange("(kc p) n -> p kc n", p=128)
            for hf in range(2):
                P.dma("pool", (lambda hf: (lambda e, _kw=dict(out=w_in_sb[:, hf * 4:(hf + 1) * 4, :], in_=w_in_v[:, hf * 4:(hf + 1) * 4, :]): e.dma_start(**_kw)))(hf),
                      writes=["w_in"], sem="w_in")
            P.dma("pool", (lambda e, _kw=dict(out=w_out_sb, in_=w_out_d[l].rearrange("(kc p) n -> p kc n", p=128)): e.dma_start(**_kw)), writes=["w_out"], sem="w_out")
            P.dma("sp", (lambda e, _kw=dict(out=vecfm[:], in_=vecfm_d[l]): e.dma_start(**_kw)), writes=["vecfm"])
            P.dma("sp", (lambda e, _kw=dict(out=gfm[:], in_=gfm_d[l]): e.dma_start(**_kw)), writes=["gfm"])
            P.dma("sp", (lambda e, _kw=dict(out=vbc[:], in_=vbc_d[l].partition_broadcast(128)): e.dma_start(**_kw)), writes=["vbc"])
            P.dma("sp", (lambda e, _kw=dict(out=gpost[:], in_=gpost_d[l, 0].partition_broadcast(128)): e.dma_start(**_kw)), writes=["gpost"])
            sguw_st = TA.get("sguw_st", (4, 128), F32)
            P.dma("sp", (lambda e, _kw=dict(out=sguw_st, in_=sguw_d[l].rearrange("g t s -> t g s")): e.dma_start(**_kw)), writes=["sguw_st"])
            for g in range(4):
                P.dma("sp", (lambda g: (lambda e, _kw=dict(out=BsT[(g % 2) * 64:(g % 2) * 64 + 64, g // 2, :], in_=sgub_d[l, g:g + 1, :].partition_broadcast(64)): e.dma_start(**_kw)))(g),
                      writes=["BsT"], sem="BsT")
                P.dma("sp", (lambda g: (lambda e, _kw=dict(out=Wpst[(g % 2) * 64:(g % 2) * 64 + 64, g // 2, :], in_=poolw_d[l, g]): e.dma_start(**_kw)))(g),
                      writes=["Wpst"], sem="Wpst")
            G((lambda e, _kw=dict(out=Wp[0:64, :, 0:64], in_=Wpst[0:64, :, :]): e.tensor_copy(**_kw)), ["Wpst"], ["Wp"])
            G((lambda e, _kw=dict(out=Wp[64:128, :, 64:128], in_=Wpst[64:128, :, :]): e.tensor_copy(**_kw)), ["Wpst"], ["Wp"])
            c0 = sc(6)
            ltmp = TA.get("ltmp", (64,), F32)
            V((lambda e, _kw=dict(out=ltmp[:, 0:32], in0=vbc[:, VB_LQ1:VB_LQ1 + 32], in1=vbc[:, VB_LK1:VB_LK1 + 32], op=ALU.mult): e.tensor_tensor(**_kw)), ["vbc"], ["ltmp"])
            V((lambda e, _kw=dict(out=ltmp[:, 32:64], in0=vbc[:, VB_LQ2:VB_LQ2 + 32], in1=vbc[:, VB_LK2:VB_LK2 + 32], op=ALU.mult): e.tensor_tensor(**_kw)), ["vbc"], ["ltmp"])
            V((lambda e, _kw=dict(out=stt[:, c0:c0 + 2], in_=ltmp.rearrange("p (a b) -> p a b", a=2), axis=AX.X, op=ALU.add): e.tensor_reduce(**_kw)), ["ltmp"], ["stt"])
            A((lambda e, _kw=dict(out=stt[:, c0 + 2:c0 + 4], in_=stt[:, c0:c0 + 2], func=AF.Exp): e.activation(**_kw)), ["stt"], ["stt"])
            V((lambda e, _kw=dict(out=stt[:, c0 + 4:c0 + 5], in0=stt[:, c0 + 3:c0 + 4], in1=stt[:, c0 + 2:c0 + 3], op=ALU.subtract): e.tensor_tensor(**_kw)), ["stt"], ["stt"])
            V((lambda e, _kw=dict(out=nlam[:, 0:1], in0=stt[:, c0 + 4:c0 + 5], scalar1=lamc[:, 0:1], scalar2=None, op0=ALU.add): e.tensor_scalar(**_kw)), ["stt", "lamc"], ["nlam"])
            b = gb()
            for g in range(4):
                TR(ps[b][:, g * 128:(g + 1) * 128], sguw_st[:, g, :], idf, ["sguw_st", "cst"], [f"ps{b}"], g == 3)
            V((lambda e, _kw=dict(out=WsT[:], in0=ps[b][:].rearrange("p (g t) -> p g t", g=4),
                                        in1=cst[:, C_TRI:C_TRI + 128].unsqueeze(1).to_broadcast([128, 4, 128]), op=ALU.mult): e.tensor_tensor(**_kw)), [f"ps{b}", "cst"], ["WsT"])
            sel = cstb[0:4, B_SEL:B_SEL + 128]
            b = gb()
            for g in range(4):
                MM(ps[b][0:4, g * 128:(g + 1) * 128], WsT[0:4, g, 0:4], sel, True, True, ["WsT", "cstb"], [f"ps{b}"], g == 3)
            asb = TA.get("asb", (4, 128), BF16)
            A((lambda e, _kw=dict(out=asb[0:4], in_=ps[b][0:4, :].rearrange("p (g t) -> p g t", g=4)): e.copy(**_kw)), [f"ps{b}"], ["asb"])
            b2 = gb()
            for g in range(4):
                MM(ps[b2][:, g * 128:(g + 1) * 128], asb[0:4, g, :], sel, True, True, ["asb", "cstb"], [f"ps{b2}"], g == 3)
            V((lambda e, _kw=dict(out=WsS[:], in0=ps[b2][:].rearrange("p (g t) -> p g t", g=4),
                                        in1=cst[:, C_BLK:C_BLK + 128].unsqueeze(1).to_broadcast([128, 4, 128]), op=ALU.mult): e.tensor_tensor(**_kw)), [f"ps{b2}", "cst"], ["WsS"])
            ARK.update(TA.keys)

            phase_barrier()
            TA.reset()
            W = 256
            hT = TA.get("hT", (8, 512), BF16)
            kT_all = TA.get("kT_all", (2, S), BF16)
            V1 = TA.get("V1", (NT, 4, 65), BF16)
            Qbd = TA.get("Qbd", (2, 512), BF16)
            mixT = TA.get("mixT", (8, W), BF16)
            NPT = 3
            pT = [TA.get(f"pT{i}", (512,), BF16) for i in range(NPT)]
            hconv = TA.get("hconv", (2, 30 + W), F32)
            cy = TA.get("cy", (2, W), F32)
            ybf = TA.get("ybf", (2, W), BF16)
            ysq = TA.get("ysq", (2, W), BF16)
            cmean = TA.get("cmean", (W,), F32)
            cm2 = TA.get("cm2", (W,), F32)
            crstd = TA.get("crstd", (W,), F32)
            sg = TA.get("sg", (2, W), F32)
            uT = TA.get("uT", (2, W), BF16)
            vg = TA.get("vg", (256,), F32)
            vn32 = TA.get("vn32", (256,), F32)
            vnz = TA.get("vnz", (4, 128), BF16)
            sgt = TA.get("sgt", (2, 128), F32)
            xdT = TA.get("xdT", (2, 16 + W), F32)
            pA = TA.get("pA", (2, 16 + W), F32)
            pB = TA.get("pB", (2, 16 + W), F32)
            dT = TA.get("dT", (2, W), BF16)
            kvo = TA.get("kvo", (512,), F32)
            obuf = TA.get("obuf", (2, 256), F32)
            osq = TA.get("osq", (256,), F32)
            ycb = TA.get("ycb", (256,), BF16)
            tm1 = TA.get("tm1", (256,), F32)
            tm2 = TA.get("tm2", (256,), F32)
            ARK.update(TA.keys)
            G((lambda e, _kw=dict(ap=V1, constant=1.0): e.memset(**_kw)), [], ["V1"])
            G((lambda e, _kw=dict(ap=vnz, constant=0.0): e.memset(**_kw)), [], ["vnz"])
            G((lambda e, _kw=dict(ap=hconv[:, :, 0:30], constant=0.0): e.memset(**_kw)), [], ["hconv"])
            G((lambda e, _kw=dict(ap=xdT[:, :, 0:15], constant=0.0): e.memset(**_kw)), [], ["xdT"])

            for blk in range(NB):
                t0 = 2 * blk
                for tl in range(2):
                    prenorm_tile(x[:, t0 + tl, :], f"x{t0 + tl}", 0, hT, "hT", tl * 128)
                for ch in range(2):
                    fm_chunk(256 + ch * 128, hT, "hT", W,
                             lambda b, ch=ch: A((lambda e, _kw=dict(out=sg[:, ch, :], in_=ps[b][:, 0:W], func=AF.Sigmoid): e.activation(**_kw)), [f"ps{b}"], ["sg"]))
                for ch in range(2):
                    fm_chunk(ch * 128, hT, "hT", W,
                             lambda b, ch=ch: V((lambda e, _kw=dict(out=hconv[:, ch, 30:30 + W], in0=ps[b][:, 0:W], in1=sg[:, ch, :], op=ALU.mult): e.tensor_tensor(**_kw)),
                                                [f"ps{b}", "sg"], ["hconv"]))
                for ch in range(2):
                    fm_chunk(512 + ch * 128, hT, "hT", W,
                             lambda b, ch=ch: A((lambda e, _kw=dict(out=uT[:, ch, :], in_=ps[b][:, 0:W], func=AF.Gelu): e.activation(**_kw)), [f"ps{b}"], ["uT"]))
                for chk in range(2):
                    def ev_qp(b, chk=chk):
                        for c in range(2):
                            V((lambda e, _kw=dict(out=Qbd[:, chk, c * 256:(c + 1) * 256], in0=ps[b][:, 0:W], scalar1=cst[:, C_MC + c:C_MC + c + 1],
                                                             scalar2=None, op0=ALU.mult): e.tensor_scalar(**_kw)), [f"ps{b}", "cst"], ["Qbd"])
                    fm_chunk(1024 + chk * 128, hT, "hT", W, ev_qp)
                for ch in range(2):
                    fm_chunk(1280 + ch * 128, hT, "hT", W,
                             lambda b, ch=ch: A((lambda e, _kw=dict(out=kT_all[:, ch, blk * W:(blk + 1) * W], in_=ps[b][:, 0:W]): e.copy(**_kw)), [f"ps{b}"], ["kT_all"]))
                for ch in range(2):
                    fm_chunk(1792 + ch * 128, hT, "hT", W,
                             lambda b, ch=ch: A((lambda e, _kw=dict(out=xdT[:, ch, 15:15 + W], in_=ps[b][:, 0:W]): e.copy(**_kw)), [f"ps{b}"], ["xdT"]))
                for tl in range(2):
                    tg = t0 + tl
                    b = tm_cols(1280, 512, hT, "hT", tl * 128)
                    V((lambda e, _kw=dict(out=kvo, in_=ps[b][:]): e.tensor_copy(**_kw)), [f"ps{b}"], ["kvo"])
                    P.dma("sp", (lambda tg: (lambda e, _kw=dict(out=o_kp[l, tg * 128:(tg + 1) * 128, :], in_=kvo[:, 0:256]): e.dma_start(**_kw)))(tg), reads=["kvo"], writes=["o_kp"], sem="st_kvo")
                    P.dma("sp", (lambda tg: (lambda e, _kw=dict(out=o_vp[l, tg * 128:(tg + 1) * 128, :], in_=kvo[:, 256:512]): e.dma_start(**_kw)))(tg), reads=["kvo"], writes=["o_vp"], sem="st_kvo")
                    G((lambda e, _kw=dict(out=V1[:, tg, :, 0:64], in_=kvo[:, 256:512].rearrange("p (h e) -> p h e", h=4)): e.tensor_copy(**_kw)), ["kvo"], ["V1"])
                    b = tm_cols(768, 256, hT, "hT", tl * 128)
                    A((lambda e, _kw=dict(out=vg, in_=ps[b][:, 0:256], func=AF.Gelu): e.activation(**_kw)), [f"ps{b}"], ["vg"])
                    ln_rows(vg, "vg", vn32, "vn32")
                    to_vnz(vn32, "vn32", vnz)
                    b = gb()
                    for ch in range(2):
                        for j in range(2):
                            MM(ps[b][:, ch * 128:(ch + 1) * 128], vnz[:, 2 * ch + j, :], WsT[:, 2 * ch + j, :], j == 0, j == 1, ["vnz", "WsT"], [f"ps{b}"], ch == 1 and j == 1)
                    V((lambda e, _kw=dict(out=sgt, in0=ps[b][:, 0:256].rearrange("p (c t) -> p c t", c=2), in1=BsT[:], op=ALU.add): e.tensor_tensor(**_kw)), [f"ps{b}", "BsT"], ["sgt"])
                    V((lambda e, _kw=dict(out=mixT[:, 2:4, tl * 128:(tl + 1) * 128], in0=sgt, in1=uT[:, :, tl * 128:(tl + 1) * 128], op=ALU.mult): e.tensor_tensor(**_kw)),
                      ["sgt", "uT"], ["mixB"])
                    if tg == NT - 1:
                        b = tm_cols(0, 512, hT, "hT", tl * 128)
                        A((lambda e, _kw=dict(out=tm1, in_=ps[b][:, 256:512], func=AF.Sigmoid): e.activation(**_kw)), [f"ps{b}"], ["tm1"])
                        V((lambda e, _kw=dict(out=tm1, in0=ps[b][:, 0:256], in1=tm1, op=ALU.mult): e.tensor_tensor(**_kw)), [f"ps{b}", "tm1"], ["tm1"])
                        P.dma("sp", (lambda e, _kw=dict(out=o_cp[l], in_=tm1[98:128, :]): e.dma_start(**_kw)), reads=["tm1"], writes=["o_cp"], sem="st_tm1")
                        b = tm_cols(1792, 256, hT, "hT", tl * 128)
                        A((lambda e, _kw=dict(out=tm2, in_=ps[b][:, 0:256]): e.copy(**_kw)), [f"ps{b}"], ["tm2"])
                        P.dma("sp", (lambda e, _kw=dict(out=o_pp[l], in_=tm2[113:128, :]): e.dma_start(**_kw)), reads=["tm2"], writes=["o_pp"], sem="st_tm2")
                for j in range(31):
                    for ch in range(2):
                        if j == 0:
                            V((lambda e, _kw=dict(out=cy[:, ch, :], in0=hconv[:, ch, 0:W], scalar1=vecfm[:, ch, 4:5], scalar2=vecfm[:, ch, 0:1],
                                                               op0=ALU.mult, op1=ALU.add): e.tensor_scalar(**_kw)), ["hconv", "vecfm"], [f"cy{ch}", "cy"])
                        else:
                            V((lambda e, _kw=dict(out=cy[:, ch, :], in0=hconv[:, ch, j:j + W], scalar=vecfm[:, ch, 4 + j:5 + j], in1=cy[:, ch, :],
                                                                           op0=ALU.mult, op1=ALU.add): e.scalar_tensor_tensor(**_kw)), ["hconv", "vecfm", f"cy{ch}"], [f"cy{ch}"])
                G((lambda e, _kw=dict(out=hconv[:, :, 0:30], in_=hconv[:, :, W:W + 30]): e.tensor_copy(**_kw)), ["hconv"], ["hconv"])
                P.op("pool", lambda e: e.nop(), ["cy0", "cy1"], ["cy"])
                conv_ln_silu(cy, W, ybf, ysq, cmean, cm2, crstd, mixT, "mixA", 0)
                V((lambda e, _kw=dict(out=pA[:, :, 1:15 + W], in0=xdT[:, :, 1:15 + W], in1=xdT[:, :, 0:14 + W], op=ALU.add): e.tensor_tensor(**_kw)), ["xdT"], ["pA"])
                V((lambda e, _kw=dict(out=pB[:, :, 3:15 + W], in0=pA[:, :, 3:15 + W], in1=pA[:, :, 1:13 + W], op=ALU.add): e.tensor_tensor(**_kw)), ["pA"], ["pB"])
                V((lambda e, _kw=dict(out=pA[:, 1, 7:15 + W], in0=pB[:, 1, 7:15 + W], in1=pB[:, 1, 3:11 + W], op=ALU.add): e.tensor_tensor(**_kw)), ["pB", "pA"], ["pA"])
                V((lambda e, _kw=dict(out=pB[64:128, 1, 15:15 + W], in0=pA[64:128, 1, 15:15 + W], in1=pA[64:128, 1, 7:7 + W], op=ALU.add): e.tensor_tensor(**_kw)), ["pA", "pB"], ["pB"])
                for ch in range(2):
                    for hf, src in ((0, pA), (1, pB)):
                        lo, hi = hf * 64, hf * 64 + 64
                        V((lambda e, _kw=dict(out=dT[lo:hi, ch, :], in0=src[lo:hi, ch, 15:15 + W], scalar=cst[lo:hi, C_INVW + ch:C_INVW + ch + 1],
                                                                                         in1=xdT[lo:hi, ch, 15:15 + W], op0=ALU.mult, op1=ALU.subtract): e.scalar_tensor_tensor(**_kw)), ["pA", "pB", "xdT", "cst"], ["dT"])
                        if blk == 0:
                            c16 = sc(16)
                            V((lambda e, _kw=dict(out=stt[lo:hi, c16:c16 + 16], in0=src[lo:hi, ch, 15:31],
                                                                                               in1=cst[lo:hi, C_ICNT + ch * 16:C_ICNT + ch * 16 + 16], op=ALU.mult): e.tensor_tensor(**_kw)), ["pA", "pB", "cst"], ["stt"])
                            V((lambda e, _kw=dict(out=dT[lo:hi, ch, 0:16], in0=stt[lo:hi, c16:c16 + 16], in1=xdT[lo:hi, ch, 15:31], op=ALU.subtract): e.tensor_tensor(**_kw)),
                              ["stt", "xdT", "dT"], ["dT"])
                G((lambda e, _kw=dict(out=xdT[:, :, 0:15], in_=xdT[:, :, W:W + 15]): e.tensor_copy(**_kw)), ["xdT", "pA", "pB"], ["xdT"])
                for ch in range(2):
                    b = gb()
                    MM(ps[b][:, 0:W], Wp[:, ch, :], dT[:, ch, :], True, True, ["Wp", "dT"], [f"ps{b}"], True)
                    V((lambda e, _kw=dict(out=mixT[:, 6 + ch, :], in0=ps[b][:, 0:W], scalar1=vecfm[:, ch, 3:4], scalar2=None, op0=ALU.mult): e.tensor_scalar(**_kw)),
                      [f"ps{b}", "vecfm"], ["mixD"])
                qb0, qb1 = 2 * blk, 2 * blk + 1
                pcount = 0
                for h in range(4):
                    bo = 6 + (h % 2)
                    r0 = (h % 2) * 64
                    MM(ps[bo][:], zer[:, 0:128], zer[:], True, True, ["zer"], [f"ps{bo}"], False)
                    for kb in range(qb1 + 1):
                        b = gb()
                        MM(ps[b][:], kT_all[r0:r0 + 64, h // 2, kb * 128:(kb + 1) * 128], Qbd[r0:r0 + 64, h // 2, :], True, True, ["kT_all", "Qbd"], [f"ps{b}"], True)
                        pi = pcount % NPT
                        pcount += 1
                        dd = qb1 - kb
                        A((lambda e, _kw=dict(out=pT[pi], in_=ps[b][:], func=AF.Exp, scale=ISQ, bias=cst[:, C_BT + h * 16 + dd:C_BT + h * 16 + dd + 1]): e.activation(**_kw)),
                          [f"ps{b}", "cst"], [f"pT{pi}"])
                        if kb >= qb0:
                            mo = B_MA if kb == qb0 else B_MB
                            G((lambda e, _kw=dict(out=pT[pi], in0=pT[pi], in1=cstb[:, mo:mo + 512], op=ALU.mult): e.tensor_tensor(**_kw)), [f"pT{pi}", "cstb"], [f"pT{pi}"])
                        for c in range(2):
                            for qi in range(2):
                                if kb <= qb0 + qi:
                                    a = c * 2 + qi
                                    last = (kb == qb0 + qi)
                                    MM(ps[bo][:, a * 128:a * 128 + 65], pT[pi][:, c * 256 + qi * 128:c * 256 + qi * 128 + 128], V1[:, kb, h, :], False, last,
                                       [f"pT{pi}", "V1"], [f"ps{bo}"], last and c == 1)
                    for qi in range(2):
                        a0, a1 = qi, 2 + qi
                        attn_finish(ps[bo][:, a0 * 128:a0 * 128 + 64], ps[bo][:, a0 * 128 + 64:a0 * 128 + 65],
                                    ps[bo][:, a1 * 128:a1 * 128 + 64], ps[bo][:, a1 * 128 + 64:a1 * 128 + 65], [f"ps{bo}"],
                                    obuf[:, qi, h * 64:(h + 1) * 64], f"obuf{qi}")
                for qi in range(2):
                    head_norm_to_mix(obuf[:, qi, :], f"obuf{qi}", osq, ycb, lam_init, mixT, "mixC", qi * 128)
                for tl in range(2):
                    out_proj_tile(mixT, ["mixA", "mixB", "mixC", "mixD"], tl * 128, x[:, t0 + tl, :], f"x{t0 + tl}")

            phase_barrier()
            TA.reset()
            W = 128
            hTs = TA.get("hT", (8, 128), BF16)
            mixT = TA.get("mixT", (8, W), BF16)
            sg = TA.get("sg", (2, W), F32)
            hs32 = TA.get("hs32", (2, 32, 34), F32)
            cy = TA.get("cy", (2, W), F32)
            ybf = TA.get("ybf", (2, W), BF16)
            ysq = TA.get("ysq", (2, W), BF16)
            cmean = TA.get("cmean", (W,), F32)
            cm2 = TA.get("cm2", (W,), F32)
            crstd = TA.get("crstd", (W,), F32)
            uT = TA.get("uT", (2, W), BF16)
            vg = TA.get("vg", (256,), F32)
            vn32 = TA.get("vn32", (256,), F32)
            vnz = TA.get("vnz", (4, 128), BF16)
            sgt = TA.get("sgt", (2, 128), F32)
            xds = TA.get("xds", (2, 32, 19), F32)
            sA = TA.get("pA", (2, 32, 19), F32)
            sB = TA.get("pB", (2, 32, 19), F32)
            dT = TA.get("dT", (2, W), BF16)
            R = TA.get("R", (8, 65), F32)
            R8 = TA.get("R8", (NCORES, 8 * 65), F32)
            obuf = TA.get("obuf", (256,), F32)
            osq = TA.get("osq", (256,), F32)
            ycb = TA.get("ycb", (256,), BF16)
            tm1 = TA.get("tm1", (256,), F32)
            tm2 = TA.get("tm2", (256,), F32)
            ARK.update(TA.keys)
            G((lambda e, _kw=dict(ap=vnz, constant=0.0): e.memset(**_kw)), [], ["vnz"])
            for ch in range(2):
                P.dma("sp", (lambda ch: (lambda e, _kw=dict(out=hs32[:, ch, :, 0:30], in_=sconv_fm_d[l, ch * 128:(ch + 1) * 128]): e.dma_start(**_kw)))(ch), writes=["hs32"], sem="hs32")
                P.dma("sp", (lambda ch: (lambda e, _kw=dict(out=xds[:, ch, :, 0:15], in_=spool_fm_d[l, ch * 128:(ch + 1) * 128]): e.dma_start(**_kw)))(ch), writes=["xds"], sem="xds")
            P.dma("sp", (lambda e, _kw=dict(out=o_cs[l, :, 0:26, :], in_=sconv_raw_d[l, :, 4:30, :]): e.dma_start(**_kw)), writes=["o_cs"], sem="st_d2d")
            P.dma("sp", (lambda e, _kw=dict(out=o_ps[l, :, 0:11, :], in_=spool_raw_d[l, :, 4:15, :]): e.dma_start(**_kw)), writes=["o_ps"], sem="st_d2d")
            pv_ = part_d.rearrange("c (b t a) e -> b t c (a e)", t=4, a=8)
            for t in range(4):
                P.dma("sp", (lambda t: (lambda e, _kw=dict(out=R8[32 * t:32 * t + 32, :, :], in_=pv_[:, t, :, :]): e.dma_start(**_kw)))(t),
                      writes=["R8"], sem="R8")
            Rf = R.rearrange("p a e -> p (a e)")
            V((lambda e, _kw=dict(out=Rf, in0=R8[:, 0, :], in1=R8[:, 1, :], op=ALU.add): e.tensor_tensor(**_kw)), ["R8"], ["R"])
            for c_ in range(2, NCORES):
                V((lambda e, _kw=dict(out=Rf, in0=Rf, in1=R8[:, c_, :], op=ALU.add): e.tensor_tensor(**_kw)), ["R8", "R"], ["R"])
            prenorm_tile(xs[:], "xs", 0, hTs, "hT", 0)
            bt = lambda ap: ap.rearrange("p (t b) -> p b t", t=4)
            for ch in range(2):
                fm_chunk(256 + ch * 128, hTs, "hT", W,
                         lambda b, ch=ch: A((lambda e, _kw=dict(out=sg[:, ch, :], in_=ps[b][:, 0:W], func=AF.Sigmoid): e.activation(**_kw)), [f"ps{b}"], ["sg"]))
            for ch in range(2):
                fm_chunk(ch * 128, hTs, "hT", W,
                         lambda b, ch=ch: V((lambda e, _kw=dict(out=hs32[:, ch, :, 30:34], in0=bt(ps[b][:, 0:W]), in1=bt(sg[:, ch, :]), op=ALU.mult): e.tensor_tensor(**_kw)),
                                            [f"ps{b}", "sg"], ["hs32"]))
            for ch in range(2):
                fm_chunk(512 + ch * 128, hTs, "hT", W,
                         lambda b, ch=ch: A((lambda e, _kw=dict(out=uT[:, ch, :], in_=ps[b][:, 0:W], func=AF.Gelu): e.activation(**_kw)), [f"ps{b}"], ["uT"]))
            for ch in range(2):
                fm_chunk(1792 + ch * 128, hTs, "hT", W,
                         lambda b, ch=ch: A((lambda e, _kw=dict(out=xds[:, ch, :, 15:19], in_=bt(ps[b][:, 0:W])): e.copy(**_kw)), [f"ps{b}"], ["xds"]))
            b = tm_cols(0, 512, hTs, "hT", 0)
            A((lambda e, _kw=dict(out=tm1, in_=ps[b][:, 256:512], func=AF.Sigmoid): e.activation(**_kw)), [f"ps{b}"], ["tm1"])
            V((lambda e, _kw=dict(out=tm1, in0=ps[b][:, 0:256], in1=tm1, op=ALU.mult): e.tensor_tensor(**_kw)), [f"ps{b}", "tm1"], ["tm1"])
            b = tm_cols(1792, 256, hTs, "hT", 0)
            A((lambda e, _kw=dict(out=tm2, in_=ps[b][:, 0:256]): e.copy(**_kw)), [f"ps{b}"], ["tm2"])
            b = tm_cols(768, 256, hTs, "hT", 0)
            A((lambda e, _kw=dict(out=vg, in_=ps[b][:, 0:256], func=AF.Gelu): e.activation(**_kw)), [f"ps{b}"], ["vg"])
            ln_rows(vg, "vg", vn32, "vn32")
            for t in range(4):
                P.dma("sp", (lambda t: (lambda e, _kw=dict(out=o_cs[l, :, 26 + t, :], in_=tm1[32 * t:32 * t + 32, :]): e.dma_start(**_kw)))(t), reads=["tm1"], writes=["o_cs"], sem="st_tm1")
                P.dma("sp", (lambda t: (lambda e, _kw=dict(out=o_ps[l, :, 11 + t, :], in_=tm2[32 * t:32 * t + 32, :]): e.dma_start(**_kw)))(t), reads=["tm2"], writes=["o_ps"], sem="st_tm2")
            P.dma("sp", (lambda e, _kw=dict(out=o_gs[l], in_=vn32): e.dma_start(**_kw)), reads=["vn32"], writes=["o_gs"], sem="st_vn32")
            to_vnz(vn32, "vn32", vnz)
            b = gb()
            for ch in range(2):
                for j in range(2):
                    MM(ps[b][:, ch * 128:(ch + 1) * 128], vnz[:, 2 * ch + j, :], WsS[:, 2 * ch + j, :], j == 0, j == 1, ["vnz", "WsS"], [f"ps{b}"], ch == 1 and j == 1)
            V((lambda e, _kw=dict(out=sgt.rearrange("p c (t b) -> p c t b", t=4), in0=ps[b][:, 0:256].rearrange("p (c t b) -> p c t b", c=2, t=4),
                                             in1=BsT[:, :, 0:4].unsqueeze(3).to_broadcast([128, 2, 4, 32]), op=ALU.add): e.tensor_tensor(**_kw)), [f"ps{b}", "BsT"], ["sgt"])
            V((lambda e, _kw=dict(out=mixT[:, 2:4, :], in0=sgt, in1=uT, op=ALU.mult): e.tensor_tensor(**_kw)), ["sgt", "uT"], ["mixB"])
            cy4 = [cy[:, ch, :].rearrange("p (t b) -> p b t", t=4) for ch in range(2)]
            for j in range(31):
                for ch in range(2):
                    if j == 0:
                        V((lambda e, _kw=dict(out=cy4[ch], in0=hs32[:, ch, :, 0:4], scalar1=vecfm[:, ch, 4:5], scalar2=vecfm[:, ch, 0:1], op0=ALU.mult, op1=ALU.add): e.tensor_scalar(**_kw)),
                          ["hs32", "vecfm"], [f"cy{ch}", "cy"])
                    else:
                        V((lambda e, _kw=dict(out=cy4[ch], in0=hs32[:, ch, :, j:j + 4], scalar=vecfm[:, ch, 4 + j:5 + j], in1=cy4[ch], op0=ALU.mult, op1=ALU.add): e.scalar_tensor_tensor(**_kw)),
                          ["hs32", "vecfm", f"cy{ch}"], [f"cy{ch}"])
            P.op("pool", lambda e: e.nop(), ["cy0", "cy1"], ["cy"])
            conv_ln_silu(cy, W, ybf, ysq, cmean, cm2, crstd, mixT, "mixA", 0)
            V((lambda e, _kw=dict(out=sA[:, :, :, 1:19], in0=xds[:, :, :, 1:19], in1=xds[:, :, :, 0:18], op=ALU.add): e.tensor_tensor(**_kw)), ["xds"], ["pA"])
            V((lambda e, _kw=dict(out=sB[:, :, :, 3:19], in0=sA[:, :, :, 3:19], in1=sA[:, :, :, 1:17], op=ALU.add): e.tensor_tensor(**_kw)), ["pA"], ["pB"])
            V((lambda e, _kw=dict(out=sA[:, 1, :, 7:19], in0=sB[:, 1, :, 7:19], in1=sB[:, 1, :, 3:15], op=ALU.add): e.tensor_tensor(**_kw)), ["pB", "pA"], ["pA"])
            V((lambda e, _kw=dict(out=sB[64:128, 1, :, 15:19], in0=sA[64:128, 1, :, 15:19], in1=sA[64:128, 1, :, 7:11], op=ALU.add): e.tensor_tensor(**_kw)), ["pA", "pB"], ["pB"])
            for ch in range(2):
                for hf, src in ((0, sA), (1, sB)):
                    lo, hi = hf * 64, hf * 64 + 64
                    V((lambda e, _kw=dict(out=dT[lo:hi, ch, :].rearrange("p (t b) -> p b t", t=4), in0=src[lo:hi, ch, :, 15:19],
                                                                                     scalar=cst[lo:hi, C_INVW + ch:C_INVW + ch + 1], in1=xds[lo:hi, ch, :, 15:19],
                                                                                     op0=ALU.mult, op1=ALU.subtract): e.scalar_tensor_tensor(**_kw)), ["pA", "pB", "xds", "cst"], ["dT"])
            for ch in range(2):
                b = gb()
                MM(ps[b][:, 0:W], Wp[:, ch, :], dT[:, ch, :], True, True, ["Wp", "dT"], [f"ps{b}"], True)
                V((lambda e, _kw=dict(out=mixT[:, 6 + ch, :], in0=ps[b][:, 0:W], scalar1=vecfm[:, ch, 3:4], scalar2=None, op0=ALU.mult): e.tensor_scalar(**_kw)), [f"ps{b}", "vecfm"], ["mixD"])
            for h in range(4):
                attn_finish(R[:, 2 * h, 0:64], R[:, 2 * h, 64:65], R[:, 2 * h + 1, 0:64], R[:, 2 * h + 1, 64:65], ["R"], obuf[:, h * 64:(h + 1) * 64], "obuf0")
            head_norm_to_mix(obuf, "obuf0", osq, ycb, lam_init, mixT, "mixC", 0)
            if DBG and l == 0:
                dbgt = TA.get("dbgt", (8, 128), F32)
                ARK.update(TA.keys)
                G((lambda e, _kw=dict(out=dbgt, in_=mixT): e.tensor_copy(**_kw)), ["mixA", "mixB", "mixC", "mixD"], ["dbgt"])
                P.dma("sp", (lambda e, _kw=dict(out=o_dbg, in_=dbgt): e.dma_start(**_kw)), reads=["dbgt"], writes=["o_dbg"], sem="st_dbg")
            out_proj_tile(mixT, ["mixA", "mixB", "mixC", "mixD"], 0, xs[:], "xs")

            phase_barrier(["w_in", "w_out", "w_down"])
            TA.reset()
            hT = TA.get("hT", (8, 512), BF16)
            hTs = TA.get("hTs", (8, 128), BF16)
            NW = 3
            wgu = [TA.get(f"wgu{i}", (8, 256), BF16) for i in range(NW)]
            sgb = [TA.get(f"sgb{i}", (512,), F32) for i in range(2)]
            actT = TA.get("actT", (NFC, 512), BF16)
            actTs = TA.get("actTs", (NFC, 128), BF16)
            ARK.update(TA.keys)
            wd_v = w_down_d[l].rearrange("(fc p) n -> p fc n", p=128)
            for hf in range(2):
                P.dma("pool", (lambda hf: (lambda e, _kw=dict(out=w_down_sb[:, hf * 11:(hf + 1) * 11, :], in_=wd_v[:, hf * 11:(hf + 1) * 11, :]): e.dma_start(**_kw)))(hf),
                      writes=["w_down"], sem="w_down")
            P.dma("sp", (lambda e, _kw=dict(out=gpost[:], in_=gpost_d[l, 1].partition_broadcast(128)): e.dma_start(**_kw)), writes=["gpost"])
            prenorm_tile(xs[:], "xs", 1, hTs, "hTs", 0)
            wcount = 0
            for fb in range(NF):
                last_blk = (fb == NF - 1)
                for tl in range(4):
                    prenorm_tile(x[:, 4 * fb + tl, :], f"x{4 * fb + tl}", 1, hT, "hT", tl * 128)
                for f_ in range(NFC):
                    wi = wcount % NW
                    wcount += 1
                    P.dma("pool", (lambda wi, f_: (lambda e, _kw=dict(out=wgu[wi], in_=w_gu_d[l, f_]): e.dma_start(**_kw)))(wi, f_), writes=[f"wgu{wi}"], sem=f"wgu{wi}")
                    bg, bu = gb(), gb()
                    for kc in range(8):
                        MM(ps[bg][:], wgu[wi][:, kc, 0:128], hT[:, kc, :], kc == 0, kc == 7, [f"wgu{wi}", "hT"], [f"ps{bg}"], kc == 7)
                    for kc in range(8):
                        MM(ps[bu][:], wgu[wi][:, kc, 128:256], hT[:, kc, :], kc == 0, kc == 7, [f"wgu{wi}", "hT"], [f"ps{bu}"], kc == 7)
                    si = f_ % 2
                    A((lambda e, _kw=dict(out=sgb[si], in_=ps[bg][:], func=AF.Silu): e.activation(**_kw)), [f"ps{bg}"], [f"sgb{si}"])
                    V((lambda e, _kw=dict(out=actT[:, f_, :], in0=ps[bu][:], in1=sgb[si], op=ALU.mult): e.tensor_tensor(**_kw)), [f"ps{bu}", f"sgb{si}"], ["actT"])
                    if last_blk:
                        bg = gb()
                        for kc in range(8):
                            MM(ps[bg][:, 0:128], wgu[wi][:, kc, 0:128], hTs[:, kc, :], kc == 0, kc == 7, [f"wgu{wi}", "hTs"], [f"ps{bg}"], False)
                        for kc in range(8):
                            MM(ps[bg][:, 128:256], wgu[wi][:, kc, 128:256], hTs[:, kc, :], kc == 0, kc == 7, [f"wgu{wi}", "hTs"], [f"ps{bg}"], kc == 7)
                        A((lambda e, _kw=dict(out=sgb[si][:, 0:128], in_=ps[bg][:, 0:128], func=AF.Silu): e.activation(**_kw)), [f"ps{bg}"], [f"sgb{si}"])
                        V((lambda e, _kw=dict(out=actTs[:, f_, :], in0=ps[bg][:, 128:256], in1=sgb[si][:, 0:128], op=ALU.mult): e.tensor_tensor(**_kw)),
                          [f"ps{bg}", f"sgb{si}"], ["actTs"])

                def down_tile(aT, akey, mcol, xap, xkey):
                    bA, bB = gb(), gb()
                    for n, bk in enumerate((bA, bB)):
                        for f2 in range(NFC):
                            MM(ps[bk][:], aT[:, f2, mcol:mcol + 128], w_down_sb[:, f2, n * 512:(n + 1) * 512], f2 == 0, f2 == NFC - 1, [akey, "w_down"], [f"ps{bk}"], f2 == NFC - 1)
                    postnorm_add(bA, bB, xap, xkey)
                for tl in range(4):
                    down_tile(actT, "actT", tl * 128, x[:, 4 * fb + tl, :], f"x{4 * fb + tl}")
                if last_blk:
                    down_tile(actTs, "actTs", 0, xs[:], "xs")

        if doA:
            phase_barrier(["w_in", "w_out", "w_down"])
            P.dma("pool", (lambda e, _kw=dict(out=w_in_sb[:, :, 1024:1792], in_=w_in_a_d.rearrange("(kc p) n -> p kc n", p=128)): e.dma_start(**_kw)), writes=["w_in"], sem="w_in")
            P.dma("sp", (lambda e, _kw=dict(out=gfm[:], in_=gfm_a_d): e.dma_start(**_kw)), writes=["gfm"])
            phase_barrier()
            TA.reset()
            hTs = TA.get("hT", (8, 128), BF16)
            Qblk = TA.get("Qblk", (2, 32, 32), BF16)
            kTs = TA.get("kTs", (2, 128), BF16)
            Vnew = TA.get("Vnew", (258,), BF16)
            kvo = TA.get("kvo", (512,), F32)
            NKV = 4
            kvs = [TA.get(f"kvs{i}", (514,), BF16) for i in range(NKV)]
            ktT = [TA.get(f"ktT{i}", (256,), BF16) for i in range(2)]
            Sx = [TA.get(f"Sx{i}", (32,), F32) for i in range(2)]
            pTs = [TA.get(f"pTs{i}", (32,), BF16) for i in range(2)]
            msk = TA.get("msk", (256,), F32)
            stg = [TA.get(f"stg{i}", (66,), F32) for i in range(2)]
            ARK.update(TA.keys)
            G((lambda e, _kw=dict(ap=Qblk, constant=0.0): e.memset(**_kw)), [], ["Qblk"])
            G((lambda e, _kw=dict(ap=Vnew[:, 256:258], constant=0.125): e.memset(**_kw)), [], ["Vnew"])
            for i in range(NKV):
                G((lambda e, _kw=dict(ap=kvs[i][:, 512:514], constant=1.0): e.memset(**_kw)), [], [f"kvs{i}"])
            prenorm_tile(xs[:], "xs", 0, hTs, "hT", 0)
            for chk in range(2):
                def ev_q(b, chk=chk):
                    pv = ps[b][:, 0:128].rearrange("p (t b) -> p b t", t=4)
                    for h2 in range(2):
                        for c in range(2):
                            h = 2 * chk + h2
                            V((lambda e, _kw=dict(out=Qblk[:, chk, :, :].rearrange("p b (t x) -> p b t x", t=4)[:, :, :, h * 2 + c],
                                                                          in0=pv, scalar1=cst[:, C_MH2C + h2 * 2 + c:C_MH2C + h2 * 2 + c + 1], scalar2=None, op0=ALU.mult): e.tensor_scalar(**_kw)),
                              [f"ps{b}", "cst"], ["Qblk"])
                fm_chunk(1024 + chk * 128, hTs, "hT", 128, ev_q)
            for chk in range(2):
                fm_chunk(1280 + chk * 128, hTs, "hT", 128,
                         lambda b, chk=chk: A((lambda e, _kw=dict(out=kTs[:, chk, :], in_=ps[b][:, 0:128]): e.copy(**_kw)), [f"ps{b}"], ["kTs"]))
            b = tm_cols(1280, 512, hTs, "hT", 0)
            V((lambda e, _kw=dict(out=kvo, in_=ps[b][:]): e.tensor_copy(**_kw)), [f"ps{b}"], ["kvo"])
            P.dma("sp", (lambda e, _kw=dict(out=o_ks, in_=kvo[:, 0:256]): e.dma_start(**_kw)), reads=["kvo"], writes=["o_ks"], sem="st_kvo")
            P.dma("sp", (lambda e, _kw=dict(out=o_vs, in_=kvo[:, 256:512]): e.dma_start(**_kw)), reads=["kvo"], writes=["o_vs"], sem="st_kvo")
            G((lambda e, _kw=dict(out=Vnew[:, 0:256], in0=kvo[:, 256:512], scalar1=0.125, scalar2=None, op0=ALU.mult): e.tensor_scalar(**_kw)), ["kvo"], ["Vnew"])
            cin_ap = o_part
            kvl = kv_d
            tcount = 0
            bo_s = 6
            for bb in range(32):
                bo_s = 6 + (bb % 2)
                for t in range(T8 + 1):
                    new = (t == T8)
                    if not new:
                        sl_ = tcount % NKV
                        kr = tcount % 2
                        tcount += 1
                        col = bb * T8 + t
                        P.dma("pool", (lambda sl_, col: (lambda e, _kw=dict(out=kvs[sl_][:, 0:512], out_offset=None, in_=kvl,
                                                                                         in_offset=bass.IndirectOffsetOnAxis(ap=idx[:, col:col + 1], axis=0)): e.indirect_dma_start(**_kw)))(sl_, col),
                              reads=["idx"], writes=[f"kvs{sl_}"], sem=f"kvs{sl_}")
                        b1 = gb()
                        for hh in range(2):
                            TR(psb[b1][:, hh * 128:(hh + 1) * 128], kvs[sl_][:, hh * 128:(hh + 1) * 128], idb, [f"kvs{sl_}", "cstb"], [f"ps{b1}"], hh == 1)
                        A((lambda e, _kw=dict(out=ktT[kr], in_=psb[b1][:, 0:256]): e.copy(**_kw)), [f"ps{b1}"], [f"ktT{kr}"])
                        lh = [ktT[kr][:, 0:128], ktT[kr][:, 128:256]]
                        lk = [f"ktT{kr}"]
                        bias_ap = cst[:, C_BIAS + t * 32:C_BIAS + (t + 1) * 32]
                        bkey = "cst"
                        rhs_v = kvs[sl_][:, 256:513]
                        rk = [f"kvs{sl_}"]
                    else:
                        lh = [kTs[:, 0, :], kTs[:, 1, :]]
                        lk = ["kTs"]
                        bias_ap = cstb[:, B_BN + bb * 32:B_BN + (bb + 1) * 32]
                        bkey = "cstb"
                        rhs_v = Vnew[:, 0:257]
                        rk = ["Vnew"]
                    b2 = gb()
                    for hh in range(2):
                        MM(ps[b2][:, 0:32], lh[hh], Qblk[:, hh, bb, :], hh == 0, hh == 1, lk + ["Qblk"], [f"ps{b2}"], hh == 1)
                    si = (bb * (T8 + 1) + t) % 2
                    V((lambda e, _kw=dict(out=Sx[si], in0=ps[b2][:, 0:32], scalar=ISQ, in1=bias_ap, op0=ALU.mult, op1=ALU.add): e.scalar_tensor_tensor(**_kw)),
                      [f"ps{b2}", bkey], [f"Sx{si}"])
                    A((lambda e, _kw=dict(out=pTs[si], in_=Sx[si], func=AF.Exp): e.activation(**_kw)), [f"Sx{si}"], [f"pTs{si}"])
                    MM(ps[bo_s][0:32, 0:257], pTs[si], rhs_v, t == 0, new, [f"pTs{si}"] + rk, [f"ps{bo_s}"], new)
                sg_i = bb % 2
                V((lambda e, _kw=dict(out=msk[0:32, :], in0=ps[bo_s][0:32, 0:256], in1=cst[0:32, C_HM:C_HM + 256], op=ALU.mult): e.tensor_tensor(**_kw)),
                  [f"ps{bo_s}", "cst"], ["msk"])
                V((lambda e, _kw=dict(out=stg[sg_i][0:32, 0:64], in_=msk[0:32, :].rearrange("p (h e) -> p e h", h=4), axis=AX.X, op=ALU.add): e.tensor_reduce(**_kw)),
                  ["msk"], [f"stg{sg_i}"])
                A((lambda e, _kw=dict(out=stg[sg_i][0:32, 64:65], in_=ps[bo_s][0:32, 256:257]): e.copy(**_kw)), [f"ps{bo_s}"], [f"stg{sg_i}"])
                P.dma("sp", (lambda sg_i, bb: (lambda e, _kw=dict(out=cin_ap[bb * 32:(bb + 1) * 32, :], in_=stg[sg_i][0:32, 0:65]): e.dma_start(**_kw)))(sg_i, bb),
                      reads=[f"stg{sg_i}"], writes=["o_part"], sem=f"st_stg{sg_i}")

        if doB:
            yv = o_yp.rearrange("(t p) d -> p t d", p=128)
            for t0 in range(0, NT, 4):
                P.dma("sp", (lambda t0: (lambda e, _kw=dict(out=yv[:, t0:t0 + 4, :], in_=x[:, t0:t0 + 4, :]): e.dma_start(**_kw)))(t0),
                      reads=[f"x{t}" for t in range(t0, t0 + 4)], writes=["o_yp"], sem="st_x")
            P.dma("sp", (lambda e, _kw=dict(out=o_ys, in_=xs[:]): e.dma_start(**_kw)), reads=["xs"], writes=["o_ys"], sem="st_xs")
        P.finish(out_keys)
        nsem = len(P.dsem) + len(P.esem)
        assert nsem < 140, nsem
    return nc


_CACHE = {}


def run(inputs, cfg):
    L, S = cfg["L"], cfg["S"]
    inp = {k: np.asarray(v) for k, v in inputs.items()}
    st = host_static(inp, cfg)
    state = {"xp": [inp["x_prompt"][c] for c in range(NCORES)],
             "xs": inp["x_sample"].transpose(1, 0, 2).reshape(128, D), "part": None}
    acc = {k: [None] * L for k in ("kp", "vp", "cp", "pp", "ks", "vs", "cs", "ps", "gs")}
    for stage in range(L + 1):
        doA, doB = stage < L, stage >= 1
        key = (tuple(sorted(cfg.items())), doA, doB)
        if key not in _CACHE:
            _CACHE[key] = build(cfg, stage)
        nc = _CACHE[key]
        maps = stage_maps(inp, cfg, stage, st, state)
        res = run_bass_kernel_spmd(nc, maps, core_ids=list(range(NCORES)))
        R_ = res.results
        del maps
        cat = lambda k: np.stack([np.asarray(R_[c][k]) for c in range(NCORES)], axis=0)
        if doB:
            lb = stage - 1
            acc["kp"][lb] = cat("o_kp")[:, 0]
            acc["vp"][lb] = cat("o_vp")[:, 0]
            acc["cp"][lb] = cat("o_cp")[:, 0]
            acc["pp"][lb] = cat("o_pp")[:, 0]
            r0 = R_[0]
            acc["cs"][lb] = np.asarray(r0["o_cs"])[0]
            acc["ps"][lb] = np.asarray(r0["o_ps"])[0]
            acc["gs"][lb] = np.asarray(r0["o_gs"])[0].reshape(4, 32, 256).transpose(1, 0, 2)
            state["xp"] = [np.asarray(R_[c]["o_yp"]) for c in range(NCORES)]
            state["xs"] = np.asarray(r0["o_ys"])
        if doA:
            la = stage
            r0 = R_[0]
            acc["ks"][la] = np.asarray(r0["o_ks"]).reshape(4, 32, 256).transpose(1, 0, 2)
            acc["vs"][la] = np.asarray(r0["o_vs"]).reshape(4, 32, 256).transpose(1, 0, 2)
            state["part"] = cat("o_part")
    y_p = np.stack(state["xp"], axis=0)
    y_s = state["xs"].reshape(4, 32, D).transpose(1, 0, 2)
    stk = lambda k: np.stack(acc[k], axis=0)
    k_p = stk("kp").reshape(L, NCORES, S, 4, 2, 32)
    v_p = stk("vp").reshape(L, NCORES, S, 4, 64)
    k_s = stk("ks").reshape(L, 32, 4, 4, 2, 32)
    v_s = stk("vs").reshape(L, 32, 4, 4, 64)
    outs = (y_p, y_s, k_p, v_p, stk("cp"), stk("pp"), k_s, v_s, stk("cs"), stk("ps"), stk("gs"))
    return tuple(np.ascontiguousarray(o, dtype=np.float32) for o in outs)


def kernel(**inputs):
    cfg = make_cfg()
    return run(inputs, cfg)
```

```python
import contextlib
import math
import numpy as np
import concourse.bass as bass
import concourse.mybir as mybir
from concourse.bass_utils import run_bass_kernel_spmd

F32 = mybir.dt.float32
BF16 = mybir.dt.bfloat16
I32 = mybir.dt.int32
AF = mybir.ActivationFunctionType
ALU = mybir.AluOpType
AX = mybir.AxisListType

NCORES = 8
D = 1024
DFF = 2816
NFC = 22
RMS_EPS = 1e-6
LN_EPS = 1e-5
SLOPES = [2.0 ** (-8.0 * (h + 1) / 4) for h in range(4)]
ISQ = 32 ** -0.5
NEG = -30000.0
ENGS = ["pe", "act", "dve", "pool", "sp"]

C_ID, C_TRI, C_BLK, C_BT, C_INVW, C_ICNT, C_MC, C_MH2C, C_I16, C_HM, C_E8, C_BIAS = 0, 128, 256, 384, 448, 450, 482, 484, 488, 489, 745, 873
B_ID, B_MA, B_MB, B_ONE, B_SEL, B_BN, NCB = 0, 128, 640, 1152, 1280, 1408, 2432
VB_LNG, VB_LNB, VB_GSUB, VB_LQ1, VB_LK1, VB_LQ2, VB_LK2, NVB = 0, 256, 512, 576, 608, 640, 672, 704


class Prog:
    def __init__(self, nc, stack):
        self.nc = nc
        self.stack = stack
        self.q = {e: [] for e in ENGS}
        self.esem = {e: stack.enter_context(nc.semaphore("es_" + e)) for e in ENGS}
        self.ecnt = {e: 0 for e in ENGS}
        self.seen = {e: {} for e in ENGS}
        self.buf = {}
        self.dsem = {}
        self.dcnt = {}
        self.scope = None
        self.use_scopes = False

    def _st(self, k):
        if k not in self.buf:
            self.buf[k] = {"w": None, "r": []}
        return self.buf[k]

    def _waits(self, eng, reads, writes):
        evs = []
        for k in reads:
            s = self._st(k)
            if s["w"] is not None:
                evs.append(s["w"])
        for k in writes:
            s = self._st(k)
            if s["w"] is not None:
                evs.append(s["w"])
            evs.extend(s["r"])
        need = {}
        for kind, sid, val in evs:
            if kind == "E" and sid == eng and eng == "pe":
                continue
            key = (kind, sid)
            if self.seen[eng].get(key, 0) >= val:
                continue
            need[key] = max(need.get(key, 0), val)
        out = []
        for key, val in need.items():
            self.seen[eng][key] = val
            sem = self.esem[key[1]] if key[0] == "E" else self.dsem[key[1]]
            out.append((sem, val))
        return out

    def _record(self, ev, reads, writes):
        for k in reads:
            self._st(k)["r"].append(ev)
        for k in writes:
            s = self._st(k)
            s["w"] = ev
            s["r"] = []

    def op(self, eng, fn, reads=(), writes=(), signal=True):
        waits = self._waits(eng, reads, writes)
        if signal:
            self.ecnt[eng] += 1
            ev = ("E", eng, self.ecnt[eng])
            inc = (self.esem[eng], 1)
        else:
            ev = ("E", eng, self.ecnt[eng] + 1)
            inc = None
        self.q[eng].append((waits, fn, inc, self.scope))
        self._record(ev, reads, writes)

    def dma(self, eng, fn, reads=(), writes=(), sem=None, inc=16):
        if sem is None:
            sem = writes[0]
        if sem not in self.dsem:
            self.dsem[sem] = self.stack.enter_context(self.nc.semaphore("ds_" + sem))
            self.dcnt[sem] = 0
        waits = self._waits(eng, reads, writes)
        self.dcnt[sem] += inc
        ev = ("D", sem, self.dcnt[sem])
        self.q[eng].append((waits, fn, (self.dsem[sem], inc), self.scope))
        self._record(ev, reads, writes)

    def barrier(self, keys):
        self.op("sp", lambda e: e.nop(), reads=(), writes=list(keys))

    def finish(self, out_keys):
        waits = self._waits("sp", out_keys, [])
        self.q["sp"].append((waits, None, None, None))
        nc = self.nc
        with nc.Block() as block:
            def run(engobj, lst):
                cur, cur_id = None, None
                for waits, fn, inc, scope in lst:
                    if self.use_scopes and scope != cur:
                        if cur is not None:
                            nc.leave_named_scope(cur, cur_id, False)
                        cur = scope
                        if cur is not None:
                            cur_id, _ = nc.enter_named_scope(cur, False)
                    for sem, val in waits:
                        engobj.wait_ge(sem, val)
                    if fn is None:
                        continue
                    ins = fn(engobj)
                    if inc is not None:
                        ins.then_inc(inc[0], inc[1])
                if self.use_scopes and cur is not None:
                    nc.leave_named_scope(cur, cur_id, False)

            @block.tensor
            def _(e):
                run(e, self.q["pe"])

            @block.scalar
            def _(e):
                run(e, self.q["act"])

            @block.vector
            def _(e):
                run(e, self.q["dve"])

            @block.gpsimd
            def _(e):
                run(e, self.q["pool"])

            @block.sync
            def _(e):
                run(e, self.q["sp"])


class Arena:
    def __init__(self, ap, ncols):
        self.ap = ap
        self.n = ncols
        self.off = 0
        self.keys = set()
        self.peak = 0

    def reset(self):
        self.off = 0

    def get(self, key, free, dtype):
        n = int(np.prod(free))
        cols = n * (2 if dtype in (F32, I32) else 1)
        cols = (cols + 1) // 2 * 2
        assert self.off + cols <= self.n, ("arena overflow", key, self.off + cols, self.n)
        a = self.ap[:, self.off:self.off + cols]
        self.off += cols
        self.peak = max(self.peak, self.off)
        self.keys.add(key)
        if dtype in (F32, I32):
            a = a.bitcast(dtype)
        a = a[:, 0:n]
        if len(free) == 2:
            a = a.rearrange("p (a b) -> p a b", a=free[0])
        elif len(free) == 3:
            a = a.rearrange("p (a b c) -> p a b c", a=free[0], b=free[1])
        return a


def make_cfg(L=4, S=2048, NP=64, NPOOL=2560):
    return dict(L=L, S=S, NP=NP, NPOOL=NPOOL, NT=S // 128, NB=S // 256, NF=S // 512, T8=NP // 8, PAST=NP * 128)


def host_consts(cfg, core):
    T8, PAST = cfg["T8"], cfg["PAST"]
    p = np.arange(128)
    cst = np.zeros((128, C_BIAS + T8 * 32), np.float32)
    cst[:, C_ID:C_ID + 128] = np.eye(128)
    cst[:, C_TRI:C_TRI + 128] = (p[:, None] <= p[None, :])
    cst[:, C_BLK:C_BLK + 128] = ((p[:, None] % 32) == (p[None, :] % 32))
    for h in range(4):
        for dd in range(16):
            cst[:, C_BT + h * 16 + dd] = SLOPES[h] * (p - 127 - dd * 128)
    wtab = np.zeros((128, 2))
    wtab[:64, 0], wtab[64:, 0], wtab[:64, 1], wtab[64:, 1] = 2, 4, 8, 16
    cst[:, C_INVW:C_INVW + 2] = 1.0 / wtab
    for ch in range(2):
        for pos in range(16):
            cst[:, C_ICNT + ch * 16 + pos] = 1.0 / np.minimum(pos + 1, wtab[:, ch])
    for c in range(2):
        cst[:, C_MC + c] = ((p // 32) % 2 == c)
    for h2 in range(2):
        for c in range(2):
            cst[:, C_MH2C + h2 * 2 + c] = ((p // 64 == h2) & ((p // 32) % 2 == c))
    cst[:, C_I16] = p % 16
    r = np.arange(32)
    cols = np.arange(256)
    cst[:32, C_HM:C_HM + 256] = (((r % 8) // 2)[:, None] == (cols // 64)[None, :])
    cst[:8, C_E8:C_E8 + 128] = (np.arange(8)[:, None] == (p // 16)[None, :])
    col = np.arange(32)
    hcol = (col % 8) // 2
    sl = np.array(SLOPES)[hcol]
    for t in range(T8):
        kpos = (8 * t + p // 16) * 128 + 16 * core + p % 16
        cst[:, C_BIAS + t * 32:C_BIAS + (t + 1) * 32] = sl[None, :] * (kpos[:, None] - (PAST + 3))
    cb = np.zeros((128, NCB), np.float32)
    cb[:, B_ID:B_ID + 128] = np.eye(128)
    tri = (p[:, None] <= p[None, :]).astype(np.float32)
    for c in range(2):
        cb[:, B_MA + c * 256:B_MA + c * 256 + 128] = tri
        cb[:, B_MA + c * 256 + 128:B_MA + c * 256 + 256] = 1.0
        cb[:, B_MB + c * 256 + 128:B_MB + c * 256 + 256] = tri
    cb[:, B_ONE:B_ONE + 128] = 1.0 / 256
    cb[:4, B_SEL:B_SEL + 128] = (np.arange(4)[:, None] == (p // 32)[None, :])
    tq = col // 8
    tk, bk = p // 32, p % 32
    for b in range(32):
        ok = (bk[:, None] == b) & (tk[:, None] <= tq[None, :])
        cb[:, B_BN + b * 32:B_BN + (b + 1) * 32] = np.where(ok, sl[None, :] * (tk[:, None] - 3.0), NEG)
    return cst, cb


def host_static(inp, cfg):
    L, S, NP, NPOOL, T8 = cfg["L"], cfg["S"], cfg["NP"], cfg["NPOOL"], cfg["T8"]
    f = lambda a: np.ascontiguousarray(a, dtype=np.float32)
    st = {}
    st["pt8"] = np.ascontiguousarray(np.asarray(inp["page_table"]).reshape(32, T8, 8).transpose(2, 0, 1).reshape(8, 32 * T8).astype(np.int32))
    vec = np.zeros((L, 128, 2, 35), np.float32)
    fm2 = lambda a: np.asarray(a).reshape(L, 2, 128).transpose(0, 2, 1)
    vec[:, :, :, 0] = fm2(inp["conv_b"])
    vec[:, :, :, 1] = fm2(inp["conv_ln_g"])
    vec[:, :, :, 2] = fm2(inp["conv_ln_b"])
    vec[:, :, :, 3] = fm2(inp["pool_scale"])
    vec[:, :, :, 4:35] = np.asarray(inp["conv_w"]).reshape(L, 31, 2, 128).transpose(0, 3, 2, 1)
    st["vecfm"] = vec
    gfm = np.zeros((L, 128, 2, 8), np.float32)
    gfm[:, :, 0, :] = np.asarray(inp["g_mix_pre"]).reshape(L, 8, 128).transpose(0, 2, 1)
    gfm[:, :, 1, :] = np.asarray(inp["g_ffn_pre"]).reshape(L, 8, 128).transpose(0, 2, 1)
    st["gfm"] = gfm
    st["vbc"] = f(np.concatenate([np.asarray(inp[k]).reshape(L, -1) for k in
                                  ("sgu_ln_g", "sgu_ln_b", "attn_sub_g", "lambda_q1", "lambda_k1", "lambda_q2", "lambda_k2")], axis=1).reshape(L, 1, NVB))
    st["gpost"] = f(np.stack([np.asarray(inp["g_mix_post"]), np.asarray(inp["g_ffn_post"])], axis=1).reshape(L, 2, 1, D))
    st["consts"] = [host_consts(cfg, c) for c in range(NCORES)]
    return st


def stage_maps(inp, cfg, stage, st, state):
    L, S, NP, NPOOL, T8 = cfg["L"], cfg["S"], cfg["NP"], cfg["NPOOL"], cfg["T8"]
    f = lambda a: np.ascontiguousarray(a, dtype=np.float32)
    doA, doB = stage < L, stage >= 1
    shared = {"xs": f(state["xs"])}
    if doB:
        lb = stage - 1
        sl = slice(lb, lb + 1)
        sc = np.asarray(inp["state_conv"])[sl]
        sp_ = np.asarray(inp["state_pool"])[sl]
        shared["sconv_fm"] = f(sc.transpose(0, 3, 1, 2))
        shared["spool_fm"] = f(sp_.transpose(0, 3, 1, 2))
        shared["sconv_raw"] = f(sc)
        shared["spool_raw"] = f(sp_)
        shared["w_in"] = f(np.asarray(inp["w_in"])[sl])
        shared["w_out"] = f(np.asarray(inp["w_out"])[sl])
        wgu = np.asarray(inp["w_gu"])[sl].reshape(1, 8, 128, 2, NFC, 128)
        shared["w_gu_t"] = f(wgu.transpose(0, 4, 2, 1, 3, 5).reshape(1, NFC, 128, 8, 256))
        shared["w_down"] = f(np.asarray(inp["w_down"])[sl])
        for k in ("vecfm", "gfm", "vbc", "gpost"):
            shared[k] = f(st[k][sl])
        shared["sgu_w"] = f(np.asarray(inp["sgu_w"])[sl])
        shared["sgu_b"] = f(np.asarray(inp["sgu_b"])[sl])
        shared["pool_w"] = f(np.asarray(inp["pool_w"])[sl])
        lam_init = 0.8 - 0.6 * math.exp(-0.3 * lb)
        lamc = np.zeros((128, 2), np.float32)
        lamc[:, 0] = -lam_init
        lamc[:, 1] = 1.0 - lam_init
        shared["lamc"] = lamc
        shared["part"] = f(state["part"])
    if doA:
        la = stage
        shared["pt8"] = st["pt8"]
        shared["w_in_a"] = f(np.asarray(inp["w_in"])[la][:, 1024:1792])
        shared["gfm_a"] = f(st["gfm"][la])
        ck = np.asarray(inp["cache_k"])[la].reshape(NPOOL, 128, 256)
        cv = np.asarray(inp["cache_v"])[la].reshape(NPOOL, 128, 256)
    maps = []
    for c in range(NCORES):
        m = dict(shared)
        m["cst"], m["cstb"] = st["consts"][c]
        if doB:
            m["xp"] = f(state["xp"][c])
        if doA:
            kv = np.empty((NPOOL, 16, 512), np.float32)
            kv[..., 0:256] = ck[:, 16 * c:16 * c + 16]
            kv[..., 256:512] = cv[:, 16 * c:16 * c + 16]
            m["kv"] = kv.reshape(NPOOL * 16, 512)
        maps.append(m)
    return maps


def build(cfg, stage):
    L, S, NP, NPOOL, NT, NB, NF, T8, PAST = (cfg[k] for k in ("L", "S", "NP", "NPOOL", "NT", "NB", "NF", "T8", "PAST"))
    doA = stage < L
    doB = stage >= 1
    nc = bass.Bass("TRN2", target_bir_lowering=False)
    NC_ = C_BIAS + T8 * 32

    def din(name, shape, dt=F32):
        return nc.dram_tensor(name, list(shape), dt, kind="ExternalInput").ap()

    def dout(name, shape):
        return nc.dram_tensor(name, list(shape), F32, kind="ExternalOutput").ap()

    xs_d = din("xs", [128, D])
    cst_d = din("cst", [128, NC_]); cstb_d = din("cstb", [128, NCB])
    out_keys = []
    if doB:
        xp_d = din("xp", [S, D])
        sconv_fm_d = din("sconv_fm", [1, 256, 32, 30]); spool_fm_d = din("spool_fm", [1, 256, 32, 15])
        sconv_raw_d = din("sconv_raw", [1, 32, 30, 256]); spool_raw_d = din("spool_raw", [1, 32, 15, 256])
        w_in_d = din("w_in", [1, D, 2048]); w_out_d = din("w_out", [1, D, D])
        w_gu_d = din("w_gu_t", [1, NFC, 128, 8, 256]); w_down_d = din("w_down", [1, DFF, D])
        vecfm_d = din("vecfm", [1, 128, 2, 35]); gfm_d = din("gfm", [1, 128, 2, 8]); vbc_d = din("vbc", [1, 1, NVB])
        gpost_d = din("gpost", [1, 2, 1, D]); sguw_d = din("sgu_w", [1, 4, 128, 128]); sgub_d = din("sgu_b", [1, 4, 128])
        poolw_d = din("pool_w", [1, 4, 64, 64]); lamc_d = din("lamc", [128, 2]); part_d = din("part", [NCORES, 1024, 65])
        o_yp = dout("o_yp", [S, D]); o_ys = dout("o_ys", [128, D])
        o_kp = dout("o_kp", [1, S, 256]); o_vp = dout("o_vp", [1, S, 256])
        o_cp = dout("o_cp", [1, 30, 256]); o_pp = dout("o_pp", [1, 15, 256])
        o_cs = dout("o_cs", [1, 32, 30, 256]); o_ps = dout("o_ps", [1, 32, 15, 256]); o_gs = dout("o_gs", [1, 128, 256])
        out_keys += ["o_yp", "o_ys", "o_kp", "o_vp", "o_cp", "o_pp", "o_cs", "o_ps", "o_gs"]
    if doA:
        kv_d = din("kv", [NPOOL * 16, 512])
        pt8_d = din("pt8", [8, 32 * T8], I32)
        w_in_a_d = din("w_in_a", [D, 768]); gfm_a_d = din("gfm_a", [128, 2, 8])
        o_ks = dout("o_ks", [128, 256]); o_vs = dout("o_vs", [128, 256]); o_part = dout("o_part", [1024, 65])
        out_keys += ["o_ks", "o_vs", "o_part"]
    DBG = False

    with contextlib.ExitStack() as st:
        P = Prog(nc, st)
        P.use_scopes = bool(cfg.get("scopes", False))
        P.scope = "setup"
        SB = lambda n, s, d: st.enter_context(nc.sbuf_tensor(n, list(s), d))
        x = SB("x", [128, NT, D], F32)
        xs = SB("xs_sb", [128, D], F32)
        cst = SB("cst_sb", [128, NC_], F32)
        cstb = SB("cstb_sb", [128, NCB], BF16)
        Wt = SB("Wt", [128, 24576], BF16)
        TCOLS = 32896
        Tt = SB("Tt", [128, TCOLS], BF16)
        hn = SB("hn", [128, D], BF16)
        tmpn = SB("tmpn", [128, 512], F32)
        gpost = SB("gpost_sb", [128, D], F32)
        vbc = SB("vbc_sb", [128, NVB], F32)
        vecfm = SB("vecfm_sb", [128, 2, 35], F32)
        gfm = SB("gfm_sb", [128, 2, 8], F32)
        stt = SB("stt", [128, 48], F32)
        idx = SB("idx", [128, 32 * T8], I32)
        WsT = SB("WsT", [128, 4, 128], BF16)
        WsS = SB("WsS", [128, 4, 128], BF16)
        BsT = SB("BsT", [128, 2, 128], F32)
        Wp = SB("Wp", [128, 2, 128], BF16)
        Wpst = SB("Wpst", [128, 2, 64], F32)
        zer = SB("zer", [128, 512], BF16)
        nlam = SB("nlam", [128, 4], F32)
        lamc = SB("lamc_sb", [128, 2], F32)
        ps = [st.enter_context(nc.psum_tensor(f"ps{i}", [128, 512], F32)) for i in range(8)]
        psb = [p_[:].bitcast(BF16) for p_ in ps]
        TA = Arena(Tt[:], TCOLS)

        idf = cst[:, C_ID:C_ID + 128]
        idb = cstb[:, B_ID:B_ID + 128]
        w_in_sb = Wt[:, 0:16384].rearrange("p (k n) -> p k n", k=8)
        w_out_sb = Wt[:, 16384:24576].rearrange("p (k n) -> p k n", k=8)
        w_down_sb = Wt[:, 0:22528].rearrange("p (k n) -> p k n", k=NFC)

        V = lambda fn, r, w: P.op("dve", fn, r, w)
        A = lambda fn, r, w: P.op("act", fn, r, w)
        G = lambda fn, r, w: P.op("pool", fn, r, w)

        def MM(out, lhsT, rhs, start, stop, r, w, signal):
            P.op("pe", lambda e: e.matmul(out, lhsT=lhsT, rhs=rhs, start=start, stop=stop, skip_group_check=True), r, w, signal)

        def TR(out, in_, ident, r, w, signal):
            P.op("pe", lambda e: e.transpose(out=out, in_=in_, identity=ident), r, w, signal)

        gbc = [0]

        def gb():
            gbc[0] = (gbc[0] + 1) % 6
            return gbc[0]

        sctr = [0]

        def sc(n=1):
            if sctr[0] + n > 48:
                sctr[0] = 0
            a = sctr[0]
            sctr[0] += n
            return a

        P.dma("sp", (lambda e, _kw=dict(out=cst[:], in_=cst_d): e.dma_start(**_kw)), writes=["cst"])
        P.dma("pool", (lambda e, _kw=dict(out=cstb[:], in_=cstb_d): e.dma_start(**_kw)), writes=["cstb"])
        G((lambda e, _kw=dict(ap=zer[:], constant=0.0): e.memset(**_kw)), [], ["zer"])
        G((lambda e, _kw=dict(ap=Wp[:], constant=0.0): e.memset(**_kw)), [], ["Wp"])
        if doB:
            xv = xp_d.rearrange("(t p) d -> p t d", p=128)
            for t0 in range(0, NT, 4):
                P.dma("sp", (lambda t0: (lambda e, _kw=dict(out=x[:, t0:t0 + 4, :], in_=xv[:, t0:t0 + 4, :]): e.dma_start(**_kw)))(t0),
                      writes=[f"x{t}" for t in range(t0, t0 + 4)], sem=f"xl{t0 // 4 % 4}")
            P.dma("sp", (lambda e, _kw=dict(out=lamc[:], in_=lamc_d): e.dma_start(**_kw)), writes=["lamc"])
        P.dma("sp", (lambda e, _kw=dict(out=xs[:], in_=xs_d): e.dma_start(**_kw)), writes=["xs"])
        if doA:
            TA.reset()
            pt8i = TA.get("pt8i", (32 * T8,), I32)
            pt8f = TA.get("pt8f", (32 * T8,), F32)
            idxf = TA.get("idxf", (32 * T8,), F32)
            P.dma("sp", (lambda e, _kw=dict(out=pt8i[0:8, :], in_=pt8_d): e.dma_start(**_kw)), writes=["pt8i"])
            V((lambda e, _kw=dict(out=pt8f[0:8, :], in_=pt8i[0:8, :]): e.tensor_copy(**_kw)), ["pt8i"], ["pt8f"])
            b0 = gb()
            MM(ps[b0][:, 0:32 * T8], cst[0:8, C_E8:C_E8 + 128], pt8f[0:8, :], True, True, ["cst", "pt8f"], [f"ps{b0}"], True)
            V((lambda e, _kw=dict(out=idxf, in0=ps[b0][:, 0:32 * T8], scalar1=16.0, scalar2=cst[:, C_I16:C_I16 + 1], op0=ALU.mult, op1=ALU.add): e.tensor_scalar(**_kw)),
              [f"ps{b0}", "cst"], ["idxf"])
            V((lambda e, _kw=dict(out=idx[:], in_=idxf): e.tensor_copy(**_kw)), ["idxf"], ["idx"])

        def prenorm_tile(xap, xkey, gsel, hT, hkey, tcol):
            c0 = sc(3)
            hnj = hn[:]
            A((lambda e, _kw=dict(out=hnj, in_=xap, func=AF.Square, accum_out=stt[:, c0:c0 + 1]): e.activation(**_kw)), [xkey], ["hn", "stt"])
            A((lambda e, _kw=dict(out=stt[:, c0 + 1:c0 + 2], in_=stt[:, c0:c0 + 1], func=AF.Sqrt, scale=1.0 / D, bias=RMS_EPS): e.activation(**_kw)), ["stt"], ["stt"])
            V((lambda e, _kw=dict(out=stt[:, c0 + 2:c0 + 3], in_=stt[:, c0 + 1:c0 + 2]): e.reciprocal(**_kw)), ["stt"], ["stt"])
            V((lambda e, _kw=dict(out=hnj, in0=xap, scalar1=stt[:, c0 + 2:c0 + 3], scalar2=None, op0=ALU.mult): e.tensor_scalar(**_kw)), [xkey, "stt"], ["hn"])
            b = gb()
            for kc in range(8):
                TR(psb[b][:, kc * 128:(kc + 1) * 128], hn[:, kc * 128:(kc + 1) * 128], idb, ["hn", "cstb"], [f"ps{b}"], kc == 7)
            V((lambda e, _kw=dict(out=hT[:, :, tcol:tcol + 128], in0=psb[b].rearrange("p (k t) -> p k t", k=8),
                                        in1=gfm[:, gsel, :].unsqueeze(2).to_broadcast([128, 8, 128]), op=ALU.mult): e.tensor_tensor(**_kw)),
              [f"ps{b}", "gfm"], [hkey])

        def postnorm_add(bA, bB, xap, xkey):
            c0 = sc(5)
            tj = tmpn[:].bitcast(BF16)
            A((lambda e, _kw=dict(out=tj[:, 0:512], in_=ps[bA][:], func=AF.Square, accum_out=stt[:, c0:c0 + 1]): e.activation(**_kw)), [f"ps{bA}"], ["tmpn", "stt"])
            A((lambda e, _kw=dict(out=tj[:, 512:1024], in_=ps[bB][:], func=AF.Square, accum_out=stt[:, c0 + 1:c0 + 2]): e.activation(**_kw)), [f"ps{bB}"], ["tmpn", "stt"])
            V((lambda e, _kw=dict(out=stt[:, c0 + 2:c0 + 3], in0=stt[:, c0:c0 + 1], in1=stt[:, c0 + 1:c0 + 2], op=ALU.add): e.tensor_tensor(**_kw)), ["stt"], ["stt"])
            A((lambda e, _kw=dict(out=stt[:, c0 + 3:c0 + 4], in_=stt[:, c0 + 2:c0 + 3], func=AF.Sqrt, scale=1.0 / D, bias=RMS_EPS): e.activation(**_kw)), ["stt"], ["stt"])
            V((lambda e, _kw=dict(out=stt[:, c0 + 4:c0 + 5], in_=stt[:, c0 + 3:c0 + 4]): e.reciprocal(**_kw)), ["stt"], ["stt"])
            for n, bk in enumerate((bA, bB)):
                V((lambda e, _kw=dict(out=tmpn[:], in0=ps[bk][:], scalar=stt[:, c0 + 4:c0 + 5], in1=gpost[:, n * 512:(n + 1) * 512],
                                                            op0=ALU.mult, op1=ALU.mult): e.scalar_tensor_tensor(**_kw)), [f"ps{bk}", "stt", "gpost"], ["tmpn"])
                G((lambda e, _kw=dict(out=xap[:, n * 512:(n + 1) * 512], in0=xap[:, n * 512:(n + 1) * 512], in1=tmpn[:], op=ALU.add): e.tensor_tensor(**_kw)),
                  ["tmpn", xkey], [xkey])

        def fm_chunk(col, hT, hkey, W, evac):
            b = gb()
            for kc in range(8):
                MM(ps[b][:, 0:W], w_in_sb[:, kc, col:col + 128], hT[:, kc, 0:W], kc == 0, kc == 7, ["w_in", hkey], [f"ps{b}"], kc == 7)
            evac(b)

        def tm_cols(col, N, hT, hkey, tcol):
            b = gb()
            for kc in range(8):
                MM(ps[b][:, 0:N], hT[:, kc, tcol:tcol + 128], w_in_sb[:, kc, col:col + N], kc == 0, kc == 7, ["w_in", hkey], [f"ps{b}"], kc == 7)
            return b

        def ln_rows(vg, vgk, vn32, vnk):
            c0 = sc(10)
            V((lambda e, _kw=dict(out=stt[:, c0:c0 + 6], in_=vg): e.bn_stats(**_kw)), [vgk], ["stt"])
            V((lambda e, _kw=dict(out=stt[:, c0 + 6:c0 + 8], in_=stt[:, c0:c0 + 6]): e.bn_aggr(**_kw)), ["stt"], ["stt"])
            A((lambda e, _kw=dict(out=stt[:, c0 + 8:c0 + 9], in_=stt[:, c0 + 7:c0 + 8], func=AF.Sqrt, scale=1.0, bias=LN_EPS): e.activation(**_kw)), ["stt"], ["stt"])
            V((lambda e, _kw=dict(out=stt[:, c0 + 9:c0 + 10], in_=stt[:, c0 + 8:c0 + 9]): e.reciprocal(**_kw)), ["stt"], ["stt"])
            V((lambda e, _kw=dict(out=vn32, in0=vg, scalar1=stt[:, c0 + 6:c0 + 7], scalar2=stt[:, c0 + 9:c0 + 10], op0=ALU.subtract, op1=ALU.mult): e.tensor_scalar(**_kw)),
              [vgk, "stt"], [vnk])
            V((lambda e, _kw=dict(out=vn32, in0=vn32, in1=vbc[:, VB_LNG:VB_LNG + 256], op=ALU.mult): e.tensor_tensor(**_kw)), [vnk, "vbc"], [vnk])
            V((lambda e, _kw=dict(out=vn32, in0=vn32, in1=vbc[:, VB_LNB:VB_LNB + 256], op=ALU.add): e.tensor_tensor(**_kw)), [vnk, "vbc"], [vnk])

        def to_vnz(vn32, vnk, vnz):
            v4 = vn32.rearrange("p (a two c) -> p a two c", two=2, c=64)
            z4 = vnz.rearrange("p (a two) c -> p a two c", two=2)
            G((lambda e, _kw=dict(out=z4[:, :, 0, 0:64], in_=v4[:, :, 0, :]): e.tensor_copy(**_kw)), [vnk], ["vnz"])
            G((lambda e, _kw=dict(out=z4[:, :, 1, 64:128], in_=v4[:, :, 1, :]): e.tensor_copy(**_kw)), [vnk], ["vnz"])

        def conv_ln_silu(cy, W, ybf, ysq, mean, m2, rstd, mixT, mkey, mcol):
            G((lambda e, _kw=dict(out=ybf, in_=cy): e.tensor_copy(**_kw)), ["cy"], ["ybf"])
            G((lambda e, _kw=dict(out=ysq, in0=cy, in1=cy, op=ALU.mult): e.tensor_tensor(**_kw)), ["cy"], ["ysq"])
            b = gb()
            one = cstb[:, B_ONE:B_ONE + 128]
            for ch in range(2):
                MM(ps[b][:, 0:W], one, ybf[:, ch, :], ch == 0, ch == 1, ["cstb", "ybf"], [f"ps{b}"], False)
            for ch in range(2):
                MM(ps[b][:, 256:256 + W], one, ysq[:, ch, :], ch == 0, ch == 1, ["cstb", "ysq"], [f"ps{b}"], ch == 1)
            A((lambda e, _kw=dict(out=mean, in_=ps[b][:, 0:W]): e.copy(**_kw)), [f"ps{b}"], ["cmean"])
            G((lambda e, _kw=dict(out=m2, in0=mean, in1=mean, op=ALU.mult): e.tensor_tensor(**_kw)), ["cmean"], ["cm2"])
            V((lambda e, _kw=dict(out=m2, in0=ps[b][:, 256:256 + W], in1=m2, op=ALU.subtract): e.tensor_tensor(**_kw)), [f"ps{b}", "cm2"], ["cm2"])
            A((lambda e, _kw=dict(out=rstd, in_=m2, func=AF.Sqrt, scale=1.0, bias=LN_EPS): e.activation(**_kw)), ["cm2"], ["crstd"])
            V((lambda e, _kw=dict(out=rstd, in_=rstd): e.reciprocal(**_kw)), ["crstd"], ["crstd"])
            V((lambda e, _kw=dict(out=cy, in0=cy, in1=mean.unsqueeze(1).to_broadcast([128, 2, W]), op=ALU.subtract): e.tensor_tensor(**_kw)), ["cy", "cmean"], ["cy"])
            V((lambda e, _kw=dict(out=cy, in0=cy, in1=rstd.unsqueeze(1).to_broadcast([128, 2, W]), op=ALU.mult): e.tensor_tensor(**_kw)), ["cy", "crstd"], ["cy"])
            for ch in range(2):
                A((lambda e, _kw=dict(out=mixT[:, ch, mcol:mcol + W], in_=cy[:, ch, :], func=AF.Silu, scale=vecfm[:, ch, 1:2], bias=vecfm[:, ch, 2:3]): e.activation(**_kw)),
                  ["cy", "vecfm"], [mkey])

        def attn_finish(o0, l0, o1, l1, rk, obuf_h, okey):
            c0 = sc(3)
            V((lambda e, _kw=dict(out=stt[:, c0:c0 + 1], in_=l0): e.reciprocal(**_kw)), rk, ["stt"])
            V((lambda e, _kw=dict(out=stt[:, c0 + 1:c0 + 2], in_=l1): e.reciprocal(**_kw)), rk, ["stt"])
            V((lambda e, _kw=dict(out=stt[:, c0 + 2:c0 + 3], in0=stt[:, c0 + 1:c0 + 2], in1=nlam[:, 0:1], op=ALU.mult): e.tensor_tensor(**_kw)), ["stt", "nlam"], ["stt"])
            V((lambda e, _kw=dict(out=obuf_h, in0=o0, scalar1=stt[:, c0:c0 + 1], scalar2=None, op0=ALU.mult): e.tensor_scalar(**_kw)), rk + ["stt"], [okey])
            V((lambda e, _kw=dict(out=obuf_h, in0=o1, scalar=stt[:, c0 + 2:c0 + 3], in1=obuf_h, op0=ALU.mult, op1=ALU.add): e.scalar_tensor_tensor(**_kw)), rk + ["stt", okey], [okey])

        def head_norm_to_mix(ob, okey, osq, ycb, lam_init, mixT, mkey, mcol):
            c0 = sc(8)
            o3 = ob.rearrange("p (h e) -> p h e", h=4)
            G((lambda e, _kw=dict(out=osq, in0=ob, in1=ob, op=ALU.mult): e.tensor_tensor(**_kw)), [okey], ["osq"])
            V((lambda e, _kw=dict(out=stt[:, c0:c0 + 4], in_=osq.rearrange("p (h e) -> p h e", h=4), axis=AX.X, op=ALU.add): e.tensor_reduce(**_kw)), ["osq"], ["stt"])
            A((lambda e, _kw=dict(out=stt[:, c0 + 4:c0 + 8], in_=stt[:, c0:c0 + 4], func=AF.Sqrt, scale=1.0 / 64, bias=LN_EPS): e.activation(**_kw)), ["stt"], ["stt"])
            V((lambda e, _kw=dict(out=stt[:, c0:c0 + 4], in_=stt[:, c0 + 4:c0 + 8]): e.reciprocal(**_kw)), ["stt"], ["stt"])
            V((lambda e, _kw=dict(out=o3, in0=o3, scalar=lamc[:, 1:2], in1=stt[:, c0:c0 + 4].unsqueeze(2).to_broadcast([128, 4, 64]),
                                               op0=ALU.mult, op1=ALU.mult): e.scalar_tensor_tensor(**_kw)), [okey, "stt", "lamc"], [okey])
            V((lambda e, _kw=dict(out=ycb.rearrange("p (h e) -> p h e", h=4), in0=o3,
                                        in1=vbc[:, VB_GSUB:VB_GSUB + 64].unsqueeze(1).to_broadcast([128, 4, 64]), op=ALU.mult): e.tensor_tensor(**_kw)), [okey, "vbc"], ["ycb"])
            b = gb()
            for ch in range(2):
                TR(psb[b][:, ch * 128:(ch + 1) * 128], ycb[:, ch * 128:(ch + 1) * 128], idb, ["ycb", "cstb"], [f"ps{b}"], ch == 1)
            A((lambda e, _kw=dict(out=mixT[:, 4:6, mcol:mcol + 128], in_=psb[b][:, 0:256].rearrange("p (c t) -> p c t", c=2)): e.copy(**_kw)), [f"ps{b}"], [mkey])

        def out_proj_tile(mixT, mkeys, mcol, xap, xkey):
            bA, bB = gb(), gb()
            for n, bk in enumerate((bA, bB)):
                for kc in range(8):
                    MM(ps[bk][:], mixT[:, kc, mcol:mcol + 128], w_out_sb[:, kc, n * 512:(n + 1) * 512], kc == 0, kc == 7, mkeys + ["w_out"], [f"ps{bk}"], kc == 7)
            postnorm_add(bA, bB, xap, xkey)

        ARK = set()

        def phase_barrier(extra=()):
            P.barrier(sorted(set(P.buf.keys()) | ARK | set(extra)))

        for l in ([0] if doB else []):
            lam_init = None
            P.scope = "loads"
            phase_barrier(["w_in", "w_out", "w_down"])
            TA.reset()
            w_in_v = w_in_d[l].rearrange("(kc p) n -> p kc n", p=128)
            for hf in range(2):
                P.dma("pool", (lambda hf: (lambda e, _kw=dict(out=w_in_sb[:, hf * 4:(hf + 1) * 4, :], in_=w_in_v[:, hf * 4:(hf + 1) * 4, :]): e.dma_start(**_kw)))(hf),
                      writes=["w_in"], sem="w_in")
            P.dma("pool", (lambda e, _kw=dict(out=w_out_sb, in_=w_out_d[l].rearrange("(kc p) n -> p kc n", p=128)): e.dma_start(**_kw)), writes=["w_out"], sem="w_out")
            P.dma("sp", (lambda e, _kw=dict(out=vecfm[:], in_=vecfm_d[l]): e.dma_start(**_kw)), writes=["vecfm"])
            P.dma("sp", (lambda e, _kw=dict(out=gfm[:], in_=gfm_d[l]): e.dma_start(**_kw)), writes=["gfm"])
            P.dma("sp", (lambda e, _kw=dict(out=vbc[:], in_=vbc_d[l].partition_broadcast(128)): e.dma_start(**_kw)), writes=["vbc"])
            P.dma("sp", (lambda e, _kw=dict(out=gpost[:], in_=gpost_d[l, 0].partition_broadcast(128)): e.dma_start(**_kw)), writes=["gpost"])
            sguw_st = TA.get("sguw_st", (4, 128), F32)
            P.dma("sp", (lambda e, _kw=dict(out=sguw_st, in_=sguw_d[l].rearrange("g t s -> t g s")): e.dma_start(**_kw)), writes=["sguw_st"])
            for g in range(4):
                P.dma("sp", (lambda g: (lambda e, _kw=dict(out=BsT[(g % 2) * 64:(g % 2) * 64 + 64, g // 2, :], in_=sgub_d[l, g:g + 1, :].partition_broadcast(64)): e.dma_start(**_kw)))(g),
                      writes=["BsT"], sem="BsT")
                P.dma("sp", (lambda g: (lambda e, _kw=dict(out=Wpst[(g % 2) * 64:(g % 2) * 64 + 64, g // 2, :], in_=poolw_d[l, g]): e.dma_start(**_kw)))(g),
                      writes=["Wpst"], sem="Wpst")
            G((lambda e, _kw=dict(out=Wp[0:64, :, 0:64], in_=Wpst[0:64, :, :]): e.tensor_copy(**_kw)), ["Wpst"], ["Wp"])
            G((lambda e, _kw=dict(out=Wp[64:128, :, 64:128], in_=Wpst[64:128, :, :]): e.tensor_copy(**_kw)), ["Wpst"], ["Wp"])
            c0 = sc(6)
            ltmp = TA.get("ltmp", (64,), F32)
            V((lambda e, _kw=dict(out=ltmp[:, 0:32], in0=vbc[:, VB_LQ1:VB_LQ1 + 32], in1=vbc[:, VB_LK1:VB_LK1 + 32], op=ALU.mult): e.tensor_tensor(**_kw)), ["vbc"], ["ltmp"])
            V((lambda e, _kw=dict(out=ltmp[:, 32:64], in0=vbc[:, VB_LQ2:VB_LQ2 + 32], in1=vbc[:, VB_LK2:VB_LK2 + 32], op=ALU.mult): e.tensor_tensor(**_kw)), ["vbc"], ["ltmp"])
            V((lambda e, _kw=dict(out=stt[:, c0:c0 + 2], in_=ltmp.rearrange("p (a b) -> p a b", a=2), axis=AX.X, op=ALU.add): e.tensor_reduce(**_kw)), ["ltmp"], ["stt"])
            A((lambda e, _kw=dict(out=stt[:, c0 + 2:c0 + 4], in_=stt[:, c0:c0 + 2], func=AF.Exp): e.activation(**_kw)), ["stt"], ["stt"])
            V((lambda e, _kw=dict(out=stt[:, c0 + 4:c0 + 5], in0=stt[:, c0 + 3:c0 + 4], in1=stt[:, c0 + 2:c0 + 3], op=ALU.subtract): e.tensor_tensor(**_kw)), ["stt"], ["stt"])
            V((lambda e, _kw=dict(out=nlam[:, 0:1], in0=stt[:, c0 + 4:c0 + 5], scalar1=lamc[:, 0:1], scalar2=None, op0=ALU.add): e.tensor_scalar(**_kw)), ["stt", "lamc"], ["nlam"])
            b = gb()
            for g in range(4):
                TR(ps[b][:, g * 128:(g + 1) * 128], sguw_st[:, g, :], idf, ["sguw_st", "cst"], [f"ps{b}"], g == 3)
            V((lambda e, _kw=dict(out=WsT[:], in0=ps[b][:].rearrange("p (g t) -> p g t", g=4),
                                        in1=cst[:, C_TRI:C_TRI + 128].unsqueeze(1).to_broadcast([128, 4, 128]), op=ALU.mult): e.tensor_tensor(**_kw)), [f"ps{b}", "cst"], ["WsT"])
            sel = cstb[0:4, B_SEL:B_SEL + 128]
            b = gb()
            for g in range(4):
                MM(ps[b][0:4, g * 128:(g + 1) * 128], WsT[0:4, g, 0:4], sel, True, True, ["WsT", "cstb"], [f"ps{b}"], g == 3)
            asb = TA.get("asb", (4, 128), BF16)
            A((lambda e, _kw=dict(out=asb[0:4], in_=ps[b][0:4, :].rearrange("p (g t) -> p g t", g=4)): e.copy(**_kw)), [f"ps{b}"], ["asb"])
            b2 = gb()
            for g in range(4):
                MM(ps[b2][:, g * 128:(g + 1) * 128], asb[0:4, g, :], sel, True, True, ["asb", "cstb"], [f"ps{b2}"], g == 3)
            V((lambda e, _kw=dict(out=WsS[:], in0=ps[b2][:].rearrange("p (g t) -> p g t", g=4),
                                        in1=cst[:, C_BLK:C_BLK + 128].unsqueeze(1).to_broadcast([128, 4, 128]), op=ALU.mult): e.tensor_tensor(**_kw)), [f"ps{b2}", "cst"], ["WsS"])
            ARK.update(TA.keys)

            P.scope = "PM"
            phase_barrier()
            TA.reset()
            W = 256
            hT = TA.get("hT", (8, 512), BF16)
            kT_all = TA.get("kT_all", (2, S), BF16)
            V1 = TA.get("V1", (NT, 4, 65), BF16)
            Qbd = TA.get("Qbd", (2, 512), BF16)
            mixT = TA.get("mixT", (8, W), BF16)
            NPT = 3
            pT = [TA.get(f"pT{i}", (512,), BF16) for i in range(NPT)]
            hconv = TA.get("hconv", (2, 30 + W), F32)
            cy = TA.get("cy", (2, W), F32)
            ybf = TA.get("ybf", (2, W), BF16)
            ysq = TA.get("ysq", (2, W), BF16)
            cmean = TA.get("cmean", (W,), F32)
            cm2 = TA.get("cm2", (W,), F32)
            crstd = TA.get("crstd", (W,), F32)
            sg = TA.get("sg", (2, W), F32)
            uT = TA.get("uT", (2, W), BF16)
            vg = TA.get("vg", (256,), F32)
            vn32 = TA.get("vn32", (256,), F32)
            vnz = TA.get("vnz", (4, 128), BF16)
            sgt = TA.get("sgt", (2, 128), F32)
            xdT = TA.get("xdT", (2, 16 + W), F32)
            pA = TA.get("pA", (2, 16 + W), F32)
            pB = TA.get("pB", (2, 16 + W), F32)
            dT = TA.get("dT", (2, W), BF16)
            kvo = TA.get("kvo", (512,), F32)
            obuf = TA.get("obuf", (2, 256), F32)
            osq = TA.get("osq", (256,), F32)
            ycb = TA.get("ycb", (256,), BF16)
            tm1 = TA.get("tm1", (256,), F32)
            tm2 = TA.get("tm2", (256,), F32)
            ARK.update(TA.keys)
            G((lambda e, _kw=dict(ap=V1, constant=1.0): e.memset(**_kw)), [], ["V1"])
            G((lambda e, _kw=dict(ap=vnz, constant=0.0): e.memset(**_kw)), [], ["vnz"])
            G((lambda e, _kw=dict(ap=hconv[:, :, 0:30], constant=0.0): e.memset(**_kw)), [], ["hconv"])
            G((lambda e, _kw=dict(ap=xdT[:, :, 0:15], constant=0.0): e.memset(**_kw)), [], ["xdT"])

            for blk in range(NB):
                t0 = 2 * blk
                for tl in range(2):
                    prenorm_tile(x[:, t0 + tl, :], f"x{t0 + tl}", 0, hT, "hT", tl * 128)
                for ch in range(2):
                    fm_chunk(256 + ch * 128, hT, "hT", W,
                             lambda b, ch=ch: A((lambda e, _kw=dict(out=sg[:, ch, :], in_=ps[b][:, 0:W], func=AF.Sigmoid): e.activation(**_kw)), [f"ps{b}"], ["sg"]))
                for ch in range(2):
                    fm_chunk(ch * 128, hT, "hT", W,
                             lambda b, ch=ch: V((lambda e, _kw=dict(out=hconv[:, ch, 30:30 + W], in0=ps[b][:, 0:W], in1=sg[:, ch, :], op=ALU.mult): e.tensor_tensor(**_kw)),
                                                [f"ps{b}", "sg"], ["hconv"]))
                for ch in range(2):
                    fm_chunk(512 + ch * 128, hT, "hT", W,
                             lambda b, ch=ch: A((lambda e, _kw=dict(out=uT[:, ch, :], in_=ps[b][:, 0:W], func=AF.Gelu): e.activation(**_kw)), [f"ps{b}"], ["uT"]))
                for chk in range(2):
                    def ev_qp(b, chk=chk):
                        for c in range(2):
                            V((lambda e, _kw=dict(out=Qbd[:, chk, c * 256:(c + 1) * 256], in0=ps[b][:, 0:W], scalar1=cst[:, C_MC + c:C_MC + c + 1],
                                                             scalar2=None, op0=ALU.mult): e.tensor_scalar(**_kw)), [f"ps{b}", "cst"], ["Qbd"])
                    fm_chunk(1024 + chk * 128, hT, "hT", W, ev_qp)
                for ch in range(2):
                    fm_chunk(1280 + ch * 128, hT, "hT", W,
                             lambda b, ch=ch: A((lambda e, _kw=dict(out=kT_all[:, ch, blk * W:(blk + 1) * W], in_=ps[b][:, 0:W]): e.copy(**_kw)), [f"ps{b}"], ["kT_all"]))
                for ch in range(2):
                    fm_chunk(1792 + ch * 128, hT, "hT", W,
                             lambda b, ch=ch: A((lambda e, _kw=dict(out=xdT[:, ch, 15:15 + W], in_=ps[b][:, 0:W]): e.copy(**_kw)), [f"ps{b}"], ["xdT"]))
                for tl in range(2):
                    tg = t0 + tl
                    b = tm_cols(1280, 512, hT, "hT", tl * 128)
                    V((lambda e, _kw=dict(out=kvo, in_=ps[b][:]): e.tensor_copy(**_kw)), [f"ps{b}"], ["kvo"])
                    P.dma("sp", (lambda tg: (lambda e, _kw=dict(out=o_kp[l, tg * 128:(tg + 1) * 128, :], in_=kvo[:, 0:256]): e.dma_start(**_kw)))(tg), reads=["kvo"], writes=["o_kp"], sem="st_kvo")
                    P.dma("sp", (lambda tg: (lambda e, _kw=dict(out=o_vp[l, tg * 128:(tg + 1) * 128, :], in_=kvo[:, 256:512]): e.dma_start(**_kw)))(tg), reads=["kvo"], writes=["o_vp"], sem="st_kvo")
                    G((lambda e, _kw=dict(out=V1[:, tg, :, 0:64], in_=kvo[:, 256:512].rearrange("p (h e) -> p h e", h=4)): e.tensor_copy(**_kw)), ["kvo"], ["V1"])
                    b = tm_cols(768, 256, hT, "hT", tl * 128)
                    A((lambda e, _kw=dict(out=vg, in_=ps[b][:, 0:256], func=AF.Gelu): e.activation(**_kw)), [f"ps{b}"], ["vg"])
                    ln_rows(vg, "vg", vn32, "vn32")
                    to_vnz(vn32, "vn32", vnz)
                    b = gb()
                    for ch in range(2):
                        for j in range(2):
                            MM(ps[b][:, ch * 128:(ch + 1) * 128], vnz[:, 2 * ch + j, :], WsT[:, 2 * ch + j, :], j == 0, j == 1, ["vnz", "WsT"], [f"ps{b}"], ch == 1 and j == 1)
                    V((lambda e, _kw=dict(out=sgt, in0=ps[b][:, 0:256].rearrange("p (c t) -> p c t", c=2), in1=BsT[:], op=ALU.add): e.tensor_tensor(**_kw)), [f"ps{b}", "BsT"], ["sgt"])
                    V((lambda e, _kw=dict(out=mixT[:, 2:4, tl * 128:(tl + 1) * 128], in0=sgt, in1=uT[:, :, tl * 128:(tl + 1) * 128], op=ALU.mult): e.tensor_tensor(**_kw)),
                      ["sgt", "uT"], ["mixB"])
                    if tg == NT - 1:
                        b = tm_cols(0, 512, hT, "hT", tl * 128)
                        A((lambda e, _kw=dict(out=tm1, in_=ps[b][:, 256:512], func=AF.Sigmoid): e.activation(**_kw)), [f"ps{b}"], ["tm1"])
                        V((lambda e, _kw=dict(out=tm1, in0=ps[b][:, 0:256], in1=tm1, op=ALU.mult): e.tensor_tensor(**_kw)), [f"ps{b}", "tm1"], ["tm1"])
                        P.dma("sp", (lambda e, _kw=dict(out=o_cp[l], in_=tm1[98:128, :]): e.dma_start(**_kw)), reads=["tm1"], writes=["o_cp"], sem="st_tm1")
                        b = tm_cols(1792, 256, hT, "hT", tl * 128)
                        A((lambda e, _kw=dict(out=tm2, in_=ps[b][:, 0:256]): e.copy(**_kw)), [f"ps{b}"], ["tm2"])
                        P.dma("sp", (lambda e, _kw=dict(out=o_pp[l], in_=tm2[113:128, :]): e.dma_start(**_kw)), reads=["tm2"], writes=["o_pp"], sem="st_tm2")
                for j in range(31):
                    for ch in range(2):
                        if j == 0:
                            V((lambda e, _kw=dict(out=cy[:, ch, :], in0=hconv[:, ch, 0:W], scalar1=vecfm[:, ch, 4:5], scalar2=vecfm[:, ch, 0:1],
                                                               op0=ALU.mult, op1=ALU.add): e.tensor_scalar(**_kw)), ["hconv", "vecfm"], [f"cy{ch}", "cy"])
                        else:
                            V((lambda e, _kw=dict(out=cy[:, ch, :], in0=hconv[:, ch, j:j + W], scalar=vecfm[:, ch, 4 + j:5 + j], in1=cy[:, ch, :],
                                                                           op0=ALU.mult, op1=ALU.add): e.scalar_tensor_tensor(**_kw)), ["hconv", "vecfm", f"cy{ch}"], [f"cy{ch}"])
                G((lambda e, _kw=dict(out=hconv[:, :, 0:30], in_=hconv[:, :, W:W + 30]): e.tensor_copy(**_kw)), ["hconv"], ["hconv"])
                P.op("pool", lambda e: e.nop(), ["cy0", "cy1"], ["cy"])
                conv_ln_silu(cy, W, ybf, ysq, cmean, cm2, crstd, mixT, "mixA", 0)
                V((lambda e, _kw=dict(out=pA[:, :, 1:15 + W], in0=xdT[:, :, 1:15 + W], in1=xdT[:, :, 0:14 + W], op=ALU.add): e.tensor_tensor(**_kw)), ["xdT"], ["pA"])
                V((lambda e, _kw=dict(out=pB[:, :, 3:15 + W], in0=pA[:, :, 3:15 + W], in1=pA[:, :, 1:13 + W], op=ALU.add): e.tensor_tensor(**_kw)), ["pA"], ["pB"])
                V((lambda e, _kw=dict(out=pA[:, 1, 7:15 + W], in0=pB[:, 1, 7:15 + W], in1=pB[:, 1, 3:11 + W], op=ALU.add): e.tensor_tensor(**_kw)), ["pB", "pA"], ["pA"])
                V((lambda e, _kw=dict(out=pB[64:128, 1, 15:15 + W], in0=pA[64:128, 1, 15:15 + W], in1=pA[64:128, 1, 7:7 + W], op=ALU.add): e.tensor_tensor(**_kw)), ["pA", "pB"], ["pB"])
                for ch in range(2):
                    for hf, src in ((0, pA), (1, pB)):
                        lo, hi = hf * 64, hf * 64 + 64
                        V((lambda e, _kw=dict(out=dT[lo:hi, ch, :], in0=src[lo:hi, ch, 15:15 + W], scalar=cst[lo:hi, C_INVW + ch:C_INVW + ch + 1],
                                                                                         in1=xdT[lo:hi, ch, 15:15 + W], op0=ALU.mult, op1=ALU.subtract): e.scalar_tensor_tensor(**_kw)), ["pA", "pB", "xdT", "cst"], ["dT"])
                        if blk == 0:
                            c16 = sc(16)
                            V((lambda e, _kw=dict(out=stt[lo:hi, c16:c16 + 16], in0=src[lo:hi, ch, 15:31],
                                                                                               in1=cst[lo:hi, C_ICNT + ch * 16:C_ICNT + ch * 16 + 16], op=ALU.mult): e.tensor_tensor(**_kw)), ["pA", "pB", "cst"], ["stt"])
                            V((lambda e, _kw=dict(out=dT[lo:hi, ch, 0:16], in0=stt[lo:hi, c16:c16 + 16], in1=xdT[lo:hi, ch, 15:31], op=ALU.subtract): e.tensor_tensor(**_kw)),
                              ["stt", "xdT", "dT"], ["dT"])
                G((lambda e, _kw=dict(out=xdT[:, :, 0:15], in_=xdT[:, :, W:W + 15]): e.tensor_copy(**_kw)), ["xdT", "pA", "pB"], ["xdT"])
                for ch in range(2):
                    b = gb()
                    MM(ps[b][:, 0:W], Wp[:, ch, :], dT[:, ch, :], True, True, ["Wp", "dT"], [f"ps{b}"], True)
                    V((lambda e, _kw=dict(out=mixT[:, 6 + ch, :], in0=ps[b][:, 0:W], scalar1=vecfm[:, ch, 3:4], scalar2=None, op0=ALU.mult): e.tensor_scalar(**_kw)),
                      [f"ps{b}", "vecfm"], ["mixD"])
                qb0, qb1 = 2 * blk, 2 * blk + 1
                pcount = 0
                for h in range(4):
                    bo = 6 + (h % 2)
                    r0 = (h % 2) * 64
                    MM(ps[bo][:], zer[:, 0:128], zer[:], True, True, ["zer"], [f"ps{bo}"], False)
                    for kb in range(qb1 + 1):
                        b = gb()
                        MM(ps[b][:], kT_all[r0:r0 + 64, h // 2, kb * 128:(kb + 1) * 128], Qbd[r0:r0 + 64, h // 2, :], True, True, ["kT_all", "Qbd"], [f"ps{b}"], True)
                        pi = pcount % NPT
                        pcount += 1
                        dd = qb1 - kb
                        A((lambda e, _kw=dict(out=pT[pi], in_=ps[b][:], func=AF.Exp, scale=ISQ, bias=cst[:, C_BT + h * 16 + dd:C_BT + h * 16 + dd + 1]): e.activation(**_kw)),
                          [f"ps{b}", "cst"], [f"pT{pi}"])
                        if kb >= qb0:
                            mo = B_MA if kb == qb0 else B_MB
                            G((lambda e, _kw=dict(out=pT[pi], in0=pT[pi], in1=cstb[:, mo:mo + 512], op=ALU.mult): e.tensor_tensor(**_kw)), [f"pT{pi}", "cstb"], [f"pT{pi}"])
                        for c in range(2):
                            for qi in range(2):
                                if kb <= qb0 + qi:
                                    a = c * 2 + qi
                                    last = (kb == qb0 + qi)
                                    MM(ps[bo][:, a * 128:a * 128 + 65], pT[pi][:, c * 256 + qi * 128:c * 256 + qi * 128 + 128], V1[:, kb, h, :], False, last,
                                       [f"pT{pi}", "V1"], [f"ps{bo}"], last and c == 1)
                    for qi in range(2):
                        a0, a1 = qi, 2 + qi
                        attn_finish(ps[bo][:, a0 * 128:a0 * 128 + 64], ps[bo][:, a0 * 128 + 64:a0 * 128 + 65],
                                    ps[bo][:, a1 * 128:a1 * 128 + 64], ps[bo][:, a1 * 128 + 64:a1 * 128 + 65], [f"ps{bo}"],
                                    obuf[:, qi, h * 64:(h + 1) * 64], f"obuf{qi}")
                for qi in range(2):
                    head_norm_to_mix(obuf[:, qi, :], f"obuf{qi}", osq, ycb, lam_init, mixT, "mixC", qi * 128)
                for tl in range(2):
                    out_proj_tile(mixT, ["mixA", "mixB", "mixC", "mixD"], tl * 128, x[:, t0 + tl, :], f"x{t0 + tl}")

            P.scope = "S2"
            phase_barrier()
            TA.reset()
            W = 128
            hTs = TA.get("hT", (8, 128), BF16)
            mixT = TA.get("mixT", (8, W), BF16)
            sg = TA.get("sg", (2, W), F32)
            hs32 = TA.get("hs32", (2, 32, 34), F32)
            cy = TA.get("cy", (2, W), F32)
            ybf = TA.get("ybf", (2, W), BF16)
            ysq = TA.get("ysq", (2, W), BF16)
            cmean = TA.get("cmean", (W,), F32)
            cm2 = TA.get("cm2", (W,), F32)
            crstd = TA.get("crstd", (W,), F32)
            uT = TA.get("uT", (2, W), BF16)
            vg = TA.get("vg", (256,), F32)
            vn32 = TA.get("vn32", (256,), F32)
            vnz = TA.get("vnz", (4, 128), BF16)
            sgt = TA.get("sgt", (2, 128), F32)
            xds = TA.get("xds", (2, 32, 19), F32)
            sA = TA.get("pA", (2, 32, 19), F32)
            sB = TA.get("pB", (2, 32, 19), F32)
            dT = TA.get("dT", (2, W), BF16)
            R = TA.get("R", (8, 65), F32)
            R8 = TA.get("R8", (NCORES, 8 * 65), F32)
            obuf = TA.get("obuf", (256,), F32)
            osq = TA.get("osq", (256,), F32)
            ycb = TA.get("ycb", (256,), BF16)
            tm1 = TA.get("tm1", (256,), F32)
            tm2 = TA.get("tm2", (256,), F32)
            ARK.update(TA.keys)
            G((lambda e, _kw=dict(ap=vnz, constant=0.0): e.memset(**_kw)), [], ["vnz"])
            for ch in range(2):
                P.dma("sp", (lambda ch: (lambda e, _kw=dict(out=hs32[:, ch, :, 0:30], in_=sconv_fm_d[l, ch * 128:(ch + 1) * 128]): e.dma_start(**_kw)))(ch), writes=["hs32"], sem="hs32")
                P.dma("sp", (lambda ch: (lambda e, _kw=dict(out=xds[:, ch, :, 0:15], in_=spool_fm_d[l, ch * 128:(ch + 1) * 128]): e.dma_start(**_kw)))(ch), writes=["xds"], sem="xds")
            P.dma("sp", (lambda e, _kw=dict(out=o_cs[l, :, 0:26, :], in_=sconv_raw_d[l, :, 4:30, :]): e.dma_start(**_kw)), writes=["o_cs"], sem="st_d2d")
            P.dma("sp", (lambda e, _kw=dict(out=o_ps[l, :, 0:11, :], in_=spool_raw_d[l, :, 4:15, :]): e.dma_start(**_kw)), writes=["o_ps"], sem="st_d2d")
            pv_ = part_d.rearrange("c (b t a) e -> b t c (a e)", t=4, a=8)
            for t in range(4):
                P.dma("sp", (lambda t: (lambda e, _kw=dict(out=R8[32 * t:32 * t + 32, :, :], in_=pv_[:, t, :, :]): e.dma_start(**_kw)))(t),
                      writes=["R8"], sem="R8")
            Rf = R.rearrange("p a e -> p (a e)")
            V((lambda e, _kw=dict(out=Rf, in0=R8[:, 0, :], in1=R8[:, 1, :], op=ALU.add): e.tensor_tensor(**_kw)), ["R8"], ["R"])
            for c_ in range(2, NCORES):
                V((lambda e, _kw=dict(out=Rf, in0=Rf, in1=R8[:, c_, :], op=ALU.add): e.tensor_tensor(**_kw)), ["R8", "R"], ["R"])
            prenorm_tile(xs[:], "xs", 0, hTs, "hT", 0)
            bt = lambda ap: ap.rearrange("p (t b) -> p b t", t=4)
            for ch in range(2):
                fm_chunk(256 + ch * 128, hTs, "hT", W,
                         lambda b, ch=ch: A((lambda e, _kw=dict(out=sg[:, ch, :], in_=ps[b][:, 0:W], func=AF.Sigmoid): e.activation(**_kw)), [f"ps{b}"], ["sg"]))
            for ch in range(2):
                fm_chunk(ch * 128, hTs, "hT", W,
                         lambda b, ch=ch: V((lambda e, _kw=dict(out=hs32[:, ch, :, 30:34], in0=bt(ps[b][:, 0:W]), in1=bt(sg[:, ch, :]), op=ALU.mult): e.tensor_tensor(**_kw)),
                                            [f"ps{b}", "sg"], ["hs32"]))
            for ch in range(2):
                fm_chunk(512 + ch * 128, hTs, "hT", W,
                         lambda b, ch=ch: A((lambda e, _kw=dict(out=uT[:, ch, :], in_=ps[b][:, 0:W], func=AF.Gelu): e.activation(**_kw)), [f"ps{b}"], ["uT"]))
            for ch in range(2):
                fm_chunk(1792 + ch * 128, hTs, "hT", W,
                         lambda b, ch=ch: A((lambda e, _kw=dict(out=xds[:, ch, :, 15:19], in_=bt(ps[b][:, 0:W])): e.copy(**_kw)), [f"ps{b}"], ["xds"]))
            b = tm_cols(0, 512, hTs, "hT", 0)
            A((lambda e, _kw=dict(out=tm1, in_=ps[b][:, 256:512], func=AF.Sigmoid): e.activation(**_kw)), [f"ps{b}"], ["tm1"])
            V((lambda e, _kw=dict(out=tm1, in0=ps[b][:, 0:256], in1=tm1, op=ALU.mult): e.tensor_tensor(**_kw)), [f"ps{b}", "tm1"], ["tm1"])
            b = tm_cols(1792, 256, hTs, "hT", 0)
            A((lambda e, _kw=dict(out=tm2, in_=ps[b][:, 0:256]): e.copy(**_kw)), [f"ps{b}"], ["tm2"])
            b = tm_cols(768, 256, hTs, "hT", 0)
            A((lambda e, _kw=dict(out=vg, in_=ps[b][:, 0:256], func=AF.Gelu): e.activation(**_kw)), [f"ps{b}"], ["vg"])
            ln_rows(vg, "vg", vn32, "vn32")
            for t in range(4):
                P.dma("sp", (lambda t: (lambda e, _kw=dict(out=o_cs[l, :, 26 + t, :], in_=tm1[32 * t:32 * t + 32, :]): e.dma_start(**_kw)))(t), reads=["tm1"], writes=["o_cs"], sem="st_tm1")
                P.dma("sp", (lambda t: (lambda e, _kw=dict(out=o_ps[l, :, 11 + t, :], in_=tm2[32 * t:32 * t + 32, :]): e.dma_start(**_kw)))(t), reads=["tm2"], writes=["o_ps"], sem="st_tm2")
            P.dma("sp", (lambda e, _kw=dict(out=o_gs[l], in_=vn32): e.dma_start(**_kw)), reads=["vn32"], writes=["o_gs"], sem="st_vn32")
            to_vnz(vn32, "vn32", vnz)
            b = gb()
            for ch in range(2):
                for j in range(2):
                    MM(ps[b][:, ch * 128:(ch + 1) * 128], vnz[:, 2 * ch + j, :], WsS[:, 2 * ch + j, :], j == 0, j == 1, ["vnz", "WsS"], [f"ps{b}"], ch == 1 and j == 1)
            V((lambda e, _kw=dict(out=sgt.rearrange("p c (t b) -> p c t b", t=4), in0=ps[b][:, 0:256].rearrange("p (c t b) -> p c t b", c=2, t=4),
                                             in1=BsT[:, :, 0:4].unsqueeze(3).to_broadcast([128, 2, 4, 32]), op=ALU.add): e.tensor_tensor(**_kw)), [f"ps{b}", "BsT"], ["sgt"])
            V((lambda e, _kw=dict(out=mixT[:, 2:4, :], in0=sgt, in1=uT, op=ALU.mult): e.tensor_tensor(**_kw)), ["sgt", "uT"], ["mixB"])
            cy4 = [cy[:, ch, :].rearrange("p (t b) -> p b t", t=4) for ch in range(2)]
            for j in range(31):
                for ch in range(2):
                    if j == 0:
                        V((lambda e, _kw=dict(out=cy4[ch], in0=hs32[:, ch, :, 0:4], scalar1=vecfm[:, ch, 4:5], scalar2=vecfm[:, ch, 0:1], op0=ALU.mult, op1=ALU.add): e.tensor_scalar(**_kw)),
                          ["hs32", "vecfm"], [f"cy{ch}", "cy"])
                    else:
                        V((lambda e, _kw=dict(out=cy4[ch], in0=hs32[:, ch, :, j:j + 4], scalar=vecfm[:, ch, 4 + j:5 + j], in1=cy4[ch], op0=ALU.mult, op1=ALU.add): e.scalar_tensor_tensor(**_kw)),
                          ["hs32", "vecfm", f"cy{ch}"], [f"cy{ch}"])
            P.op("pool", lambda e: e.nop(), ["cy0", "cy1"], ["cy"])
            conv_ln_silu(cy, W, ybf, ysq, cmean, cm2, crstd, mixT, "mixA", 0)
            V((lambda e, _kw=dict(out=sA[:, :, :, 1:19], in0=xds[:, :, :, 1:19], in1=xds[:, :, :, 0:18], op=ALU.add): e.tensor_tensor(**_kw)), ["xds"], ["pA"])
            V((lambda e, _kw=dict(out=sB[:, :, :, 3:19], in0=sA[:, :, :, 3:19], in1=sA[:, :, :, 1:17], op=ALU.add): e.tensor_tensor(**_kw)), ["pA"], ["pB"])
            V((lambda e, _kw=dict(out=sA[:, 1, :, 7:19], in0=sB[:, 1, :, 7:19], in1=sB[:, 1, :, 3:15], op=ALU.add): e.tensor_tensor(**_kw)), ["pB", "pA"], ["pA"])
            V((lambda e, _kw=dict(out=sB[64:128, 1, :, 15:19], in0=sA[64:128, 1, :, 15:19], in1=sA[64:128, 1, :, 7:11], op=ALU.add): e.tensor_tensor(**_kw)), ["pA", "pB"], ["pB"])
            for ch in range(2):
                for hf, src in ((0, sA), (1, sB)):
                    lo, hi = hf * 64, hf * 64 + 64
                    V((lambda e, _kw=dict(out=dT[lo:hi, ch, :].rearrange("p (t b) -> p b t", t=4), in0=src[lo:hi, ch, :, 15:19],
                                                                                     scalar=cst[lo:hi, C_INVW + ch:C_INVW + ch + 1], in1=xds[lo:hi, ch, :, 15:19],
                                                                                     op0=ALU.mult, op1=ALU.subtract): e.scalar_tensor_tensor(**_kw)), ["pA", "pB", "xds", "cst"], ["dT"])
            for ch in range(2):
                b = gb()
                MM(ps[b][:, 0:W], Wp[:, ch, :], dT[:, ch, :], True, True, ["Wp", "dT"], [f"ps{b}"], True)
                V((lambda e, _kw=dict(out=mixT[:, 6 + ch, :], in0=ps[b][:, 0:W], scalar1=vecfm[:, ch, 3:4], scalar2=None, op0=ALU.mult): e.tensor_scalar(**_kw)), [f"ps{b}", "vecfm"], ["mixD"])
            for h in range(4):
                attn_finish(R[:, 2 * h, 0:64], R[:, 2 * h, 64:65], R[:, 2 * h + 1, 0:64], R[:, 2 * h + 1, 64:65], ["R"], obuf[:, h * 64:(h + 1) * 64], "obuf0")
            head_norm_to_mix(obuf, "obuf0", osq, ycb, lam_init, mixT, "mixC", 0)
            if DBG and l == 0:
                dbgt = TA.get("dbgt", (8, 128), F32)
                ARK.update(TA.keys)
                G((lambda e, _kw=dict(out=dbgt, in_=mixT): e.tensor_copy(**_kw)), ["mixA", "mixB", "mixC", "mixD"], ["dbgt"])
                P.dma("sp", (lambda e, _kw=dict(out=o_dbg, in_=dbgt): e.dma_start(**_kw)), reads=["dbgt"], writes=["o_dbg"], sem="st_dbg")
            out_proj_tile(mixT, ["mixA", "mixB", "mixC", "mixD"], 0, xs[:], "xs")

            P.scope = "F"
            phase_barrier(["w_in", "w_out", "w_down"])
            TA.reset()
            hT = TA.get("hT", (8, 512), BF16)
            hTs = TA.get("hTs", (8, 128), BF16)
            NW = 3
            wgu = [TA.get(f"wgu{i}", (8, 256), BF16) for i in range(NW)]
            sgb = [TA.get(f"sgb{i}", (512,), F32) for i in range(2)]
            actT = TA.get("actT", (NFC, 512), BF16)
            actTs = TA.get("actTs", (NFC, 128), BF16)
            ARK.update(TA.keys)
            wd_v = w_down_d[l].rearrange("(fc p) n -> p fc n", p=128)
            for hf in range(2):
                P.dma("pool", (lambda hf: (lambda e, _kw=dict(out=w_down_sb[:, hf * 11:(hf + 1) * 11, :], in_=wd_v[:, hf * 11:(hf + 1) * 11, :]): e.dma_start(**_kw)))(hf),
                      writes=["w_down"], sem="w_down")
            P.dma("sp", (lambda e, _kw=dict(out=gpost[:], in_=gpost_d[l, 1].partition_broadcast(128)): e.dma_start(**_kw)), writes=["gpost"])
            prenorm_tile(xs[:], "xs", 1, hTs, "hTs", 0)
            wcount = 0
            for fb in range(NF):
                last_blk = (fb == NF - 1)
                for tl in range(4):
                    prenorm_tile(x[:, 4 * fb + tl, :], f"x{4 * fb + tl}", 1, hT, "hT", tl * 128)
                for f_ in range(NFC):
                    wi = wcount % NW
                    wcount += 1
                    P.dma("pool", (lambda wi, f_: (lambda e, _kw=dict(out=wgu[wi], in_=w_gu_d[l, f_]): e.dma_start(**_kw)))(wi, f_), writes=[f"wgu{wi}"], sem=f"wgu{wi}")
                    bg, bu = gb(), gb()
                    for kc in range(8):
                        MM(ps[bg][:], wgu[wi][:, kc, 0:128], hT[:, kc, :], kc == 0, kc == 7, [f"wgu{wi}", "hT"], [f"ps{bg}"], kc == 7)
                    for kc in range(8):
                        MM(ps[bu][:], wgu[wi][:, kc, 128:256], hT[:, kc, :], kc == 0, kc == 7, [f"wgu{wi}", "hT"], [f"ps{bu}"], kc == 7)
                    si = f_ % 2
                    A((lambda e, _kw=dict(out=sgb[si], in_=ps[bg][:], func=AF.Silu): e.activation(**_kw)), [f"ps{bg}"], [f"sgb{si}"])
                    V((lambda e, _kw=dict(out=actT[:, f_, :], in0=ps[bu][:], in1=sgb[si], op=ALU.mult): e.tensor_tensor(**_kw)), [f"ps{bu}", f"sgb{si}"], ["actT"])
                    if last_blk:
                        bg = gb()
                        for kc in range(8):
                            MM(ps[bg][:, 0:128], wgu[wi][:, kc, 0:128], hTs[:, kc, :], kc == 0, kc == 7, [f"wgu{wi}", "hTs"], [f"ps{bg}"], False)
                        for kc in range(8):
                            MM(ps[bg][:, 128:256], wgu[wi][:, kc, 128:256], hTs[:, kc, :], kc == 0, kc == 7, [f"wgu{wi}", "hTs"], [f"ps{bg}"], kc == 7)
                        A((lambda e, _kw=dict(out=sgb[si][:, 0:128], in_=ps[bg][:, 0:128], func=AF.Silu): e.activation(**_kw)), [f"ps{bg}"], [f"sgb{si}"])
                        V((lambda e, _kw=dict(out=actTs[:, f_, :], in0=ps[bg][:, 128:256], in1=sgb[si][:, 0:128], op=ALU.mult): e.tensor_tensor(**_kw)),
                          [f"ps{bg}", f"sgb{si}"], ["actTs"])

                def down_tile(aT, akey, mcol, xap, xkey):
                    bA, bB = gb(), gb()
                    for n, bk in enumerate((bA, bB)):
                        for f2 in range(NFC):
                            MM(ps[bk][:], aT[:, f2, mcol:mcol + 128], w_down_sb[:, f2, n * 512:(n + 1) * 512], f2 == 0, f2 == NFC - 1, [akey, "w_down"], [f"ps{bk}"], f2 == NFC - 1)
                    postnorm_add(bA, bB, xap, xkey)
                for tl in range(4):
                    down_tile(actT, "actT", tl * 128, x[:, 4 * fb + tl, :], f"x{4 * fb + tl}")
                if last_blk:
                    down_tile(actTs, "actTs", 0, xs[:], "xs")

        if doA:
            P.scope = "S1"
            phase_barrier(["w_in", "w_out", "w_down"])
            P.dma("pool", (lambda e, _kw=dict(out=w_in_sb[:, :, 1024:1792], in_=w_in_a_d.rearrange("(kc p) n -> p kc n", p=128)): e.dma_start(**_kw)), writes=["w_in"], sem="w_in")
            P.dma("sp", (lambda e, _kw=dict(out=gfm[:], in_=gfm_a_d): e.dma_start(**_kw)), writes=["gfm"])
            phase_barrier()
            TA.reset()
            hTs = TA.get("hT", (8, 128), BF16)
            Qblk = TA.get("Qblk", (2, 32, 32), BF16)
            kTs = TA.get("kTs", (2, 128), BF16)
            Vnew = TA.get("Vnew", (258,), BF16)
            kvo = TA.get("kvo", (512,), F32)
            NKV, NKT, NSX = 8, 3, 4
            kvs = [TA.get(f"kvs{i}", (514,), BF16) for i in range(NKV)]
            ktT = [TA.get(f"ktT{i}", (256,), BF16) for i in range(NKT)]
            Sx = [TA.get(f"Sx{i}", (32,), F32) for i in range(NSX)]
            pTs = [TA.get(f"pTs{i}", (32,), BF16) for i in range(NSX)]
            msk = TA.get("msk", (256,), F32)
            stg = [TA.get(f"stg{i}", (66,), F32) for i in range(2)]
            ARK.update(TA.keys)
            G((lambda e, _kw=dict(ap=Qblk, constant=0.0): e.memset(**_kw)), [], ["Qblk"])
            G((lambda e, _kw=dict(ap=Vnew[:, 256:258], constant=0.125): e.memset(**_kw)), [], ["Vnew"])
            for i in range(NKV):
                G((lambda e, _kw=dict(ap=kvs[i][:, 512:514], constant=1.0): e.memset(**_kw)), [], [f"kvs{i}"])
            prenorm_tile(xs[:], "xs", 0, hTs, "hT", 0)
            for chk in range(2):
                def ev_q(b, chk=chk):
                    pv = ps[b][:, 0:128].rearrange("p (t b) -> p b t", t=4)
                    for h2 in range(2):
                        for c in range(2):
                            h = 2 * chk + h2
                            V((lambda e, _kw=dict(out=Qblk[:, chk, :, :].rearrange("p b (t x) -> p b t x", t=4)[:, :, :, h * 2 + c],
                                                                          in0=pv, scalar1=cst[:, C_MH2C + h2 * 2 + c:C_MH2C + h2 * 2 + c + 1], scalar2=None, op0=ALU.mult): e.tensor_scalar(**_kw)),
                              [f"ps{b}", "cst"], ["Qblk"])
                fm_chunk(1024 + chk * 128, hTs, "hT", 128, ev_q)
            for chk in range(2):
                fm_chunk(1280 + chk * 128, hTs, "hT", 128,
                         lambda b, chk=chk: A((lambda e, _kw=dict(out=kTs[:, chk, :], in_=ps[b][:, 0:128]): e.copy(**_kw)), [f"ps{b}"], ["kTs"]))
            b = tm_cols(1280, 512, hTs, "hT", 0)
            V((lambda e, _kw=dict(out=kvo, in_=ps[b][:]): e.tensor_copy(**_kw)), [f"ps{b}"], ["kvo"])
            P.dma("sp", (lambda e, _kw=dict(out=o_ks, in_=kvo[:, 0:256]): e.dma_start(**_kw)), reads=["kvo"], writes=["o_ks"], sem="st_kvo")
            P.dma("sp", (lambda e, _kw=dict(out=o_vs, in_=kvo[:, 256:512]): e.dma_start(**_kw)), reads=["kvo"], writes=["o_vs"], sem="st_kvo")
            G((lambda e, _kw=dict(out=Vnew[:, 0:256], in0=kvo[:, 256:512], scalar1=0.125, scalar2=None, op0=ALU.mult): e.tensor_scalar(**_kw)), ["kvo"], ["Vnew"])
            cin_ap = o_part
            kvl = kv_d
            tiles = []
            tcount = 0
            for bb in range(32):
                for t in range(T8 + 1):
                    td = dict(bb=bb, t=t, new=(t == T8), n=len(tiles))
                    if not td["new"]:
                        td["sl"] = tcount % NKV
                        td["kr"] = tcount % NKT
                        tcount += 1
                    tiles.append(td)

            def stage_a(td):
                if td["new"]:
                    return
                sl_, kr, col = td["sl"], td["kr"], td["bb"] * T8 + td["t"]
                P.dma("pool", (lambda sl_, col: (lambda e, _kw=dict(out=kvs[sl_][:, 0:512], out_offset=None, in_=kvl,
                                                                                 in_offset=bass.IndirectOffsetOnAxis(ap=idx[:, col:col + 1], axis=0)): e.indirect_dma_start(**_kw)))(sl_, col),
                      reads=["idx"], writes=[f"kvs{sl_}"], sem=f"kvs{sl_}")
                b1 = gb()
                for hh in range(2):
                    TR(psb[b1][:, hh * 128:(hh + 1) * 128], kvs[sl_][:, hh * 128:(hh + 1) * 128], idb, [f"kvs{sl_}", "cstb"], [f"ps{b1}"], hh == 1)
                A((lambda e, _kw=dict(out=ktT[kr], in_=psb[b1][:, 0:256]): e.copy(**_kw)), [f"ps{b1}"], [f"ktT{kr}"])

            def stage_b(td):
                bb, t = td["bb"], td["t"]
                if not td["new"]:
                    kr = td["kr"]
                    lh = [ktT[kr][:, 0:128], ktT[kr][:, 128:256]]
                    lk = [f"ktT{kr}"]
                    bias_ap = cst[:, C_BIAS + t * 32:C_BIAS + (t + 1) * 32]
                    bkey = "cst"
                else:
                    lh = [kTs[:, 0, :], kTs[:, 1, :]]
                    lk = ["kTs"]
                    bias_ap = cstb[:, B_BN + bb * 32:B_BN + (bb + 1) * 32]
                    bkey = "cstb"
                b2 = gb()
                for hh in range(2):
                    MM(ps[b2][:, 0:32], lh[hh], Qblk[:, hh, bb, :], hh == 0, hh == 1, lk + ["Qblk"], [f"ps{b2}"], hh == 1)
                si = td["n"] % NSX
                V((lambda e, _kw=dict(out=Sx[si], in0=ps[b2][:, 0:32], scalar=ISQ, in1=bias_ap, op0=ALU.mult, op1=ALU.add): e.scalar_tensor_tensor(**_kw)),
                  [f"ps{b2}", bkey], [f"Sx{si}"])
                A((lambda e, _kw=dict(out=pTs[si], in_=Sx[si], func=AF.Exp): e.activation(**_kw)), [f"Sx{si}"], [f"pTs{si}"])

            def stage_c(td):
                bb, t, new = td["bb"], td["t"], td["new"]
                bo_s = 6 + (bb % 2)
                si = td["n"] % NSX
                if not new:
                    rhs_v = kvs[td["sl"]][:, 256:513]
                    rk = [f"kvs{td['sl']}"]
                else:
                    rhs_v = Vnew[:, 0:257]
                    rk = ["Vnew"]
                MM(ps[bo_s][0:32, 0:257], pTs[si], rhs_v, t == 0, new, [f"pTs{si}"] + rk, [f"ps{bo_s}"], new)
                if not new:
                    return
                sg_i = bb % 2
                V((lambda e, _kw=dict(out=msk[0:32, :], in0=ps[bo_s][0:32, 0:256], in1=cst[0:32, C_HM:C_HM + 256], op=ALU.mult): e.tensor_tensor(**_kw)),
                  [f"ps{bo_s}", "cst"], ["msk"])
                V((lambda e, _kw=dict(out=stg[sg_i][0:32, 0:64], in_=msk[0:32, :].rearrange("p (h e) -> p e h", h=4), axis=AX.X, op=ALU.add): e.tensor_reduce(**_kw)),
                  ["msk"], [f"stg{sg_i}"])
                A((lambda e, _kw=dict(out=stg[sg_i][0:32, 64:65], in_=ps[bo_s][0:32, 256:257]): e.copy(**_kw)), [f"ps{bo_s}"], [f"stg{sg_i}"])
                P.dma("sp", (lambda sg_i, bb: (lambda e, _kw=dict(out=cin_ap[bb * 32:(bb + 1) * 32, :], in_=stg[sg_i][0:32, 0:65]): e.dma_start(**_kw)))(sg_i, bb),
                      reads=[f"stg{sg_i}"], writes=["o_part"], sem=f"st_stg{sg_i}")

            NTL = len(tiles)
            for i in range(NTL + 3):
                if i < NTL:
                    stage_a(tiles[i])
                if 0 <= i - 1 < NTL:
                    stage_b(tiles[i - 1])
                if 0 <= i - 3 < NTL:
                    stage_c(tiles[i - 3])

        P.scope = "final"
        if doB:
            yv = o_yp.rearrange("(t p) d -> p t d", p=128)
            for t0 in range(0, NT, 4):
                P.dma("sp", (lambda t0: (lambda e, _kw=dict(out=yv[:, t0:t0 + 4, :], in_=x[:, t0:t0 + 4, :]): e.dma_start(**_kw)))(t0),
                      reads=[f"x{t}" for t in range(t0, t0 + 4)], writes=["o_yp"], sem="st_x")
            P.dma("sp", (lambda e, _kw=dict(out=o_ys, in_=xs[:]): e.dma_start(**_kw)), reads=["xs"], writes=["o_ys"], sem="st_xs")
        P.finish(out_keys)
        nsem = len(P.dsem) + len(P.esem)
        assert nsem < 140, nsem
    return nc


_CACHE = {}


def run(inputs, cfg):
    L, S = cfg["L"], cfg["S"]
    inp = {k: np.asarray(v) for k, v in inputs.items()}
    st = host_static(inp, cfg)
    state = {"xp": [inp["x_prompt"][c] for c in range(NCORES)],
             "xs": inp["x_sample"].transpose(1, 0, 2).reshape(128, D), "part": None}
    acc = {k: [None] * L for k in ("kp", "vp", "cp", "pp", "ks", "vs", "cs", "ps", "gs")}
    for stage in range(L + 1):
        doA, doB = stage < L, stage >= 1
        key = (tuple(sorted(cfg.items())), doA, doB)
        if key not in _CACHE:
            _CACHE[key] = build(cfg, stage)
        nc = _CACHE[key]
        maps = stage_maps(inp, cfg, stage, st, state)
        res = run_bass_kernel_spmd(nc, maps, core_ids=list(range(NCORES)))
        R_ = res.results
        del maps
        cat = lambda k: np.stack([np.asarray(R_[c][k]) for c in range(NCORES)], axis=0)
        if doB:
            lb = stage - 1
            acc["kp"][lb] = cat("o_kp")[:, 0]
            acc["vp"][lb] = cat("o_vp")[:, 0]
            acc["cp"][lb] = cat("o_cp")[:, 0]
            acc["pp"][lb] = cat("o_pp")[:, 0]
            r0 = R_[0]
            acc["cs"][lb] = np.asarray(r0["o_cs"])[0]
            acc["ps"][lb] = np.asarray(r0["o_ps"])[0]
            acc["gs"][lb] = np.asarray(r0["o_gs"])[0].reshape(4, 32, 256).transpose(1, 0, 2)
            state["xp"] = [np.asarray(R_[c]["o_yp"]) for c in range(NCORES)]
            state["xs"] = np.asarray(r0["o_ys"])
        if doA:
            la = stage
            r0 = R_[0]
            acc["ks"][la] = np.asarray(r0["o_ks"]).reshape(4, 32, 256).transpose(1, 0, 2)
            acc["vs"][la] = np.asarray(r0["o_vs"]).reshape(4, 32, 256).transpose(1, 0, 2)
            state["part"] = cat("o_part")
    y_p = np.stack(state["xp"], axis=0)
    y_s = state["xs"].reshape(4, 32, D).transpose(1, 0, 2)
    stk = lambda k: np.stack(acc[k], axis=0)
    k_p = stk("kp").reshape(L, NCORES, S, 4, 2, 32)
    v_p = stk("vp").reshape(L, NCORES, S, 4, 64)
    k_s = stk("ks").reshape(L, 32, 4, 4, 2, 32)
    v_s = stk("vs").reshape(L, 32, 4, 4, 64)
    outs = (y_p, y_s, k_p, v_p, stk("cp"), stk("pp"), k_s, v_s, stk("cs"), stk("ps"), stk("gs"))
    return tuple(np.ascontiguousarray(o, dtype=np.float32) for o in outs)


def kernel(**inputs):
    cfg = make_cfg()
    return run(inputs, cfg)
```

```python
import contextlib
import math
import numpy as np
import concourse.bass as bass
import concourse.mybir as mybir
from concourse.bass_utils import run_bass_kernel_spmd

F32 = mybir.dt.float32
BF16 = mybir.dt.bfloat16
I32 = mybir.dt.int32
AF = mybir.ActivationFunctionType
ALU = mybir.AluOpType
AX = mybir.AxisListType

NCORES = 8
D = 1024
DFF = 2816
NFC = 22
RMS_EPS = 1e-6
LN_EPS = 1e-5
SLOPES = [2.0 ** (-8.0 * (h + 1) / 4) for h in range(4)]
ISQ = 32 ** -0.5
NEG = -30000.0
ENGS = ["pe", "act", "dve", "pool", "sp"]

C_ID, C_TRI, C_BLK, C_BT, C_INVW, C_ICNT, C_MC, C_MH2C, C_I16, C_HM, C_E8, C_BIAS = 0, 128, 256, 384, 448, 450, 482, 484, 488, 489, 745, 873
B_ID, B_MA, B_MB, B_ONE, B_SEL, B_BN, NCB = 0, 128, 640, 1152, 1280, 1408, 2432
VB_LNG, VB_LNB, VB_GSUB, VB_LQ1, VB_LK1, VB_LQ2, VB_LK2, NVB = 0, 256, 512, 576, 608, 640, 672, 704


class Prog:
    def __init__(self, nc, stack):
        self.nc = nc
        self.stack = stack
        self.q = {e: [] for e in ENGS}
        self.esem = {e: stack.enter_context(nc.semaphore("es_" + e)) for e in ENGS}
        self.ecnt = {e: 0 for e in ENGS}
        self.seen = {e: {} for e in ENGS}
        self.buf = {}
        self.dsem = {}
        self.dcnt = {}
        self.scope = None
        self.use_scopes = False

    def _st(self, k):
        if k not in self.buf:
            self.buf[k] = {"w": None, "r": []}
        return self.buf[k]

    def _waits(self, eng, reads, writes):
        evs = []
        for k in reads:
            s = self._st(k)
            if s["w"] is not None:
                evs.append(s["w"])
        for k in writes:
            s = self._st(k)
            if s["w"] is not None:
                evs.append(s["w"])
            evs.extend(s["r"])
        need = {}
        for kind, sid, val in evs:
            if kind == "E" and sid == eng and eng == "pe":
                continue
            key = (kind, sid)
            if self.seen[eng].get(key, 0) >= val:
                continue
            need[key] = max(need.get(key, 0), val)
        out = []
        for key, val in need.items():
            self.seen[eng][key] = val
            sem = self.esem[key[1]] if key[0] == "E" else self.dsem[key[1]]
            out.append((sem, val))
        return out

    def _record(self, ev, reads, writes):
        for k in reads:
            self._st(k)["r"].append(ev)
        for k in writes:
            s = self._st(k)
            s["w"] = ev
            s["r"] = []

    def op(self, eng, fn, reads=(), writes=(), signal=True):
        waits = self._waits(eng, reads, writes)
        if signal:
            self.ecnt[eng] += 1
            ev = ("E", eng, self.ecnt[eng])
            inc = (self.esem[eng], 1)
        else:
            ev = ("E", eng, self.ecnt[eng] + 1)
            inc = None
        self.q[eng].append((waits, fn, inc, self.scope))
        self._record(ev, reads, writes)

    def dma(self, eng, fn, reads=(), writes=(), sem=None, inc=16):
        if sem is None:
            sem = writes[0]
        if sem not in self.dsem:
            self.dsem[sem] = self.stack.enter_context(self.nc.semaphore("ds_" + sem))
            self.dcnt[sem] = 0
        waits = self._waits(eng, reads, writes)
        self.dcnt[sem] += inc
        ev = ("D", sem, self.dcnt[sem])
        self.q[eng].append((waits, fn, (self.dsem[sem], inc), self.scope))
        self._record(ev, reads, writes)

    def barrier(self, keys):
        self.op("sp", lambda e: e.nop(), reads=(), writes=list(keys))

    def finish(self, out_keys):
        waits = self._waits("sp", out_keys, [])
        self.q["sp"].append((waits, None, None, None))
        nc = self.nc
        with nc.Block() as block:
            def run(engobj, lst):
                cur, cur_id = None, None
                for waits, fn, inc, scope in lst:
                    if self.use_scopes and scope != cur:
                        if cur is not None:
                            nc.leave_named_scope(cur, cur_id, False)
                        cur = scope
                        if cur is not None:
                            cur_id, _ = nc.enter_named_scope(cur, False)
                    for sem, val in waits:
                        engobj.wait_ge(sem, val)
                    if fn is None:
                        continue
                    ins = fn(engobj)
                    if inc is not None:
                        ins.then_inc(inc[0], inc[1])
                if self.use_scopes and cur is not None:
                    nc.leave_named_scope(cur, cur_id, False)

            @block.tensor
            def _(e):
                run(e, self.q["pe"])

            @block.scalar
            def _(e):
                run(e, self.q["act"])

            @block.vector
            def _(e):
                run(e, self.q["dve"])

            @block.gpsimd
            def _(e):
                run(e, self.q["pool"])

            @block.sync
            def _(e):
                run(e, self.q["sp"])


class Arena:
    def __init__(self, ap, ncols):
        self.ap = ap
        self.n = ncols
        self.off = 0
        self.keys = set()
        self.peak = 0

    def reset(self):
        self.off = 0

    def get(self, key, free, dtype):
        n = int(np.prod(free))
        cols = n * (2 if dtype in (F32, I32) else 1)
        cols = (cols + 1) // 2 * 2
        assert self.off + cols <= self.n, ("arena overflow", key, self.off + cols, self.n)
        a = self.ap[:, self.off:self.off + cols]
        self.off += cols
        self.peak = max(self.peak, self.off)
        self.keys.add(key)
        if dtype in (F32, I32):
            a = a.bitcast(dtype)
        a = a[:, 0:n]
        if len(free) == 2:
            a = a.rearrange("p (a b) -> p a b", a=free[0])
        elif len(free) == 3:
            a = a.rearrange("p (a b c) -> p a b c", a=free[0], b=free[1])
        return a


def make_cfg(L=4, S=2048, NP=64, NPOOL=2560):
    return dict(L=L, S=S, NP=NP, NPOOL=NPOOL, NT=S // 128, NB=S // 256, NF=S // 512, T8=NP // 8, PAST=NP * 128)


def host_consts(cfg, core):
    T8, PAST = cfg["T8"], cfg["PAST"]
    p = np.arange(128)
    cst = np.zeros((128, C_BIAS + T8 * 32), np.float32)
    cst[:, C_ID:C_ID + 128] = np.eye(128)
    cst[:, C_TRI:C_TRI + 128] = (p[:, None] <= p[None, :])
    cst[:, C_BLK:C_BLK + 128] = ((p[:, None] % 32) == (p[None, :] % 32))
    for h in range(4):
        for dd in range(16):
            cst[:, C_BT + h * 16 + dd] = SLOPES[h] * (p - 127 - dd * 128)
    wtab = np.zeros((128, 2))
    wtab[:64, 0], wtab[64:, 0], wtab[:64, 1], wtab[64:, 1] = 2, 4, 8, 16
    cst[:, C_INVW:C_INVW + 2] = 1.0 / wtab
    for ch in range(2):
        for pos in range(16):
            cst[:, C_ICNT + ch * 16 + pos] = 1.0 / np.minimum(pos + 1, wtab[:, ch])
    for c in range(2):
        cst[:, C_MC + c] = ((p // 32) % 2 == c)
    for h2 in range(2):
        for c in range(2):
            cst[:, C_MH2C + h2 * 2 + c] = ((p // 64 == h2) & ((p // 32) % 2 == c))
    cst[:, C_I16] = p % 16
    r = np.arange(32)
    cols = np.arange(256)
    cst[:32, C_HM:C_HM + 256] = (((r % 8) // 2)[:, None] == (cols // 64)[None, :])
    cst[:8, C_E8:C_E8 + 128] = (np.arange(8)[:, None] == (p // 16)[None, :])
    col = np.arange(32)
    hcol = (col % 8) // 2
    sl = np.array(SLOPES)[hcol]
    for t in range(T8):
        kpos = (8 * t + p // 16) * 128 + 16 * core + p % 16
        cst[:, C_BIAS + t * 32:C_BIAS + (t + 1) * 32] = sl[None, :] * (kpos[:, None] - (PAST + 3))
    cb = np.zeros((128, NCB), np.float32)
    cb[:, B_ID:B_ID + 128] = np.eye(128)
    tri = (p[:, None] <= p[None, :]).astype(np.float32)
    for c in range(2):
        cb[:, B_MA + c * 256:B_MA + c * 256 + 128] = tri
        cb[:, B_MA + c * 256 + 128:B_MA + c * 256 + 256] = 1.0
        cb[:, B_MB + c * 256 + 128:B_MB + c * 256 + 256] = tri
    cb[:, B_ONE:B_ONE + 128] = 1.0 / 256
    cb[:4, B_SEL:B_SEL + 128] = (np.arange(4)[:, None] == (p // 32)[None, :])
    tq = col // 8
    tk, bk = p // 32, p % 32
    for b in range(32):
        ok = (bk[:, None] == b) & (tk[:, None] <= tq[None, :])
        cb[:, B_BN + b * 32:B_BN + (b + 1) * 32] = np.where(ok, sl[None, :] * (tk[:, None] - 3.0), NEG)
    return cst, cb


def host_static(inp, cfg):
    L, S, NP, NPOOL, T8 = cfg["L"], cfg["S"], cfg["NP"], cfg["NPOOL"], cfg["T8"]
    f = lambda a: np.ascontiguousarray(a, dtype=np.float32)
    st = {}
    st["pt8"] = np.ascontiguousarray(np.asarray(inp["page_table"]).reshape(32, T8, 8).transpose(2, 0, 1).reshape(8, 32 * T8).astype(np.int32))
    vec = np.zeros((L, 128, 2, 35), np.float32)
    fm2 = lambda a: np.asarray(a).reshape(L, 2, 128).transpose(0, 2, 1)
    vec[:, :, :, 0] = fm2(inp["conv_b"])
    vec[:, :, :, 1] = fm2(inp["conv_ln_g"])
    vec[:, :, :, 2] = fm2(inp["conv_ln_b"])
    vec[:, :, :, 3] = fm2(inp["pool_scale"])
    vec[:, :, :, 4:35] = np.asarray(inp["conv_w"]).reshape(L, 31, 2, 128).transpose(0, 3, 2, 1)
    st["vecfm"] = vec
    gfm = np.zeros((L, 128, 2, 8), np.float32)
    gfm[:, :, 0, :] = np.asarray(inp["g_mix_pre"]).reshape(L, 8, 128).transpose(0, 2, 1)
    gfm[:, :, 1, :] = np.asarray(inp["g_ffn_pre"]).reshape(L, 8, 128).transpose(0, 2, 1)
    st["gfm"] = gfm
    st["vbc"] = f(np.concatenate([np.asarray(inp[k]).reshape(L, -1) for k in
                                  ("sgu_ln_g", "sgu_ln_b", "attn_sub_g", "lambda_q1", "lambda_k1", "lambda_q2", "lambda_k2")], axis=1).reshape(L, 1, NVB))
    st["gpost"] = f(np.stack([np.asarray(inp["g_mix_post"]), np.asarray(inp["g_ffn_post"])], axis=1).reshape(L, 2, 1, D))
    st["consts"] = [host_consts(cfg, c) for c in range(NCORES)]
    return st


def stage_maps(inp, cfg, stage, st, state):
    L, S, NP, NPOOL, T8 = cfg["L"], cfg["S"], cfg["NP"], cfg["NPOOL"], cfg["T8"]
    f = lambda a: np.ascontiguousarray(a, dtype=np.float32)
    doA, doB = stage < L, stage >= 1
    shared = {"xs": f(state["xs"])}
    if doB:
        lb = stage - 1
        sl = slice(lb, lb + 1)
        sc = np.asarray(inp["state_conv"])[sl]
        sp_ = np.asarray(inp["state_pool"])[sl]
        shared["sconv_fm"] = f(sc.transpose(0, 3, 1, 2))
        shared["spool_fm"] = f(sp_.transpose(0, 3, 1, 2))
        shared["sconv_raw"] = f(sc)
        shared["spool_raw"] = f(sp_)
        shared["w_in"] = f(np.asarray(inp["w_in"])[sl])
        shared["w_out"] = f(np.asarray(inp["w_out"])[sl])
        wgu = np.asarray(inp["w_gu"])[sl].reshape(1, 8, 128, 2, NFC, 128)
        shared["w_gu_t"] = f(wgu.transpose(0, 4, 2, 1, 3, 5).reshape(1, NFC, 128, 8, 256))
        shared["w_down"] = f(np.asarray(inp["w_down"])[sl])
        for k in ("vecfm", "gfm", "vbc", "gpost"):
            shared[k] = f(st[k][sl])
        shared["sgu_w"] = f(np.asarray(inp["sgu_w"])[sl])
        shared["sgu_b"] = f(np.asarray(inp["sgu_b"])[sl])
        shared["pool_w"] = f(np.asarray(inp["pool_w"])[sl])
        lam_init = 0.8 - 0.6 * math.exp(-0.3 * lb)
        lamc = np.zeros((128, 2), np.float32)
        lamc[:, 0] = -lam_init
        lamc[:, 1] = 1.0 - lam_init
        shared["lamc"] = lamc
        shared["part"] = f(state["part"])
    if doA:
        la = stage
        shared["pt8"] = st["pt8"]
        shared["w_in_a"] = f(np.asarray(inp["w_in"])[la][:, 1024:1792])
        shared["gfm_a"] = f(st["gfm"][la])
        ck = np.asarray(inp["cache_k"])[la].reshape(NPOOL, 128, 256)
        cv = np.asarray(inp["cache_v"])[la].reshape(NPOOL, 128, 256)
    maps = []
    for c in range(NCORES):
        m = dict(shared)
        m["cst"], m["cstb"] = st["consts"][c]
        if doB:
            m["xp"] = f(state["xp"][c])
        if doA:
            kv = np.empty((NPOOL, 16, 512), np.float32)
            kv[..., 0:256] = ck[:, 16 * c:16 * c + 16]
            kv[..., 256:512] = cv[:, 16 * c:16 * c + 16]
            m["kv"] = kv.reshape(NPOOL * 16, 512)
        maps.append(m)
    return maps


def build(cfg, stage):
    L, S, NP, NPOOL, NT, NB, NF, T8, PAST = (cfg[k] for k in ("L", "S", "NP", "NPOOL", "NT", "NB", "NF", "T8", "PAST"))
    doA = stage < L
    doB = stage >= 1
    nc = bass.Bass("TRN2", target_bir_lowering=False)
    NC_ = C_BIAS + T8 * 32

    def din(name, shape, dt=F32):
        return nc.dram_tensor(name, list(shape), dt, kind="ExternalInput").ap()

    def dout(name, shape):
        return nc.dram_tensor(name, list(shape), F32, kind="ExternalOutput").ap()

    xs_d = din("xs", [128, D])
    cst_d = din("cst", [128, NC_]); cstb_d = din("cstb", [128, NCB])
    out_keys = []
    if doB:
        xp_d = din("xp", [S, D])
        sconv_fm_d = din("sconv_fm", [1, 256, 32, 30]); spool_fm_d = din("spool_fm", [1, 256, 32, 15])
        sconv_raw_d = din("sconv_raw", [1, 32, 30, 256]); spool_raw_d = din("spool_raw", [1, 32, 15, 256])
        w_in_d = din("w_in", [1, D, 2048]); w_out_d = din("w_out", [1, D, D])
        w_gu_d = din("w_gu_t", [1, NFC, 128, 8, 256]); w_down_d = din("w_down", [1, DFF, D])
        vecfm_d = din("vecfm", [1, 128, 2, 35]); gfm_d = din("gfm", [1, 128, 2, 8]); vbc_d = din("vbc", [1, 1, NVB])
        gpost_d = din("gpost", [1, 2, 1, D]); sguw_d = din("sgu_w", [1, 4, 128, 128]); sgub_d = din("sgu_b", [1, 4, 128])
        poolw_d = din("pool_w", [1, 4, 64, 64]); lamc_d = din("lamc", [128, 2]); part_d = din("part", [NCORES, 1024, 65])
        o_yp = dout("o_yp", [S, D]); o_ys = dout("o_ys", [128, D])
        o_kp = dout("o_kp", [1, S, 256]); o_vp = dout("o_vp", [1, S, 256])
        o_cp = dout("o_cp", [1, 30, 256]); o_pp = dout("o_pp", [1, 15, 256])
        o_cs = dout("o_cs", [1, 32, 30, 256]); o_ps = dout("o_ps", [1, 32, 15, 256]); o_gs = dout("o_gs", [1, 128, 256])
        out_keys += ["o_yp", "o_ys", "o_kp", "o_vp", "o_cp", "o_pp", "o_cs", "o_ps", "o_gs"]
    if doA:
        kv_d = din("kv", [NPOOL * 16, 512])
        pt8_d = din("pt8", [8, 32 * T8], I32)
        w_in_a_d = din("w_in_a", [D, 768]); gfm_a_d = din("gfm_a", [128, 2, 8])
        o_ks = dout("o_ks", [128, 256]); o_vs = dout("o_vs", [128, 256]); o_part = dout("o_part", [1024, 65])
        out_keys += ["o_ks", "o_vs", "o_part"]
    DBG = False

    with contextlib.ExitStack() as st:
        P = Prog(nc, st)
        P.use_scopes = bool(cfg.get("scopes", False))
        P.scope = "setup"
        SB = lambda n, s, d: st.enter_context(nc.sbuf_tensor(n, list(s), d))
        x = SB("x", [128, NT, D], F32)
        xs = SB("xs_sb", [128, D], F32)
        cst = SB("cst_sb", [128, NC_], F32)
        cstb = SB("cstb_sb", [128, NCB], BF16)
        Wt = SB("Wt", [128, 24576], BF16)
        TCOLS = 32896
        Tt = SB("Tt", [128, TCOLS], BF16)
        hn = SB("hn", [128, D], BF16)
        tmpn = SB("tmpn", [128, 512], F32)
        gpost = SB("gpost_sb", [128, D], F32)
        vbc = SB("vbc_sb", [128, NVB], F32)
        vecfm = SB("vecfm_sb", [128, 2, 35], F32)
        gfm = SB("gfm_sb", [128, 2, 8], F32)
        stt = SB("stt", [128, 48], F32)
        idx = SB("idx", [128, 32 * T8], I32)
        WsT = SB("WsT", [128, 4, 128], BF16)
        WsS = SB("WsS", [128, 4, 128], BF16)
        BsT = SB("BsT", [128, 2, 128], F32)
        Wp = SB("Wp", [128, 2, 128], BF16)
        Wpst = SB("Wpst", [128, 2, 64], F32)
        zer = SB("zer", [128, 512], BF16)
        nlam = SB("nlam", [128, 4], F32)
        lamc = SB("lamc_sb", [128, 2], F32)
        ps = [st.enter_context(nc.psum_tensor(f"ps{i}", [128, 512], F32)) for i in range(8)]
        psb = [p_[:].bitcast(BF16) for p_ in ps]
        TA = Arena(Tt[:], TCOLS)

        idf = cst[:, C_ID:C_ID + 128]
        idb = cstb[:, B_ID:B_ID + 128]
        w_in_sb = Wt[:, 0:16384].rearrange("p (k n) -> p k n", k=8)
        w_out_sb = Wt[:, 16384:24576].rearrange("p (k n) -> p k n", k=8)
        w_down_sb = Wt[:, 0:22528].rearrange("p (k n) -> p k n", k=NFC)

        V = lambda fn, r, w: P.op("dve", fn, r, w)
        A = lambda fn, r, w: P.op("act", fn, r, w)
        G = lambda fn, r, w: P.op("pool", fn, r, w)

        def MM(out, lhsT, rhs, start, stop, r, w, signal):
            P.op("pe", lambda e: e.matmul(out, lhsT=lhsT, rhs=rhs, start=start, stop=stop, skip_group_check=True), r, w, signal)

        def TR(out, in_, ident, r, w, signal):
            P.op("pe", lambda e: e.transpose(out=out, in_=in_, identity=ident), r, w, signal)

        gbc = [0]

        def gb():
            gbc[0] = (gbc[0] + 1) % 6
            return gbc[0]

        sctr = [0]

        def sc(n=1):
            if sctr[0] + n > 48:
                sctr[0] = 0
            a = sctr[0]
            sctr[0] += n
            return a

        P.dma("sp", (lambda e, _kw=dict(out=cst[:], in_=cst_d): e.dma_start(**_kw)), writes=["cst"])
        P.dma("pool", (lambda e, _kw=dict(out=cstb[:], in_=cstb_d): e.dma_start(**_kw)), writes=["cstb"])
        G((lambda e, _kw=dict(ap=zer[:], constant=0.0): e.memset(**_kw)), [], ["zer"])
        G((lambda e, _kw=dict(ap=Wp[:], constant=0.0): e.memset(**_kw)), [], ["Wp"])
        if doB:
            xv = xp_d.rearrange("(t p) d -> p t d", p=128)
            for t0 in range(0, NT, 4):
                P.dma("sp", (lambda t0: (lambda e, _kw=dict(out=x[:, t0:t0 + 4, :], in_=xv[:, t0:t0 + 4, :]): e.dma_start(**_kw)))(t0),
                      writes=[f"x{t}" for t in range(t0, t0 + 4)], sem=f"xl{t0 // 4 % 4}")
            P.dma("sp", (lambda e, _kw=dict(out=lamc[:], in_=lamc_d): e.dma_start(**_kw)), writes=["lamc"])
        P.dma("sp", (lambda e, _kw=dict(out=xs[:], in_=xs_d): e.dma_start(**_kw)), writes=["xs"])
        if doA:
            TA.reset()
            pt8i = TA.get("pt8i", (32 * T8,), I32)
            pt8f = TA.get("pt8f", (32 * T8,), F32)
            idxf = TA.get("idxf", (32 * T8,), F32)
            P.dma("sp", (lambda e, _kw=dict(out=pt8i[0:8, :], in_=pt8_d): e.dma_start(**_kw)), writes=["pt8i"])
            V((lambda e, _kw=dict(out=pt8f[0:8, :], in_=pt8i[0:8, :]): e.tensor_copy(**_kw)), ["pt8i"], ["pt8f"])
            b0 = gb()
            MM(ps[b0][:, 0:32 * T8], cst[0:8, C_E8:C_E8 + 128], pt8f[0:8, :], True, True, ["cst", "pt8f"], [f"ps{b0}"], True)
            V((lambda e, _kw=dict(out=idxf, in0=ps[b0][:, 0:32 * T8], scalar1=16.0, scalar2=cst[:, C_I16:C_I16 + 1], op0=ALU.mult, op1=ALU.add): e.tensor_scalar(**_kw)),
              [f"ps{b0}", "cst"], ["idxf"])
            V((lambda e, _kw=dict(out=idx[:], in_=idxf): e.tensor_copy(**_kw)), ["idxf"], ["idx"])

        def prenorm_tile(xap, xkey, gsel, hT, hkey, tcol):
            c0 = sc(3)
            hnj = hn[:]
            A((lambda e, _kw=dict(out=hnj, in_=xap, func=AF.Square, accum_out=stt[:, c0:c0 + 1]): e.activation(**_kw)), [xkey], ["hn", "stt"])
            A((lambda e, _kw=dict(out=stt[:, c0 + 1:c0 + 2], in_=stt[:, c0:c0 + 1], func=AF.Sqrt, scale=1.0 / D, bias=RMS_EPS): e.activation(**_kw)), ["stt"], ["stt"])
            V((lambda e, _kw=dict(out=stt[:, c0 + 2:c0 + 3], in_=stt[:, c0 + 1:c0 + 2]): e.reciprocal(**_kw)), ["stt"], ["stt"])
            V((lambda e, _kw=dict(out=hnj, in0=xap, scalar1=stt[:, c0 + 2:c0 + 3], scalar2=None, op0=ALU.mult): e.tensor_scalar(**_kw)), [xkey, "stt"], ["hn"])
            b = gb()
            for kc in range(8):
                TR(psb[b][:, kc * 128:(kc + 1) * 128], hn[:, kc * 128:(kc + 1) * 128], idb, ["hn", "cstb"], [f"ps{b}"], kc == 7)
            V((lambda e, _kw=dict(out=hT[:, :, tcol:tcol + 128], in0=psb[b].rearrange("p (k t) -> p k t", k=8),
                                        in1=gfm[:, gsel, :].unsqueeze(2).to_broadcast([128, 8, 128]), op=ALU.mult): e.tensor_tensor(**_kw)),
              [f"ps{b}", "gfm"], [hkey])

        def postnorm_add(bA, bB, xap, xkey):
            c0 = sc(5)
            tj = tmpn[:].bitcast(BF16)
            A((lambda e, _kw=dict(out=tj[:, 0:512], in_=ps[bA][:], func=AF.Square, accum_out=stt[:, c0:c0 + 1]): e.activation(**_kw)), [f"ps{bA}"], ["tmpn", "stt"])
            A((lambda e, _kw=dict(out=tj[:, 512:1024], in_=ps[bB][:], func=AF.Square, accum_out=stt[:, c0 + 1:c0 + 2]): e.activation(**_kw)), [f"ps{bB}"], ["tmpn", "stt"])
            V((lambda e, _kw=dict(out=stt[:, c0 + 2:c0 + 3], in0=stt[:, c0:c0 + 1], in1=stt[:, c0 + 1:c0 + 2], op=ALU.add): e.tensor_tensor(**_kw)), ["stt"], ["stt"])
            A((lambda e, _kw=dict(out=stt[:, c0 + 3:c0 + 4], in_=stt[:, c0 + 2:c0 + 3], func=AF.Sqrt, scale=1.0 / D, bias=RMS_EPS): e.activation(**_kw)), ["stt"], ["stt"])
            V((lambda e, _kw=dict(out=stt[:, c0 + 4:c0 + 5], in_=stt[:, c0 + 3:c0 + 4]): e.reciprocal(**_kw)), ["stt"], ["stt"])
            for n, bk in enumerate((bA, bB)):
                V((lambda e, _kw=dict(out=tmpn[:], in0=ps[bk][:], scalar=stt[:, c0 + 4:c0 + 5], in1=gpost[:, n * 512:(n + 1) * 512],
                                                            op0=ALU.mult, op1=ALU.mult): e.scalar_tensor_tensor(**_kw)), [f"ps{bk}", "stt", "gpost"], ["tmpn"])
                G((lambda e, _kw=dict(out=xap[:, n * 512:(n + 1) * 512], in0=xap[:, n * 512:(n + 1) * 512], in1=tmpn[:], op=ALU.add): e.tensor_tensor(**_kw)),
                  ["tmpn", xkey], [xkey])

        def fm_chunk(col, hT, hkey, W, evac):
            b = gb()
            for kc in range(8):
                MM(ps[b][:, 0:W], w_in_sb[:, kc, col:col + 128], hT[:, kc, 0:W], kc == 0, kc == 7, ["w_in", hkey], [f"ps{b}"], kc == 7)
            evac(b)

        def tm_cols(col, N, hT, hkey, tcol):
            b = gb()
            for kc in range(8):
                MM(ps[b][:, 0:N], hT[:, kc, tcol:tcol + 128], w_in_sb[:, kc, col:col + N], kc == 0, kc == 7, ["w_in", hkey], [f"ps{b}"], kc == 7)
            return b

        def ln_rows(vg, vgk, vn32, vnk):
            c0 = sc(10)
            V((lambda e, _kw=dict(out=stt[:, c0:c0 + 6], in_=vg): e.bn_stats(**_kw)), [vgk], ["stt"])
            V((lambda e, _kw=dict(out=stt[:, c0 + 6:c0 + 8], in_=stt[:, c0:c0 + 6]): e.bn_aggr(**_kw)), ["stt"], ["stt"])
            A((lambda e, _kw=dict(out=stt[:, c0 + 8:c0 + 9], in_=stt[:, c0 + 7:c0 + 8], func=AF.Sqrt, scale=1.0, bias=LN_EPS): e.activation(**_kw)), ["stt"], ["stt"])
            V((lambda e, _kw=dict(out=stt[:, c0 + 9:c0 + 10], in_=stt[:, c0 + 8:c0 + 9]): e.reciprocal(**_kw)), ["stt"], ["stt"])
            V((lambda e, _kw=dict(out=vn32, in0=vg, scalar1=stt[:, c0 + 6:c0 + 7], scalar2=stt[:, c0 + 9:c0 + 10], op0=ALU.subtract, op1=ALU.mult): e.tensor_scalar(**_kw)),
              [vgk, "stt"], [vnk])
            V((lambda e, _kw=dict(out=vn32, in0=vn32, in1=vbc[:, VB_LNG:VB_LNG + 256], op=ALU.mult): e.tensor_tensor(**_kw)), [vnk, "vbc"], [vnk])
            V((lambda e, _kw=dict(out=vn32, in0=vn32, in1=vbc[:, VB_LNB:VB_LNB + 256], op=ALU.add): e.tensor_tensor(**_kw)), [vnk, "vbc"], [vnk])

        def to_vnz(vn32, vnk, vnz):
            v4 = vn32.rearrange("p (a two c) -> p a two c", two=2, c=64)
            z4 = vnz.rearrange("p (a two) c -> p a two c", two=2)
            G((lambda e, _kw=dict(out=z4[:, :, 0, 0:64], in_=v4[:, :, 0, :]): e.tensor_copy(**_kw)), [vnk], ["vnz"])
            G((lambda e, _kw=dict(out=z4[:, :, 1, 64:128], in_=v4[:, :, 1, :]): e.tensor_copy(**_kw)), [vnk], ["vnz"])

        def conv_ln_silu(cy, W, ybf, ysq, mean, m2, rstd, mixT, mkey, mcol):
            G((lambda e, _kw=dict(out=ybf, in_=cy): e.tensor_copy(**_kw)), ["cy"], ["ybf"])
            G((lambda e, _kw=dict(out=ysq, in0=cy, in1=cy, op=ALU.mult): e.tensor_tensor(**_kw)), ["cy"], ["ysq"])
            b = gb()
            one = cstb[:, B_ONE:B_ONE + 128]
            for ch in range(2):
                MM(ps[b][:, 0:W], one, ybf[:, ch, :], ch == 0, ch == 1, ["cstb", "ybf"], [f"ps{b}"], False)
            for ch in range(2):
                MM(ps[b][:, 256:256 + W], one, ysq[:, ch, :], ch == 0, ch == 1, ["cstb", "ysq"], [f"ps{b}"], ch == 1)
            A((lambda e, _kw=dict(out=mean, in_=ps[b][:, 0:W]): e.copy(**_kw)), [f"ps{b}"], ["cmean"])
            G((lambda e, _kw=dict(out=m2, in0=mean, in1=mean, op=ALU.mult): e.tensor_tensor(**_kw)), ["cmean"], ["cm2"])
            V((lambda e, _kw=dict(out=m2, in0=ps[b][:, 256:256 + W], in1=m2, op=ALU.subtract): e.tensor_tensor(**_kw)), [f"ps{b}", "cm2"], ["cm2"])
            A((lambda e, _kw=dict(out=rstd, in_=m2, func=AF.Sqrt, scale=1.0, bias=LN_EPS): e.activation(**_kw)), ["cm2"], ["crstd"])
            V((lambda e, _kw=dict(out=rstd, in_=rstd): e.reciprocal(**_kw)), ["crstd"], ["crstd"])
            V((lambda e, _kw=dict(out=cy, in0=cy, in1=mean.unsqueeze(1).to_broadcast([128, 2, W]), op=ALU.subtract): e.tensor_tensor(**_kw)), ["cy", "cmean"], ["cy"])
            V((lambda e, _kw=dict(out=cy, in0=cy, in1=rstd.unsqueeze(1).to_broadcast([128, 2, W]), op=ALU.mult): e.tensor_tensor(**_kw)), ["cy", "crstd"], ["cy"])
            for ch in range(2):
                A((lambda e, _kw=dict(out=mixT[:, ch, mcol:mcol + W], in_=cy[:, ch, :], func=AF.Silu, scale=vecfm[:, ch, 1:2], bias=vecfm[:, ch, 2:3]): e.activation(**_kw)),
                  ["cy", "vecfm"], [mkey])

        def attn_finish(o0, l0, o1, l1, rk, obuf_h, okey):
            c0 = sc(3)
            V((lambda e, _kw=dict(out=stt[:, c0:c0 + 1], in_=l0): e.reciprocal(**_kw)), rk, ["stt"])
            V((lambda e, _kw=dict(out=stt[:, c0 + 1:c0 + 2], in_=l1): e.reciprocal(**_kw)), rk, ["stt"])
            V((lambda e, _kw=dict(out=stt[:, c0 + 2:c0 + 3], in0=stt[:, c0 + 1:c0 + 2], in1=nlam[:, 0:1], op=ALU.mult): e.tensor_tensor(**_kw)), ["stt", "nlam"], ["stt"])
            V((lambda e, _kw=dict(out=obuf_h, in0=o0, scalar1=stt[:, c0:c0 + 1], scalar2=None, op0=ALU.mult): e.tensor_scalar(**_kw)), rk + ["stt"], [okey])
            V((lambda e, _kw=dict(out=obuf_h, in0=o1, scalar=stt[:, c0 + 2:c0 + 3], in1=obuf_h, op0=ALU.mult, op1=ALU.add): e.scalar_tensor_tensor(**_kw)), rk + ["stt", okey], [okey])

        def head_norm_to_mix(ob, okey, osq, ycb, lam_init, mixT, mkey, mcol):
            c0 = sc(8)
            o3 = ob.rearrange("p (h e) -> p h e", h=4)
            G((lambda e, _kw=dict(out=osq, in0=ob, in1=ob, op=ALU.mult): e.tensor_tensor(**_kw)), [okey], ["osq"])
            V((lambda e, _kw=dict(out=stt[:, c0:c0 + 4], in_=osq.rearrange("p (h e) -> p h e", h=4), axis=AX.X, op=ALU.add): e.tensor_reduce(**_kw)), ["osq"], ["stt"])
            A((lambda e, _kw=dict(out=stt[:, c0 + 4:c0 + 8], in_=stt[:, c0:c0 + 4], func=AF.Sqrt, scale=1.0 / 64, bias=LN_EPS): e.activation(**_kw)), ["stt"], ["stt"])
            V((lambda e, _kw=dict(out=stt[:, c0:c0 + 4], in_=stt[:, c0 + 4:c0 + 8]): e.reciprocal(**_kw)), ["stt"], ["stt"])
            V((lambda e, _kw=dict(out=o3, in0=o3, scalar=lamc[:, 1:2], in1=stt[:, c0:c0 + 4].unsqueeze(2).to_broadcast([128, 4, 64]),
                                               op0=ALU.mult, op1=ALU.mult): e.scalar_tensor_tensor(**_kw)), [okey, "stt", "lamc"], [okey])
            V((lambda e, _kw=dict(out=ycb.rearrange("p (h e) -> p h e", h=4), in0=o3,
                                        in1=vbc[:, VB_GSUB:VB_GSUB + 64].unsqueeze(1).to_broadcast([128, 4, 64]), op=ALU.mult): e.tensor_tensor(**_kw)), [okey, "vbc"], ["ycb"])
            b = gb()
            for ch in range(2):
                TR(psb[b][:, ch * 128:(ch + 1) * 128], ycb[:, ch * 128:(ch + 1) * 128], idb, ["ycb", "cstb"], [f"ps{b}"], ch == 1)
            A((lambda e, _kw=dict(out=mixT[:, 4:6, mcol:mcol + 128], in_=psb[b][:, 0:256].rearrange("p (c t) -> p c t", c=2)): e.copy(**_kw)), [f"ps{b}"], [mkey])

        def out_proj_tile(mixT, mkeys, mcol, xap, xkey):
            bA, bB = gb(), gb()
            for n, bk in enumerate((bA, bB)):
                for kc in range(8):
                    MM(ps[bk][:], mixT[:, kc, mcol:mcol + 128], w_out_sb[:, kc, n * 512:(n + 1) * 512], kc == 0, kc == 7, mkeys + ["w_out"], [f"ps{bk}"], kc == 7)
            postnorm_add(bA, bB, xap, xkey)

        ARK = set()

        def phase_barrier(extra=()):
            P.barrier(sorted(set(P.buf.keys()) | ARK | set(extra)))

        for l in ([0] if doB else []):
            lam_init = None
            P.scope = "loads"
            phase_barrier(["w_in", "w_out", "w_down"])
            TA.reset()
            w_in_v = w_in_d[l].rearrange("(kc p) n -> p kc n", p=128)
            for hf in range(2):
                P.dma("pool", (lambda hf: (lambda e, _kw=dict(out=w_in_sb[:, hf * 4:(hf + 1) * 4, :], in_=w_in_v[:, hf * 4:(hf + 1) * 4, :]): e.dma_start(**_kw)))(hf),
                      writes=["w_in"], sem="w_in")
            P.dma("pool", (lambda e, _kw=dict(out=w_out_sb, in_=w_out_d[l].rearrange("(kc p) n -> p kc n", p=128)): e.dma_start(**_kw)), writes=["w_out"], sem="w_out")
            P.dma("sp", (lambda e, _kw=dict(out=vecfm[:], in_=vecfm_d[l]): e.dma_start(**_kw)), writes=["vecfm"])
            P.dma("sp", (lambda e, _kw=dict(out=gfm[:], in_=gfm_d[l]): e.dma_start(**_kw)), writes=["gfm"])
            P.dma("sp", (lambda e, _kw=dict(out=vbc[:], in_=vbc_d[l].partition_broadcast(128)): e.dma_start(**_kw)), writes=["vbc"])
            P.dma("sp", (lambda e, _kw=dict(out=gpost[:], in_=gpost_d[l, 0].partition_broadcast(128)): e.dma_start(**_kw)), writes=["gpost"])
            sguw_st = TA.get("sguw_st", (4, 128), F32)
            P.dma("sp", (lambda e, _kw=dict(out=sguw_st, in_=sguw_d[l].rearrange("g t s -> t g s")): e.dma_start(**_kw)), writes=["sguw_st"])
            for g in range(4):
                P.dma("sp", (lambda g: (lambda e, _kw=dict(out=BsT[(g % 2) * 64:(g % 2) * 64 + 64, g // 2, :], in_=sgub_d[l, g:g + 1, :].partition_broadcast(64)): e.dma_start(**_kw)))(g),
                      writes=["BsT"], sem="BsT")
                P.dma("sp", (lambda g: (lambda e, _kw=dict(out=Wpst[(g % 2) * 64:(g % 2) * 64 + 64, g // 2, :], in_=poolw_d[l, g]): e.dma_start(**_kw)))(g),
                      writes=["Wpst"], sem="Wpst")
            G((lambda e, _kw=dict(out=Wp[0:64, :, 0:64], in_=Wpst[0:64, :, :]): e.tensor_copy(**_kw)), ["Wpst"], ["Wp"])
            G((lambda e, _kw=dict(out=Wp[64:128, :, 64:128], in_=Wpst[64:128, :, :]): e.tensor_copy(**_kw)), ["Wpst"], ["Wp"])
            c0 = sc(6)
            ltmp = TA.get("ltmp", (64,), F32)
            V((lambda e, _kw=dict(out=ltmp[:, 0:32], in0=vbc[:, VB_LQ1:VB_LQ1 + 32], in1=vbc[:, VB_LK1:VB_LK1 + 32], op=ALU.mult): e.tensor_tensor(**_kw)), ["vbc"], ["ltmp"])
            V((lambda e, _kw=dict(out=ltmp[:, 32:64], in0=vbc[:, VB_LQ2:VB_LQ2 + 32], in1=vbc[:, VB_LK2:VB_LK2 + 32], op=ALU.mult): e.tensor_tensor(**_kw)), ["vbc"], ["ltmp"])
            V((lambda e, _kw=dict(out=stt[:, c0:c0 + 2], in_=ltmp.rearrange("p (a b) -> p a b", a=2), axis=AX.X, op=ALU.add): e.tensor_reduce(**_kw)), ["ltmp"], ["stt"])
            A((lambda e, _kw=dict(out=stt[:, c0 + 2:c0 + 4], in_=stt[:, c0:c0 + 2], func=AF.Exp): e.activation(**_kw)), ["stt"], ["stt"])
            V((lambda e, _kw=dict(out=stt[:, c0 + 4:c0 + 5], in0=stt[:, c0 + 3:c0 + 4], in1=stt[:, c0 + 2:c0 + 3], op=ALU.subtract): e.tensor_tensor(**_kw)), ["stt"], ["stt"])
            V((lambda e, _kw=dict(out=nlam[:, 0:1], in0=stt[:, c0 + 4:c0 + 5], scalar1=lamc[:, 0:1], scalar2=None, op0=ALU.add): e.tensor_scalar(**_kw)), ["stt", "lamc"], ["nlam"])
            b = gb()
            for g in range(4):
                TR(ps[b][:, g * 128:(g + 1) * 128], sguw_st[:, g, :], idf, ["sguw_st", "cst"], [f"ps{b}"], g == 3)
            V((lambda e, _kw=dict(out=WsT[:], in0=ps[b][:].rearrange("p (g t) -> p g t", g=4),
                                        in1=cst[:, C_TRI:C_TRI + 128].unsqueeze(1).to_broadcast([128, 4, 128]), op=ALU.mult): e.tensor_tensor(**_kw)), [f"ps{b}", "cst"], ["WsT"])
            sel = cstb[0:4, B_SEL:B_SEL + 128]
            b = gb()
            for g in range(4):
                MM(ps[b][0:4, g * 128:(g + 1) * 128], WsT[0:4, g, 0:4], sel, True, True, ["WsT", "cstb"], [f"ps{b}"], g == 3)
            asb = TA.get("asb", (4, 128), BF16)
            A((lambda e, _kw=dict(out=asb[0:4], in_=ps[b][0:4, :].rearrange("p (g t) -> p g t", g=4)): e.copy(**_kw)), [f"ps{b}"], ["asb"])
            b2 = gb()
            for g in range(4):
                MM(ps[b2][:, g * 128:(g + 1) * 128], asb[0:4, g, :], sel, True, True, ["asb", "cstb"], [f"ps{b2}"], g == 3)
            V((lambda e, _kw=dict(out=WsS[:], in0=ps[b2][:].rearrange("p (g t) -> p g t", g=4),
                                        in1=cst[:, C_BLK:C_BLK + 128].unsqueeze(1).to_broadcast([128, 4, 128]), op=ALU.mult): e.tensor_tensor(**_kw)), [f"ps{b2}", "cst"], ["WsS"])
            ARK.update(TA.keys)

            P.scope = "PM"
            phase_barrier()
            TA.reset()
            W = 256
            hT = TA.get("hT", (8, 512), BF16)
            kT_all = TA.get("kT_all", (2, S), BF16)
            V1 = TA.get("V1", (NT, 4, 65), BF16)
            Qbd = TA.get("Qbd", (2, 512), BF16)
            mixT = TA.get("mixT", (8, W), BF16)
            NPT = 3
            pT = [TA.get(f"pT{i}", (512,), BF16) for i in range(NPT)]
            hconv = TA.get("hconv", (2, 30 + W), F32)
            cy = TA.get("cy", (2, W), F32)
            ybf = TA.get("ybf", (2, W), BF16)
            ysq = TA.get("ysq", (2, W), BF16)
            cmean = TA.get("cmean", (W,), F32)
            cm2 = TA.get("cm2", (W,), F32)
            crstd = TA.get("crstd", (W,), F32)
            sg = TA.get("sg", (2, W), F32)
            uT = TA.get("uT", (2, W), BF16)
            vg = TA.get("vg", (256,), F32)
            vn32 = TA.get("vn32", (256,), F32)
            vnz = TA.get("vnz", (4, 128), BF16)
            sgt = TA.get("sgt", (2, 128), F32)
            xdT = TA.get("xdT", (2, 16 + W), F32)
            pA = TA.get("pA", (2, 16 + W), F32)
            pB = TA.get("pB", (2, 16 + W), F32)
            dT = TA.get("dT", (2, W), BF16)
            kvo = TA.get("kvo", (512,), F32)
            obuf = TA.get("obuf", (2, 256), F32)
            osq = TA.get("osq", (256,), F32)
            ycb = TA.get("ycb", (256,), BF16)
            tm1 = TA.get("tm1", (256,), F32)
            tm2 = TA.get("tm2", (256,), F32)
            ARK.update(TA.keys)
            G((lambda e, _kw=dict(ap=V1, constant=1.0): e.memset(**_kw)), [], ["V1"])
            G((lambda e, _kw=dict(ap=vnz, constant=0.0): e.memset(**_kw)), [], ["vnz"])
            G((lambda e, _kw=dict(ap=hconv[:, :, 0:30], constant=0.0): e.memset(**_kw)), [], ["hconv"])
            G((lambda e, _kw=dict(ap=xdT[:, :, 0:15], constant=0.0): e.memset(**_kw)), [], ["xdT"])

            def pm_prenorm(blk):
                off = (blk % 2) * 256
                for tl in range(2):
                    prenorm_tile(x[:, 2 * blk + tl, :], f"x{2 * blk + tl}", 0, hT, f"hT{blk % 2}", off + tl * 128)

            F_HOIST, F_SKEW, F_SIDE = cfg.get("hoist", 1), cfg.get("skew", 1), cfg.get("side", 0)
            if F_HOIST:
                pm_prenorm(0)
            for blk in range(NB):
                t0 = 2 * blk
                hoff = (blk % 2) * 256
                hTb = hT[:, :, hoff:hoff + 256]
                hk = f"hT{blk % 2}"
                if not F_HOIST:
                    pm_prenorm(blk)
                for ch in range(2):
                    fm_chunk(256 + ch * 128, hTb, hk, W,
                             lambda b, ch=ch: A((lambda e, _kw=dict(out=sg[:, ch, :], in_=ps[b][:, 0:W], func=AF.Sigmoid): e.activation(**_kw)), [f"ps{b}"], ["sg"]))
                for ch in range(2):
                    fm_chunk(ch * 128, hTb, hk, W,
                             lambda b, ch=ch: V((lambda e, _kw=dict(out=hconv[:, ch, 30:30 + W], in0=ps[b][:, 0:W], in1=sg[:, ch, :], op=ALU.mult): e.tensor_tensor(**_kw)),
                                                [f"ps{b}", "sg"], ["hconv"]))
                for ch in range(2):
                    fm_chunk(512 + ch * 128, hTb, hk, W,
                             lambda b, ch=ch: A((lambda e, _kw=dict(out=uT[:, ch, :], in_=ps[b][:, 0:W], func=AF.Gelu): e.activation(**_kw)), [f"ps{b}"], ["uT"]))
                for chk in range(2):
                    def ev_qp(b, chk=chk):
                        for c in range(2):
                            V((lambda e, _kw=dict(out=Qbd[:, chk, c * 256:(c + 1) * 256], in0=ps[b][:, 0:W], scalar1=cst[:, C_MC + c:C_MC + c + 1],
                                                             scalar2=None, op0=ALU.mult): e.tensor_scalar(**_kw)), [f"ps{b}", "cst"], ["Qbd"])
                    fm_chunk(1024 + chk * 128, hTb, hk, W, ev_qp)
                for ch in range(2):
                    fm_chunk(1280 + ch * 128, hTb, hk, W,
                             lambda b, ch=ch: A((lambda e, _kw=dict(out=kT_all[:, ch, blk * W:(blk + 1) * W], in_=ps[b][:, 0:W]): e.copy(**_kw)), [f"ps{b}"], ["kT_all"]))
                for ch in range(2):
                    fm_chunk(1792 + ch * 128, hTb, hk, W,
                             lambda b, ch=ch: A((lambda e, _kw=dict(out=xdT[:, ch, 15:15 + W], in_=ps[b][:, 0:W]): e.copy(**_kw)), [f"ps{b}"], ["xdT"]))
                for tl in range(2):
                    tg = t0 + tl
                    b = tm_cols(1280, 512, hT, hk, hoff + tl * 128)
                    V((lambda e, _kw=dict(out=kvo, in_=ps[b][:]): e.tensor_copy(**_kw)), [f"ps{b}"], ["kvo"])
                    P.dma("sp", (lambda tg: (lambda e, _kw=dict(out=o_kp[l, tg * 128:(tg + 1) * 128, :], in_=kvo[:, 0:256]): e.dma_start(**_kw)))(tg), reads=["kvo"], writes=["o_kp"], sem="st_kvo")
                    P.dma("sp", (lambda tg: (lambda e, _kw=dict(out=o_vp[l, tg * 128:(tg + 1) * 128, :], in_=kvo[:, 256:512]): e.dma_start(**_kw)))(tg), reads=["kvo"], writes=["o_vp"], sem="st_kvo")
                    G((lambda e, _kw=dict(out=V1[:, tg, :, 0:64], in_=kvo[:, 256:512].rearrange("p (h e) -> p h e", h=4)): e.tensor_copy(**_kw)), ["kvo"], ["V1"])
                    b = tm_cols(768, 256, hT, hk, hoff + tl * 128)
                    A((lambda e, _kw=dict(out=vg, in_=ps[b][:, 0:256], func=AF.Gelu): e.activation(**_kw)), [f"ps{b}"], ["vg"])
                    ln_rows(vg, "vg", vn32, "vn32")
                    to_vnz(vn32, "vn32", vnz)
                    b = gb()
                    for ch in range(2):
                        for j in range(2):
                            MM(ps[b][:, ch * 128:(ch + 1) * 128], vnz[:, 2 * ch + j, :], WsT[:, 2 * ch + j, :], j == 0, j == 1, ["vnz", "WsT"], [f"ps{b}"], ch == 1 and j == 1)
                    V((lambda e, _kw=dict(out=sgt, in0=ps[b][:, 0:256].rearrange("p (c t) -> p c t", c=2), in1=BsT[:], op=ALU.add): e.tensor_tensor(**_kw)), [f"ps{b}", "BsT"], ["sgt"])
                    V((lambda e, _kw=dict(out=mixT[:, 2:4, tl * 128:(tl + 1) * 128], in0=sgt, in1=uT[:, :, tl * 128:(tl + 1) * 128], op=ALU.mult): e.tensor_tensor(**_kw)),
                      ["sgt", "uT"], ["mixB"])
                    if tg == NT - 1:
                        b = tm_cols(0, 512, hT, hk, hoff + tl * 128)
                        A((lambda e, _kw=dict(out=tm1, in_=ps[b][:, 256:512], func=AF.Sigmoid): e.activation(**_kw)), [f"ps{b}"], ["tm1"])
                        V((lambda e, _kw=dict(out=tm1, in0=ps[b][:, 0:256], in1=tm1, op=ALU.mult): e.tensor_tensor(**_kw)), [f"ps{b}", "tm1"], ["tm1"])
                        P.dma("sp", (lambda e, _kw=dict(out=o_cp[l], in_=tm1[98:128, :]): e.dma_start(**_kw)), reads=["tm1"], writes=["o_cp"], sem="st_tm1")
                        b = tm_cols(1792, 256, hT, hk, hoff + tl * 128)
                        A((lambda e, _kw=dict(out=tm2, in_=ps[b][:, 0:256]): e.copy(**_kw)), [f"ps{b}"], ["tm2"])
                        P.dma("sp", (lambda e, _kw=dict(out=o_pp[l], in_=tm2[113:128, :]): e.dma_start(**_kw)), reads=["tm2"], writes=["o_pp"], sem="st_tm2")
                if F_HOIST and blk + 1 < NB:
                    pm_prenorm(blk + 1)

                def conv_taps(j0, j1):
                    for j in range(j0, j1):
                        for ch in range(2):
                            if j == 0:
                                V((lambda e, _kw=dict(out=cy[:, ch, :], in0=hconv[:, ch, 0:W], scalar1=vecfm[:, ch, 4:5], scalar2=vecfm[:, ch, 0:1],
                                                                   op0=ALU.mult, op1=ALU.add): e.tensor_scalar(**_kw)), ["hconv", "vecfm"], [f"cy{ch}", "cy"])
                            else:
                                V((lambda e, _kw=dict(out=cy[:, ch, :], in0=hconv[:, ch, j:j + W], scalar=vecfm[:, ch, 4 + j:5 + j], in1=cy[:, ch, :],
                                                                               op0=ALU.mult, op1=ALU.add): e.scalar_tensor_tensor(**_kw)), ["hconv", "vecfm", f"cy{ch}"], [f"cy{ch}"])

                def pool_dve(blk=blk):
                    V((lambda e, _kw=dict(out=pA[:, :, 1:15 + W], in0=xdT[:, :, 1:15 + W], in1=xdT[:, :, 0:14 + W], op=ALU.add): e.tensor_tensor(**_kw)), ["xdT"], ["pA"])
                    V((lambda e, _kw=dict(out=pB[:, :, 3:15 + W], in0=pA[:, :, 3:15 + W], in1=pA[:, :, 1:13 + W], op=ALU.add): e.tensor_tensor(**_kw)), ["pA"], ["pB"])
                    V((lambda e, _kw=dict(out=pA[:, 1, 7:15 + W], in0=pB[:, 1, 7:15 + W], in1=pB[:, 1, 3:11 + W], op=ALU.add): e.tensor_tensor(**_kw)), ["pB", "pA"], ["pA"])
                    V((lambda e, _kw=dict(out=pB[64:128, 1, 15:15 + W], in0=pA[64:128, 1, 15:15 + W], in1=pA[64:128, 1, 7:7 + W], op=ALU.add): e.tensor_tensor(**_kw)), ["pA", "pB"], ["pB"])
                    for ch in range(2):
                        for hf, src in ((0, pA), (1, pB)):
                            lo, hi = hf * 64, hf * 64 + 64
                            V((lambda e, _kw=dict(out=dT[lo:hi, ch, :], in0=src[lo:hi, ch, 15:15 + W], scalar=cst[lo:hi, C_INVW + ch:C_INVW + ch + 1],
                                                                                             in1=xdT[lo:hi, ch, 15:15 + W], op0=ALU.mult, op1=ALU.subtract): e.scalar_tensor_tensor(**_kw)), ["pA", "pB", "xdT", "cst"], ["dT"])
                            if blk == 0:
                                c16 = sc(16)
                                V((lambda e, _kw=dict(out=stt[lo:hi, c16:c16 + 16], in0=src[lo:hi, ch, 15:31],
                                                                                                   in1=cst[lo:hi, C_ICNT + ch * 16:C_ICNT + ch * 16 + 16], op=ALU.mult): e.tensor_tensor(**_kw)), ["pA", "pB", "cst"], ["stt"])
                                V((lambda e, _kw=dict(out=dT[lo:hi, ch, 0:16], in0=stt[lo:hi, c16:c16 + 16], in1=xdT[lo:hi, ch, 15:31], op=ALU.subtract): e.tensor_tensor(**_kw)),
                                  ["stt", "xdT", "dT"], ["dT"])
                    G((lambda e, _kw=dict(out=xdT[:, :, 0:15], in_=xdT[:, :, W:W + 15]): e.tensor_copy(**_kw)), ["xdT", "pA", "pB"], ["xdT"])

                def pool_mm():
                    for ch in range(2):
                        b = gb()
                        MM(ps[b][:, 0:W], Wp[:, ch, :], dT[:, ch, :], True, True, ["Wp", "dT"], [f"ps{b}"], True)
                        V((lambda e, _kw=dict(out=mixT[:, 6 + ch, :], in0=ps[b][:, 0:W], scalar1=vecfm[:, ch, 3:4], scalar2=None, op0=ALU.mult): e.tensor_scalar(**_kw)),
                          [f"ps{b}", "vecfm"], ["mixD"])

                def conv_finish():
                    G((lambda e, _kw=dict(out=hconv[:, :, 0:30], in_=hconv[:, :, W:W + 30]): e.tensor_copy(**_kw)), ["hconv"], ["hconv"])
                    P.op("pool", lambda e: e.nop(), ["cy0", "cy1"], ["cy"])
                    conv_ln_silu(cy, W, ybf, ysq, cmean, cm2, crstd, mixT, "mixA", 0)

                side = {0: [lambda: conv_taps(0, 10)], 1: [pool_dve, lambda: conv_taps(10, 17)], 2: [pool_mm, lambda: conv_taps(17, 24)],
                        3: [lambda: conv_taps(24, 31), conv_finish]}

                qb0, qb1 = 2 * blk, 2 * blk + 1
                items = [(h, kb) for h in range(4) for kb in range(qb1 + 1)]
                pinfo = {}

                def att_s(n, h, kb):
                    r0 = (h % 2) * 64
                    b = gb()
                    MM(ps[b][:], kT_all[r0:r0 + 64, h // 2, kb * 128:(kb + 1) * 128], Qbd[r0:r0 + 64, h // 2, :], True, True, ["kT_all", "Qbd"], [f"ps{b}"], True)
                    pi = n % NPT
                    pinfo[n] = pi
                    dd = qb1 - kb
                    A((lambda e, _kw=dict(out=pT[pi], in_=ps[b][:], func=AF.Exp, scale=ISQ, bias=cst[:, C_BT + h * 16 + dd:C_BT + h * 16 + dd + 1]): e.activation(**_kw)),
                      [f"ps{b}", "cst"], [f"pT{pi}"])
                    if kb >= qb0:
                        mo = B_MA if kb == qb0 else B_MB
                        G((lambda e, _kw=dict(out=pT[pi], in0=pT[pi], in1=cstb[:, mo:mo + 512], op=ALU.mult): e.tensor_tensor(**_kw)), [f"pT{pi}", "cstb"], [f"pT{pi}"])

                def att_pv(n, h, kb):
                    bo = 6 + (h % 2)
                    pi = pinfo[n]
                    if kb == 0:
                        MM(ps[bo][:], zer[:, 0:128], zer[:], True, True, ["zer"], [f"ps{bo}"], False)
                    for c in range(2):
                        for qi in range(2):
                            if kb <= qb0 + qi:
                                a_ = c * 2 + qi
                                last = (kb == qb0 + qi)
                                MM(ps[bo][:, a_ * 128:a_ * 128 + 65], pT[pi][:, c * 256 + qi * 128:c * 256 + qi * 128 + 128], V1[:, kb, h, :], False, last,
                                   [f"pT{pi}", "V1"], [f"ps{bo}"], last and c == 1)
                    if kb == qb1:
                        for fn_ in side[h]:
                            fn_()
                        for qi in range(2):
                            a0, a1 = qi, 2 + qi
                            attn_finish(ps[bo][:, a0 * 128:a0 * 128 + 64], ps[bo][:, a0 * 128 + 64:a0 * 128 + 65],
                                        ps[bo][:, a1 * 128:a1 * 128 + 64], ps[bo][:, a1 * 128 + 64:a1 * 128 + 65], [f"ps{bo}"],
                                        obuf[:, qi, h * 64:(h + 1) * 64], f"obuf{qi}")

                post_side = []
                if not F_SIDE:
                    for h_ in range(4):
                        for fn_ in side[h_]:
                            fn_()
                        side[h_] = []
                elif F_SIDE == 2:
                    side = {0: [lambda: conv_taps(0, 10)], 1: [pool_dve, lambda: conv_taps(10, 17)], 2: [lambda: conv_taps(17, 24)],
                            3: [lambda: conv_taps(24, 31)]}
                    post_side = [pool_mm, conv_finish]
                if F_SKEW:
                    for n in range(len(items) + 1):
                        if n < len(items):
                            att_s(n, *items[n])
                        if n >= 1:
                            att_pv(n - 1, *items[n - 1])
                else:
                    for n in range(len(items)):
                        att_s(n, *items[n])
                        att_pv(n, *items[n])
                for fn_ in post_side:
                    fn_()
                for qi in range(2):
                    head_norm_to_mix(obuf[:, qi, :], f"obuf{qi}", osq, ycb, lam_init, mixT, "mixC", qi * 128)
                for tl in range(2):
                    out_proj_tile(mixT, ["mixA", "mixB", "mixC", "mixD"], tl * 128, x[:, t0 + tl, :], f"x{t0 + tl}")

            P.scope = "S2"
            phase_barrier()
            TA.reset()
            W = 128
            hTs = TA.get("hT", (8, 128), BF16)
            mixT = TA.get("mixT", (8, W), BF16)
            sg = TA.get("sg", (2, W), F32)
            hs32 = TA.get("hs32", (2, 32, 34), F32)
            cy = TA.get("cy", (2, W), F32)
            ybf = TA.get("ybf", (2, W), BF16)
            ysq = TA.get("ysq", (2, W), BF16)
            cmean = TA.get("cmean", (W,), F32)
            cm2 = TA.get("cm2", (W,), F32)
            crstd = TA.get("crstd", (W,), F32)
            uT = TA.get("uT", (2, W), BF16)
            vg = TA.get("vg", (256,), F32)
            vn32 = TA.get("vn32", (256,), F32)
            vnz = TA.get("vnz", (4, 128), BF16)
            sgt = TA.get("sgt", (2, 128), F32)
            xds = TA.get("xds", (2, 32, 19), F32)
            sA = TA.get("pA", (2, 32, 19), F32)
            sB = TA.get("pB", (2, 32, 19), F32)
            dT = TA.get("dT", (2, W), BF16)
            R = TA.get("R", (8, 65), F32)
            R8 = TA.get("R8", (NCORES, 8 * 65), F32)
            obuf = TA.get("obuf", (256,), F32)
            osq = TA.get("osq", (256,), F32)
            ycb = TA.get("ycb", (256,), BF16)
            tm1 = TA.get("tm1", (256,), F32)
            tm2 = TA.get("tm2", (256,), F32)
            ARK.update(TA.keys)
            G((lambda e, _kw=dict(ap=vnz, constant=0.0): e.memset(**_kw)), [], ["vnz"])
            for ch in range(2):
                P.dma("sp", (lambda ch: (lambda e, _kw=dict(out=hs32[:, ch, :, 0:30], in_=sconv_fm_d[l, ch * 128:(ch + 1) * 128]): e.dma_start(**_kw)))(ch), writes=["hs32"], sem="hs32")
                P.dma("sp", (lambda ch: (lambda e, _kw=dict(out=xds[:, ch, :, 0:15], in_=spool_fm_d[l, ch * 128:(ch + 1) * 128]): e.dma_start(**_kw)))(ch), writes=["xds"], sem="xds")
            P.dma("sp", (lambda e, _kw=dict(out=o_cs[l, :, 0:26, :], in_=sconv_raw_d[l, :, 4:30, :]): e.dma_start(**_kw)), writes=["o_cs"], sem="st_d2d")
            P.dma("sp", (lambda e, _kw=dict(out=o_ps[l, :, 0:11, :], in_=spool_raw_d[l, :, 4:15, :]): e.dma_start(**_kw)), writes=["o_ps"], sem="st_d2d")
            pv_ = part_d.rearrange("c (b t a) e -> b t c (a e)", t=4, a=8)
            for t in range(4):
                P.dma("sp", (lambda t: (lambda e, _kw=dict(out=R8[32 * t:32 * t + 32, :, :], in_=pv_[:, t, :, :]): e.dma_start(**_kw)))(t),
                      writes=["R8"], sem="R8")
            Rf = R.rearrange("p a e -> p (a e)")
            V((lambda e, _kw=dict(out=Rf, in0=R8[:, 0, :], in1=R8[:, 1, :], op=ALU.add): e.tensor_tensor(**_kw)), ["R8"], ["R"])
            for c_ in range(2, NCORES):
                V((lambda e, _kw=dict(out=Rf, in0=Rf, in1=R8[:, c_, :], op=ALU.add): e.tensor_tensor(**_kw)), ["R8", "R"], ["R"])
            prenorm_tile(xs[:], "xs", 0, hTs, "hT", 0)
            bt = lambda ap: ap.rearrange("p (t b) -> p b t", t=4)
            for ch in range(2):
                fm_chunk(256 + ch * 128, hTs, "hT", W,
                         lambda b, ch=ch: A((lambda e, _kw=dict(out=sg[:, ch, :], in_=ps[b][:, 0:W], func=AF.Sigmoid): e.activation(**_kw)), [f"ps{b}"], ["sg"]))
            for ch in range(2):
                fm_chunk(ch * 128, hTs, "hT", W,
                         lambda b, ch=ch: V((lambda e, _kw=dict(out=hs32[:, ch, :, 30:34], in0=bt(ps[b][:, 0:W]), in1=bt(sg[:, ch, :]), op=ALU.mult): e.tensor_tensor(**_kw)),
                                            [f"ps{b}", "sg"], ["hs32"]))
            for ch in range(2):
                fm_chunk(512 + ch * 128, hTs, "hT", W,
                         lambda b, ch=ch: A((lambda e, _kw=dict(out=uT[:, ch, :], in_=ps[b][:, 0:W], func=AF.Gelu): e.activation(**_kw)), [f"ps{b}"], ["uT"]))
            for ch in range(2):
                fm_chunk(1792 + ch * 128, hTs, "hT", W,
                         lambda b, ch=ch: A((lambda e, _kw=dict(out=xds[:, ch, :, 15:19], in_=bt(ps[b][:, 0:W])): e.copy(**_kw)), [f"ps{b}"], ["xds"]))
            b = tm_cols(0, 512, hTs, "hT", 0)
            A((lambda e, _kw=dict(out=tm1, in_=ps[b][:, 256:512], func=AF.Sigmoid): e.activation(**_kw)), [f"ps{b}"], ["tm1"])
            V((lambda e, _kw=dict(out=tm1, in0=ps[b][:, 0:256], in1=tm1, op=ALU.mult): e.tensor_tensor(**_kw)), [f"ps{b}", "tm1"], ["tm1"])
            b = tm_cols(1792, 256, hTs, "hT", 0)
            A((lambda e, _kw=dict(out=tm2, in_=ps[b][:, 0:256]): e.copy(**_kw)), [f"ps{b}"], ["tm2"])
            b = tm_cols(768, 256, hTs, "hT", 0)
            A((lambda e, _kw=dict(out=vg, in_=ps[b][:, 0:256], func=AF.Gelu): e.activation(**_kw)), [f"ps{b}"], ["vg"])
            ln_rows(vg, "vg", vn32, "vn32")
            for t in range(4):
                P.dma("sp", (lambda t: (lambda e, _kw=dict(out=o_cs[l, :, 26 + t, :], in_=tm1[32 * t:32 * t + 32, :]): e.dma_start(**_kw)))(t), reads=["tm1"], writes=["o_cs"], sem="st_tm1")
                P.dma("sp", (lambda t: (lambda e, _kw=dict(out=o_ps[l, :, 11 + t, :], in_=tm2[32 * t:32 * t + 32, :]): e.dma_start(**_kw)))(t), reads=["tm2"], writes=["o_ps"], sem="st_tm2")
            P.dma("sp", (lambda e, _kw=dict(out=o_gs[l], in_=vn32): e.dma_start(**_kw)), reads=["vn32"], writes=["o_gs"], sem="st_vn32")
            to_vnz(vn32, "vn32", vnz)
            b = gb()
            for ch in range(2):
                for j in range(2):
                    MM(ps[b][:, ch * 128:(ch + 1) * 128], vnz[:, 2 * ch + j, :], WsS[:, 2 * ch + j, :], j == 0, j == 1, ["vnz", "WsS"], [f"ps{b}"], ch == 1 and j == 1)
            V((lambda e, _kw=dict(out=sgt.rearrange("p c (t b) -> p c t b", t=4), in0=ps[b][:, 0:256].rearrange("p (c t b) -> p c t b", c=2, t=4),
                                             in1=BsT[:, :, 0:4].unsqueeze(3).to_broadcast([128, 2, 4, 32]), op=ALU.add): e.tensor_tensor(**_kw)), [f"ps{b}", "BsT"], ["sgt"])
            V((lambda e, _kw=dict(out=mixT[:, 2:4, :], in0=sgt, in1=uT, op=ALU.mult): e.tensor_tensor(**_kw)), ["sgt", "uT"], ["mixB"])
            cy4 = [cy[:, ch, :].rearrange("p (t b) -> p b t", t=4) for ch in range(2)]
            for j in range(31):
                for ch in range(2):
                    if j == 0:
                        V((lambda e, _kw=dict(out=cy4[ch], in0=hs32[:, ch, :, 0:4], scalar1=vecfm[:, ch, 4:5], scalar2=vecfm[:, ch, 0:1], op0=ALU.mult, op1=ALU.add): e.tensor_scalar(**_kw)),
                          ["hs32", "vecfm"], [f"cy{ch}", "cy"])
                    else:
                        V((lambda e, _kw=dict(out=cy4[ch], in0=hs32[:, ch, :, j:j + 4], scalar=vecfm[:, ch, 4 + j:5 + j], in1=cy4[ch], op0=ALU.mult, op1=ALU.add): e.scalar_tensor_tensor(**_kw)),
                          ["hs32", "vecfm", f"cy{ch}"], [f"cy{ch}"])
            P.op("pool", lambda e: e.nop(), ["cy0", "cy1"], ["cy"])
            conv_ln_silu(cy, W, ybf, ysq, cmean, cm2, crstd, mixT, "mixA", 0)
            V((lambda e, _kw=dict(out=sA[:, :, :, 1:19], in0=xds[:, :, :, 1:19], in1=xds[:, :, :, 0:18], op=ALU.add): e.tensor_tensor(**_kw)), ["xds"], ["pA"])
            V((lambda e, _kw=dict(out=sB[:, :, :, 3:19], in0=sA[:, :, :, 3:19], in1=sA[:, :, :, 1:17], op=ALU.add): e.tensor_tensor(**_kw)), ["pA"], ["pB"])
            V((lambda e, _kw=dict(out=sA[:, 1, :, 7:19], in0=sB[:, 1, :, 7:19], in1=sB[:, 1, :, 3:15], op=ALU.add): e.tensor_tensor(**_kw)), ["pB", "pA"], ["pA"])
            V((lambda e, _kw=dict(out=sB[64:128, 1, :, 15:19], in0=sA[64:128, 1, :, 15:19], in1=sA[64:128, 1, :, 7:11], op=ALU.add): e.tensor_tensor(**_kw)), ["pA", "pB"], ["pB"])
            for ch in range(2):
                for hf, src in ((0, sA), (1, sB)):
                    lo, hi = hf * 64, hf * 64 + 64
                    V((lambda e, _kw=dict(out=dT[lo:hi, ch, :].rearrange("p (t b) -> p b t", t=4), in0=src[lo:hi, ch, :, 15:19],
                                                                                     scalar=cst[lo:hi, C_INVW + ch:C_INVW + ch + 1], in1=xds[lo:hi, ch, :, 15:19],
                                                                                     op0=ALU.mult, op1=ALU.subtract): e.scalar_tensor_tensor(**_kw)), ["pA", "pB", "xds", "cst"], ["dT"])
            for ch in range(2):
                b = gb()
                MM(ps[b][:, 0:W], Wp[:, ch, :], dT[:, ch, :], True, True, ["Wp", "dT"], [f"ps{b}"], True)
                V((lambda e, _kw=dict(out=mixT[:, 6 + ch, :], in0=ps[b][:, 0:W], scalar1=vecfm[:, ch, 3:4], scalar2=None, op0=ALU.mult): e.tensor_scalar(**_kw)), [f"ps{b}", "vecfm"], ["mixD"])
            for h in range(4):
                attn_finish(R[:, 2 * h, 0:64], R[:, 2 * h, 64:65], R[:, 2 * h + 1, 0:64], R[:, 2 * h + 1, 64:65], ["R"], obuf[:, h * 64:(h + 1) * 64], "obuf0")
            head_norm_to_mix(obuf, "obuf0", osq, ycb, lam_init, mixT, "mixC", 0)
            if DBG and l == 0:
                dbgt = TA.get("dbgt", (8, 128), F32)
                ARK.update(TA.keys)
                G((lambda e, _kw=dict(out=dbgt, in_=mixT): e.tensor_copy(**_kw)), ["mixA", "mixB", "mixC", "mixD"], ["dbgt"])
                P.dma("sp", (lambda e, _kw=dict(out=o_dbg, in_=dbgt): e.dma_start(**_kw)), reads=["dbgt"], writes=["o_dbg"], sem="st_dbg")
            out_proj_tile(mixT, ["mixA", "mixB", "mixC", "mixD"], 0, xs[:], "xs")

            P.scope = "F"
            phase_barrier(["w_in", "w_out", "w_down"])
            TA.reset()
            hT = TA.get("hT", (8, 512), BF16)
            hTs = TA.get("hTs", (8, 128), BF16)
            NW = 3
            wgu = [TA.get(f"wgu{i}", (8, 256), BF16) for i in range(NW)]
            sgb = [TA.get(f"sgb{i}", (512,), F32) for i in range(2)]
            actT = TA.get("actT", (NFC, 512), BF16)
            actTs = TA.get("actTs", (NFC, 128), BF16)
            ARK.update(TA.keys)
            wd_v = w_down_d[l].rearrange("(fc p) n -> p fc n", p=128)
            for hf in range(2):
                P.dma("pool", (lambda hf: (lambda e, _kw=dict(out=w_down_sb[:, hf * 11:(hf + 1) * 11, :], in_=wd_v[:, hf * 11:(hf + 1) * 11, :]): e.dma_start(**_kw)))(hf),
                      writes=["w_down"], sem="w_down")
            P.dma("sp", (lambda e, _kw=dict(out=gpost[:], in_=gpost_d[l, 1].partition_broadcast(128)): e.dma_start(**_kw)), writes=["gpost"])
            prenorm_tile(xs[:], "xs", 1, hTs, "hTs", 0)
            wcount = 0
            for fb in range(NF):
                last_blk = (fb == NF - 1)
                for tl in range(4):
                    prenorm_tile(x[:, 4 * fb + tl, :], f"x{4 * fb + tl}", 1, hT, "hT", tl * 128)
                for f_ in range(NFC):
                    wi = wcount % NW
                    wcount += 1
                    P.dma("pool", (lambda wi, f_: (lambda e, _kw=dict(out=wgu[wi], in_=w_gu_d[l, f_]): e.dma_start(**_kw)))(wi, f_), writes=[f"wgu{wi}"], sem=f"wgu{wi}")
                    bg, bu = gb(), gb()
                    for kc in range(8):
                        MM(ps[bg][:], wgu[wi][:, kc, 0:128], hT[:, kc, :], kc == 0, kc == 7, [f"wgu{wi}", "hT"], [f"ps{bg}"], kc == 7)
                    for kc in range(8):
                        MM(ps[bu][:], wgu[wi][:, kc, 128:256], hT[:, kc, :], kc == 0, kc == 7, [f"wgu{wi}", "hT"], [f"ps{bu}"], kc == 7)
                    si = f_ % 2
                    A((lambda e, _kw=dict(out=sgb[si], in_=ps[bg][:], func=AF.Silu): e.activation(**_kw)), [f"ps{bg}"], [f"sgb{si}"])
                    V((lambda e, _kw=dict(out=actT[:, f_, :], in0=ps[bu][:], in1=sgb[si], op=ALU.mult): e.tensor_tensor(**_kw)), [f"ps{bu}", f"sgb{si}"], ["actT"])
                    if last_blk:
                        bg = gb()
                        for kc in range(8):
                            MM(ps[bg][:, 0:128], wgu[wi][:, kc, 0:128], hTs[:, kc, :], kc == 0, kc == 7, [f"wgu{wi}", "hTs"], [f"ps{bg}"], False)
                        for kc in range(8):
                            MM(ps[bg][:, 128:256], wgu[wi][:, kc, 128:256], hTs[:, kc, :], kc == 0, kc == 7, [f"wgu{wi}", "hTs"], [f"ps{bg}"], kc == 7)
                        A((lambda e, _kw=dict(out=sgb[si][:, 0:128], in_=ps[bg][:, 0:128], func=AF.Silu): e.activation(**_kw)), [f"ps{bg}"], [f"sgb{si}"])
                        V((lambda e, _kw=dict(out=actTs[:, f_, :], in0=ps[bg][:, 128:256], in1=sgb[si][:, 0:128], op=ALU.mult): e.tensor_tensor(**_kw)),
                          [f"ps{bg}", f"sgb{si}"], ["actTs"])

                def down_tile(aT, akey, mcol, xap, xkey):
                    bA, bB = gb(), gb()
                    for n, bk in enumerate((bA, bB)):
                        for f2 in range(NFC):
                            MM(ps[bk][:], aT[:, f2, mcol:mcol + 128], w_down_sb[:, f2, n * 512:(n + 1) * 512], f2 == 0, f2 == NFC - 1, [akey, "w_down"], [f"ps{bk}"], f2 == NFC - 1)
                    postnorm_add(bA, bB, xap, xkey)
                for tl in range(4):
                    down_tile(actT, "actT", tl * 128, x[:, 4 * fb + tl, :], f"x{4 * fb + tl}")
                if last_blk:
                    down_tile(actTs, "actTs", 0, xs[:], "xs")

        if doA:
            P.scope = "S1"
            phase_barrier(["w_in", "w_out", "w_down"])
            P.dma("pool", (lambda e, _kw=dict(out=w_in_sb[:, :, 1024:1792], in_=w_in_a_d.rearrange("(kc p) n -> p kc n", p=128)): e.dma_start(**_kw)), writes=["w_in"], sem="w_in")
            P.dma("sp", (lambda e, _kw=dict(out=gfm[:], in_=gfm_a_d): e.dma_start(**_kw)), writes=["gfm"])
            phase_barrier()
            TA.reset()
            hTs = TA.get("hT", (8, 128), BF16)
            Qblk = TA.get("Qblk", (2, 32, 32), BF16)
            kTs = TA.get("kTs", (2, 128), BF16)
            Vnew = TA.get("Vnew", (258,), BF16)
            kvo = TA.get("kvo", (512,), F32)
            NKV, NKT, NSX = 8, 3, 4
            kvs = [TA.get(f"kvs{i}", (514,), BF16) for i in range(NKV)]
            ktT = [TA.get(f"ktT{i}", (256,), BF16) for i in range(NKT)]
            Sx = [TA.get(f"Sx{i}", (32,), F32) for i in range(NSX)]
            pTs = [TA.get(f"pTs{i}", (32,), BF16) for i in range(NSX)]
            msk = TA.get("msk", (256,), F32)
            stg = [TA.get(f"stg{i}", (66,), F32) for i in range(2)]
            ARK.update(TA.keys)
            G((lambda e, _kw=dict(ap=Qblk, constant=0.0): e.memset(**_kw)), [], ["Qblk"])
            G((lambda e, _kw=dict(ap=Vnew[:, 256:258], constant=0.125): e.memset(**_kw)), [], ["Vnew"])
            for i in range(NKV):
                G((lambda e, _kw=dict(ap=kvs[i][:, 512:514], constant=1.0): e.memset(**_kw)), [], [f"kvs{i}"])
            prenorm_tile(xs[:], "xs", 0, hTs, "hT", 0)
            for chk in range(2):
                def ev_q(b, chk=chk):
                    pv = ps[b][:, 0:128].rearrange("p (t b) -> p b t", t=4)
                    for h2 in range(2):
                        for c in range(2):
                            h = 2 * chk + h2
                            V((lambda e, _kw=dict(out=Qblk[:, chk, :, :].rearrange("p b (t x) -> p b t x", t=4)[:, :, :, h * 2 + c],
                                                                          in0=pv, scalar1=cst[:, C_MH2C + h2 * 2 + c:C_MH2C + h2 * 2 + c + 1], scalar2=None, op0=ALU.mult): e.tensor_scalar(**_kw)),
                              [f"ps{b}", "cst"], ["Qblk"])
                fm_chunk(1024 + chk * 128, hTs, "hT", 128, ev_q)
            for chk in range(2):
                fm_chunk(1280 + chk * 128, hTs, "hT", 128,
                         lambda b, chk=chk: A((lambda e, _kw=dict(out=kTs[:, chk, :], in_=ps[b][:, 0:128]): e.copy(**_kw)), [f"ps{b}"], ["kTs"]))
            b = tm_cols(1280, 512, hTs, "hT", 0)
            V((lambda e, _kw=dict(out=kvo, in_=ps[b][:]): e.tensor_copy(**_kw)), [f"ps{b}"], ["kvo"])
            P.dma("sp", (lambda e, _kw=dict(out=o_ks, in_=kvo[:, 0:256]): e.dma_start(**_kw)), reads=["kvo"], writes=["o_ks"], sem="st_kvo")
            P.dma("sp", (lambda e, _kw=dict(out=o_vs, in_=kvo[:, 256:512]): e.dma_start(**_kw)), reads=["kvo"], writes=["o_vs"], sem="st_kvo")
            G((lambda e, _kw=dict(out=Vnew[:, 0:256], in0=kvo[:, 256:512], scalar1=0.125, scalar2=None, op0=ALU.mult): e.tensor_scalar(**_kw)), ["kvo"], ["Vnew"])
            cin_ap = o_part
            kvl = kv_d
            tiles = []
            tcount = 0
            for bb in range(32):
                for t in range(T8 + 1):
                    td = dict(bb=bb, t=t, new=(t == T8), n=len(tiles))
                    if not td["new"]:
                        td["sl"] = tcount % NKV
                        td["kr"] = tcount % NKT
                        tcount += 1
                    tiles.append(td)

            def stage_a(td):
                if td["new"]:
                    return
                sl_, kr, col = td["sl"], td["kr"], td["bb"] * T8 + td["t"]
                P.dma("pool", (lambda sl_, col: (lambda e, _kw=dict(out=kvs[sl_][:, 0:512], out_offset=None, in_=kvl,
                                                                                 in_offset=bass.IndirectOffsetOnAxis(ap=idx[:, col:col + 1], axis=0)): e.indirect_dma_start(**_kw)))(sl_, col),
                      reads=["idx"], writes=[f"kvs{sl_}"], sem=f"kvs{sl_}")
                b1 = gb()
                for hh in range(2):
                    TR(psb[b1][:, hh * 128:(hh + 1) * 128], kvs[sl_][:, hh * 128:(hh + 1) * 128], idb, [f"kvs{sl_}", "cstb"], [f"ps{b1}"], hh == 1)
                A((lambda e, _kw=dict(out=ktT[kr], in_=psb[b1][:, 0:256]): e.copy(**_kw)), [f"ps{b1}"], [f"ktT{kr}"])

            def stage_b(td):
                bb, t = td["bb"], td["t"]
                if not td["new"]:
                    kr = td["kr"]
                    lh = [ktT[kr][:, 0:128], ktT[kr][:, 128:256]]
                    lk = [f"ktT{kr}"]
                    bias_ap = cst[:, C_BIAS + t * 32:C_BIAS + (t + 1) * 32]
                    bkey = "cst"
                else:
                    lh = [kTs[:, 0, :], kTs[:, 1, :]]
                    lk = ["kTs"]
                    bias_ap = cstb[:, B_BN + bb * 32:B_BN + (bb + 1) * 32]
                    bkey = "cstb"
                b2 = gb()
                for hh in range(2):
                    MM(ps[b2][:, 0:32], lh[hh], Qblk[:, hh, bb, :], hh == 0, hh == 1, lk + ["Qblk"], [f"ps{b2}"], hh == 1)
                si = td["n"] % NSX
                V((lambda e, _kw=dict(out=Sx[si], in0=ps[b2][:, 0:32], scalar=ISQ, in1=bias_ap, op0=ALU.mult, op1=ALU.add): e.scalar_tensor_tensor(**_kw)),
                  [f"ps{b2}", bkey], [f"Sx{si}"])
                A((lambda e, _kw=dict(out=pTs[si], in_=Sx[si], func=AF.Exp): e.activation(**_kw)), [f"Sx{si}"], [f"pTs{si}"])

            def stage_c(td):
                bb, t, new = td["bb"], td["t"], td["new"]
                bo_s = 6 + (bb % 2)
                si = td["n"] % NSX
                if not new:
                    rhs_v = kvs[td["sl"]][:, 256:513]
                    rk = [f"kvs{td['sl']}"]
                else:
                    rhs_v = Vnew[:, 0:257]
                    rk = ["Vnew"]
                MM(ps[bo_s][0:32, 0:257], pTs[si], rhs_v, t == 0, new, [f"pTs{si}"] + rk, [f"ps{bo_s}"], new)
                if not new:
                    return
                sg_i = bb % 2
                V((lambda e, _kw=dict(out=msk[0:32, :], in0=ps[bo_s][0:32, 0:256], in1=cst[0:32, C_HM:C_HM + 256], op=ALU.mult): e.tensor_tensor(**_kw)),
                  [f"ps{bo_s}", "cst"], ["msk"])
                V((lambda e, _kw=dict(out=stg[sg_i][0:32, 0:64], in_=msk[0:32, :].rearrange("p (h e) -> p e h", h=4), axis=AX.X, op=ALU.add): e.tensor_reduce(**_kw)),
                  ["msk"], [f"stg{sg_i}"])
                A((lambda e, _kw=dict(out=stg[sg_i][0:32, 64:65], in_=ps[bo_s][0:32, 256:257]): e.copy(**_kw)), [f"ps{bo_s}"], [f"stg{sg_i}"])
                P.dma("sp", (lambda sg_i, bb: (lambda e, _kw=dict(out=cin_ap[bb * 32:(bb + 1) * 32, :], in_=stg[sg_i][0:32, 0:65]): e.dma_start(**_kw)))(sg_i, bb),
                      reads=[f"stg{sg_i}"], writes=["o_part"], sem=f"st_stg{sg_i}")

            NTL = len(tiles)
            for i in range(NTL + 3):
                if i < NTL:
                    stage_a(tiles[i])
                if 0 <= i - 1 < NTL:
                    stage_b(tiles[i - 1])
                if 0 <= i - 3 < NTL:
                    stage_c(tiles[i - 3])

        P.scope = "final"
        if doB:
            yv = o_yp.rearrange("(t p) d -> p t d", p=128)
            for t0 in range(0, NT, 4):
                P.dma("sp", (lambda t0: (lambda e, _kw=dict(out=yv[:, t0:t0 + 4, :], in_=x[:, t0:t0 + 4, :]): e.dma_start(**_kw)))(t0),
                      reads=[f"x{t}" for t in range(t0, t0 + 4)], writes=["o_yp"], sem="st_x")
            P.dma("sp", (lambda e, _kw=dict(out=o_ys, in_=xs[:]): e.dma_start(**_kw)), reads=["xs"], writes=["o_ys"], sem="st_xs")
        P.finish(out_keys)
        nsem = len(P.dsem) + len(P.esem)
        assert nsem < 140, nsem
    return nc


_CACHE = {}


def run(inputs, cfg):
    L, S = cfg["L"], cfg["S"]
    inp = {k: np.asarray(v) for k, v in inputs.items()}
    st = host_static(inp, cfg)
    state = {"xp": [inp["x_prompt"][c] for c in range(NCORES)],
             "xs": inp["x_sample"].transpose(1, 0, 2).reshape(128, D), "part": None}
    acc = {k: [None] * L for k in ("kp", "vp", "cp", "pp", "ks", "vs", "cs", "ps", "gs")}
    for stage in range(L + 1):
        doA, doB = stage < L, stage >= 1
        key = (tuple(sorted(cfg.items())), doA, doB)
        if key not in _CACHE:
            _CACHE[key] = build(cfg, stage)
        nc = _CACHE[key]
        maps = stage_maps(inp, cfg, stage, st, state)
        res = run_bass_kernel_spmd(nc, maps, core_ids=list(range(NCORES)))
        R_ = res.results
        del maps
        cat = lambda k: np.stack([np.asarray(R_[c][k]) for c in range(NCORES)], axis=0)
        if doB:
            lb = stage - 1
            acc["kp"][lb] = cat("o_kp")[:, 0]
            acc["vp"][lb] = cat("o_vp")[:, 0]
            acc["cp"][lb] = cat("o_cp")[:, 0]
            acc["pp"][lb] = cat("o_pp")[:, 0]
            r0 = R_[0]
            acc["cs"][lb] = np.asarray(r0["o_cs"])[0]
            acc["ps"][lb] = np.asarray(r0["o_ps"])[0]
            acc["gs"][lb] = np.asarray(r0["o_gs"])[0].reshape(4, 32, 256).transpose(1, 0, 2)
            state["xp"] = [np.asarray(R_[c]["o_yp"]) for c in range(NCORES)]
            state["xs"] = np.asarray(r0["o_ys"])
        if doA:
            la = stage
            r0 = R_[0]
            acc["ks"][la] = np.asarray(r0["o_ks"]).reshape(4, 32, 256).transpose(1, 0, 2)
            acc["vs"][la] = np.asarray(r0["o_vs"]).reshape(4, 32, 256).transpose(1, 0, 2)
            state["part"] = cat("o_part")
    y_p = np.stack(state["xp"], axis=0)
    y_s = state["xs"].reshape(4, 32, D).transpose(1, 0, 2)
    stk = lambda k: np.stack(acc[k], axis=0)
    k_p = stk("kp").reshape(L, NCORES, S, 4, 2, 32)
    v_p = stk("vp").reshape(L, NCORES, S, 4, 64)
    k_s = stk("ks").reshape(L, 32, 4, 4, 2, 32)
    v_s = stk("vs").reshape(L, 32, 4, 4, 64)
    outs = (y_p, y_s, k_p, v_p, stk("cp"), stk("pp"), k_s, v_s, stk("cs"), stk("ps"), stk("gs"))
    return tuple(np.ascontiguousarray(o, dtype=np.float32) for o in outs)


def kernel(**inputs):
    cfg = make_cfg()
    return run(inputs, cfg)
```

```python
import contextlib
import math
import numpy as np
import concourse.bass as bass
import concourse.mybir as mybir
from concourse.bass_utils import run_bass_kernel_spmd

F32 = mybir.dt.float32
BF16 = mybir.dt.bfloat16
I32 = mybir.dt.int32
AF = mybir.ActivationFunctionType
ALU = mybir.AluOpType
AX = mybir.AxisListType

NCORES = 8
D = 1024
DFF = 2816
NFC = 22
RMS_EPS = 1e-6
LN_EPS = 1e-5
SLOPES = [2.0 ** (-8.0 * (h + 1) / 4) for h in range(4)]
ISQ = 32 ** -0.5
NEG = -30000.0
ENGS = ["pe", "act", "dve", "pool", "sp"]

C_ID, C_TRI, C_BLK, C_BT, C_INVW, C_ICNT, C_MC, C_MH2C, C_I16, C_HM, C_E8, C_BIAS = 0, 128, 256, 384, 448, 450, 482, 484, 488, 489, 745, 873
B_ID, B_MA, B_MB, B_ONE, B_SEL, B_BN, NCB = 0, 128, 640, 1152, 1280, 1408, 2432
VB_LNG, VB_LNB, VB_GSUB, VB_LQ1, VB_LK1, VB_LQ2, VB_LK2, NVB = 0, 256, 512, 576, 608, 640, 672, 704


class Prog:
    def __init__(self, nc, stack):
        self.nc = nc
        self.stack = stack
        self.q = {e: [] for e in ENGS}
        self.esem = {e: stack.enter_context(nc.semaphore("es_" + e)) for e in ENGS}
        self.ecnt = {e: 0 for e in ENGS}
        self.seen = {e: {} for e in ENGS}
        self.buf = {}
        self.dsem = {}
        self.dcnt = {}
        self.scope = None
        self.use_scopes = False

    def _st(self, k):
        if k not in self.buf:
            self.buf[k] = {"w": None, "r": []}
        return self.buf[k]

    def _waits(self, eng, reads, writes):
        evs = []
        for k in reads:
            s = self._st(k)
            if s["w"] is not None:
                evs.append(s["w"])
        for k in writes:
            s = self._st(k)
            if s["w"] is not None:
                evs.append(s["w"])
            evs.extend(s["r"])
        need = {}
        for kind, sid, val in evs:
            if kind == "E" and sid == eng and eng == "pe":
                continue
            key = (kind, sid)
            if self.seen[eng].get(key, 0) >= val:
                continue
            need[key] = max(need.get(key, 0), val)
        out = []
        for key, val in need.items():
            self.seen[eng][key] = val
            sem = self.esem[key[1]] if key[0] == "E" else self.dsem[key[1]]
            out.append((sem, val))
        return out

    def _record(self, ev, reads, writes):
        for k in reads:
            self._st(k)["r"].append(ev)
        for k in writes:
            s = self._st(k)
            s["w"] = ev
            s["r"] = []

    def op(self, eng, fn, reads=(), writes=(), signal=True):
        waits = self._waits(eng, reads, writes)
        if signal:
            self.ecnt[eng] += 1
            ev = ("E", eng, self.ecnt[eng])
            inc = (self.esem[eng], 1)
        else:
            ev = ("E", eng, self.ecnt[eng] + 1)
            inc = None
        self.q[eng].append((waits, fn, inc, self.scope))
        self._record(ev, reads, writes)

    def dma(self, eng, fn, reads=(), writes=(), sem=None, inc=16):
        if sem is None:
            sem = writes[0]
        if sem not in self.dsem:
            self.dsem[sem] = self.stack.enter_context(self.nc.semaphore("ds_" + sem))
            self.dcnt[sem] = 0
        waits = self._waits(eng, reads, writes)
        self.dcnt[sem] += inc
        ev = ("D", sem, self.dcnt[sem])
        self.q[eng].append((waits, fn, (self.dsem[sem], inc), self.scope))
        self._record(ev, reads, writes)

    def barrier(self, keys):
        self.op("sp", lambda e: e.nop(), reads=(), writes=list(keys))

    def finish(self, out_keys):
        waits = self._waits("sp", out_keys, [])
        self.q["sp"].append((waits, None, None, None))
        nc = self.nc
        with nc.Block() as block:
            def run(engobj, lst):
                cur, cur_id = None, None
                for waits, fn, inc, scope in lst:
                    if self.use_scopes and scope != cur:
                        if cur is not None:
                            nc.leave_named_scope(cur, cur_id, False)
                        cur = scope
                        if cur is not None:
                            cur_id, _ = nc.enter_named_scope(cur, False)
                    for sem, val in waits:
                        engobj.wait_ge(sem, val)
                    if fn is None:
                        continue
                    ins = fn(engobj)
                    if inc is not None:
                        ins.then_inc(inc[0], inc[1])
                if self.use_scopes and cur is not None:
                    nc.leave_named_scope(cur, cur_id, False)

            @block.tensor
            def _(e):
                run(e, self.q["pe"])

            @block.scalar
            def _(e):
                run(e, self.q["act"])

            @block.vector
            def _(e):
                run(e, self.q["dve"])

            @block.gpsimd
            def _(e):
                run(e, self.q["pool"])

            @block.sync
            def _(e):
                run(e, self.q["sp"])


class Arena:
    def __init__(self, ap, ncols):
        self.ap = ap
        self.n = ncols
        self.off = 0
        self.keys = set()
        self.peak = 0

    def reset(self):
        self.off = 0

    def get(self, key, free, dtype):
        n = int(np.prod(free))
        cols = n * (2 if dtype in (F32, I32) else 1)
        cols = (cols + 1) // 2 * 2
        assert self.off + cols <= self.n, ("arena overflow", key, self.off + cols, self.n)
        a = self.ap[:, self.off:self.off + cols]
        self.off += cols
        self.peak = max(self.peak, self.off)
        self.keys.add(key)
        if dtype in (F32, I32):
            a = a.bitcast(dtype)
        a = a[:, 0:n]
        if len(free) == 2:
            a = a.rearrange("p (a b) -> p a b", a=free[0])
        elif len(free) == 3:
            a = a.rearrange("p (a b c) -> p a b c", a=free[0], b=free[1])
        return a


def make_cfg(L=4, S=2048, NP=64, NPOOL=2560):
    return dict(L=L, S=S, NP=NP, NPOOL=NPOOL, NT=S // 128, NB=S // 256, NF=S // 512, T8=NP // 8, PAST=NP * 128)


def host_consts(cfg, core):
    T8, PAST = cfg["T8"], cfg["PAST"]
    p = np.arange(128)
    cst = np.zeros((128, C_BIAS + T8 * 32), np.float32)
    cst[:, C_ID:C_ID + 128] = np.eye(128)
    cst[:, C_TRI:C_TRI + 128] = (p[:, None] <= p[None, :])
    cst[:, C_BLK:C_BLK + 128] = ((p[:, None] % 32) == (p[None, :] % 32))
    for h in range(4):
        for dd in range(16):
            cst[:, C_BT + h * 16 + dd] = SLOPES[h] * (p - 127 - dd * 128)
    wtab = np.zeros((128, 2))
    wtab[:64, 0], wtab[64:, 0], wtab[:64, 1], wtab[64:, 1] = 2, 4, 8, 16
    cst[:, C_INVW:C_INVW + 2] = 1.0 / wtab
    for ch in range(2):
        for pos in range(16):
            cst[:, C_ICNT + ch * 16 + pos] = 1.0 / np.minimum(pos + 1, wtab[:, ch])
    for c in range(2):
        cst[:, C_MC + c] = ((p // 32) % 2 == c)
    for h2 in range(2):
        for c in range(2):
            cst[:, C_MH2C + h2 * 2 + c] = ((p // 64 == h2) & ((p // 32) % 2 == c))
    cst[:, C_I16] = p % 16
    r = np.arange(32)
    cols = np.arange(256)
    cst[:32, C_HM:C_HM + 256] = (((r % 8) // 2)[:, None] == (cols // 64)[None, :])
    cst[:8, C_E8:C_E8 + 128] = (np.arange(8)[:, None] == (p // 16)[None, :])
    col = np.arange(32)
    hcol = (col % 8) // 2
    sl = np.array(SLOPES)[hcol]
    for t in range(T8):
        kpos = (8 * t + p // 16) * 128 + 16 * core + p % 16
        cst[:, C_BIAS + t * 32:C_BIAS + (t + 1) * 32] = sl[None, :] * (kpos[:, None] - (PAST + 3))
    cb = np.zeros((128, NCB), np.float32)
    cb[:, B_ID:B_ID + 128] = np.eye(128)
    tri = (p[:, None] <= p[None, :]).astype(np.float32)
    for c in range(2):
        cb[:, B_MA + c * 256:B_MA + c * 256 + 128] = tri
        cb[:, B_MA + c * 256 + 128:B_MA + c * 256 + 256] = 1.0
        cb[:, B_MB + c * 256 + 128:B_MB + c * 256 + 256] = tri
    cb[:, B_ONE:B_ONE + 128] = 1.0 / 256
    cb[:4, B_SEL:B_SEL + 128] = (np.arange(4)[:, None] == (p // 32)[None, :])
    tq = col // 8
    tk, bk = p // 32, p % 32
    for b in range(32):
        ok = (bk[:, None] == b) & (tk[:, None] <= tq[None, :])
        cb[:, B_BN + b * 32:B_BN + (b + 1) * 32] = np.where(ok, sl[None, :] * (tk[:, None] - 3.0), NEG)
    return cst, cb


def host_static(inp, cfg):
    L, S, NP, NPOOL, T8 = cfg["L"], cfg["S"], cfg["NP"], cfg["NPOOL"], cfg["T8"]
    f = lambda a: np.ascontiguousarray(a, dtype=np.float32)
    st = {}
    st["pt8"] = np.ascontiguousarray(np.asarray(inp["page_table"]).reshape(32, T8, 8).transpose(2, 0, 1).reshape(8, 32 * T8).astype(np.int32))
    vec = np.zeros((L, 128, 2, 35), np.float32)
    fm2 = lambda a: np.asarray(a).reshape(L, 2, 128).transpose(0, 2, 1)
    vec[:, :, :, 0] = fm2(inp["conv_b"])
    vec[:, :, :, 1] = fm2(inp["conv_ln_g"])
    vec[:, :, :, 2] = fm2(inp["conv_ln_b"])
    vec[:, :, :, 3] = fm2(inp["pool_scale"])
    vec[:, :, :, 4:35] = np.asarray(inp["conv_w"]).reshape(L, 31, 2, 128).transpose(0, 3, 2, 1)
    st["vecfm"] = vec
    gfm = np.zeros((L, 128, 2, 8), np.float32)
    gfm[:, :, 0, :] = np.asarray(inp["g_mix_pre"]).reshape(L, 8, 128).transpose(0, 2, 1)
    gfm[:, :, 1, :] = np.asarray(inp["g_ffn_pre"]).reshape(L, 8, 128).transpose(0, 2, 1)
    st["gfm"] = gfm
    st["vbc"] = f(np.concatenate([np.asarray(inp[k]).reshape(L, -1) for k in
                                  ("sgu_ln_g", "sgu_ln_b", "attn_sub_g", "lambda_q1", "lambda_k1", "lambda_q2", "lambda_k2")], axis=1).reshape(L, 1, NVB))
    st["gpost"] = f(np.stack([np.asarray(inp["g_mix_post"]), np.asarray(inp["g_ffn_post"])], axis=1).reshape(L, 2, 1, D))
    st["consts"] = [host_consts(cfg, c) for c in range(NCORES)]
    return st


def stage_maps(inp, cfg, stage, st, state):
    L, S, NP, NPOOL, T8 = cfg["L"], cfg["S"], cfg["NP"], cfg["NPOOL"], cfg["T8"]
    f = lambda a: np.ascontiguousarray(a, dtype=np.float32)
    doA, doB = stage < L, stage >= 1
    shared = {"xs": f(state["xs"])}
    if doB:
        lb = stage - 1
        sl = slice(lb, lb + 1)
        sc = np.asarray(inp["state_conv"])[sl]
        sp_ = np.asarray(inp["state_pool"])[sl]
        shared["sconv_fm"] = f(sc.transpose(0, 3, 1, 2))
        shared["spool_fm"] = f(sp_.transpose(0, 3, 1, 2))
        shared["sconv_raw"] = f(sc)
        shared["spool_raw"] = f(sp_)
        shared["w_in"] = f(np.asarray(inp["w_in"])[sl])
        shared["w_out"] = f(np.asarray(inp["w_out"])[sl])
        wgu = np.asarray(inp["w_gu"])[sl].reshape(1, 8, 128, 2, NFC, 128)
        shared["w_gu_t"] = f(wgu.transpose(0, 4, 2, 1, 3, 5).reshape(1, NFC, 128, 8, 256))
        shared["w_down"] = f(np.asarray(inp["w_down"])[sl])
        for k in ("vecfm", "gfm", "vbc", "gpost"):
            shared[k] = f(st[k][sl])
        shared["sgu_w"] = f(np.asarray(inp["sgu_w"])[sl])
        shared["sgu_b"] = f(np.asarray(inp["sgu_b"])[sl])
        shared["pool_w"] = f(np.asarray(inp["pool_w"])[sl])
        lam_init = 0.8 - 0.6 * math.exp(-0.3 * lb)
        lamc = np.zeros((128, 2), np.float32)
        lamc[:, 0] = -lam_init
        lamc[:, 1] = 1.0 - lam_init
        shared["lamc"] = lamc
        shared["part"] = f(state["part"])
    if doA:
        la = stage
        shared["pt8"] = st["pt8"]
        shared["w_in_a"] = f(np.asarray(inp["w_in"])[la][:, 1024:1792])
        shared["gfm_a"] = f(st["gfm"][la])
        ck = np.asarray(inp["cache_k"])[la].reshape(NPOOL, 128, 256)
        cv = np.asarray(inp["cache_v"])[la].reshape(NPOOL, 128, 256)
    maps = []
    for c in range(NCORES):
        m = dict(shared)
        m["cst"], m["cstb"] = st["consts"][c]
        if doB:
            m["xp"] = f(state["xp"][c])
        if doA:
            kv = np.empty((NPOOL, 16, 512), np.float32)
            kv[..., 0:256] = ck[:, 16 * c:16 * c + 16]
            kv[..., 256:512] = cv[:, 16 * c:16 * c + 16]
            m["kv"] = kv.reshape(NPOOL * 16, 512)
        maps.append(m)
    return maps


def build(cfg, stage):
    L, S, NP, NPOOL, NT, NB, NF, T8, PAST = (cfg[k] for k in ("L", "S", "NP", "NPOOL", "NT", "NB", "NF", "T8", "PAST"))
    doA = stage < L
    doB = stage >= 1
    nc = bass.Bass("TRN2", target_bir_lowering=False)
    NC_ = C_BIAS + T8 * 32

    def din(name, shape, dt=F32):
        return nc.dram_tensor(name, list(shape), dt, kind="ExternalInput").ap()

    def dout(name, shape):
        return nc.dram_tensor(name, list(shape), F32, kind="ExternalOutput").ap()

    xs_d = din("xs", [128, D])
    cst_d = din("cst", [128, NC_]); cstb_d = din("cstb", [128, NCB])
    out_keys = []
    if doB:
        xp_d = din("xp", [S, D])
        sconv_fm_d = din("sconv_fm", [1, 256, 32, 30]); spool_fm_d = din("spool_fm", [1, 256, 32, 15])
        sconv_raw_d = din("sconv_raw", [1, 32, 30, 256]); spool_raw_d = din("spool_raw", [1, 32, 15, 256])
        w_in_d = din("w_in", [1, D, 2048]); w_out_d = din("w_out", [1, D, D])
        w_gu_d = din("w_gu_t", [1, NFC, 128, 8, 256]); w_down_d = din("w_down", [1, DFF, D])
        vecfm_d = din("vecfm", [1, 128, 2, 35]); gfm_d = din("gfm", [1, 128, 2, 8]); vbc_d = din("vbc", [1, 1, NVB])
        gpost_d = din("gpost", [1, 2, 1, D]); sguw_d = din("sgu_w", [1, 4, 128, 128]); sgub_d = din("sgu_b", [1, 4, 128])
        poolw_d = din("pool_w", [1, 4, 64, 64]); lamc_d = din("lamc", [128, 2]); part_d = din("part", [NCORES, 1024, 65])
        o_yp = dout("o_yp", [S, D]); o_ys = dout("o_ys", [128, D])
        o_kp = dout("o_kp", [1, S, 256]); o_vp = dout("o_vp", [1, S, 256])
        o_cp = dout("o_cp", [1, 30, 256]); o_pp = dout("o_pp", [1, 15, 256])
        o_cs = dout("o_cs", [1, 32, 30, 256]); o_ps = dout("o_ps", [1, 32, 15, 256]); o_gs = dout("o_gs", [1, 128, 256])
        out_keys += ["o_yp", "o_ys", "o_kp", "o_vp", "o_cp", "o_pp", "o_cs", "o_ps", "o_gs"]
    if doA:
        kv_d = din("kv", [NPOOL * 16, 512])
        pt8_d = din("pt8", [8, 32 * T8], I32)
        w_in_a_d = din("w_in_a", [D, 768]); gfm_a_d = din("gfm_a", [128, 2, 8])
        o_ks = dout("o_ks", [128, 256]); o_vs = dout("o_vs", [128, 256]); o_part = dout("o_part", [1024, 65])
        out_keys += ["o_ks", "o_vs", "o_part"]
    DBG = False

    with contextlib.ExitStack() as st:
        P = Prog(nc, st)
        P.use_scopes = bool(cfg.get("scopes", False))
        P.scope = "setup"
        SB = lambda n, s, d: st.enter_context(nc.sbuf_tensor(n, list(s), d))
        x = SB("x", [128, NT, D], F32)
        xs = SB("xs_sb", [128, D], F32)
        cst = SB("cst_sb", [128, NC_], F32)
        cstb = SB("cstb_sb", [128, NCB], BF16)
        Wt = SB("Wt", [128, 24576], BF16)
        TCOLS = 32896
        Tt = SB("Tt", [128, TCOLS], BF16)
        hn = SB("hn", [128, D], BF16)
        tmpn = SB("tmpn", [128, 512], F32)
        gpost = SB("gpost_sb", [128, D], F32)
        vbc = SB("vbc_sb", [128, NVB], F32)
        vecfm = SB("vecfm_sb", [128, 2, 35], F32)
        gfm = SB("gfm_sb", [128, 2, 8], F32)
        stt = SB("stt", [128, 48], F32)
        idx = SB("idx", [128, 32 * T8], I32)
        WsT = SB("WsT", [128, 4, 128], BF16)
        WsS = SB("WsS", [128, 4, 128], BF16)
        BsT = SB("BsT", [128, 2, 128], F32)
        Wp = SB("Wp", [128, 2, 128], BF16)
        Wpst = SB("Wpst", [128, 2, 64], F32)
        zer = SB("zer", [128, 512], BF16)
        nlam = SB("nlam", [128, 4], F32)
        lamc = SB("lamc_sb", [128, 2], F32)
        ps = [st.enter_context(nc.psum_tensor(f"ps{i}", [128, 512], F32)) for i in range(8)]
        psb = [p_[:].bitcast(BF16) for p_ in ps]
        TA = Arena(Tt[:], TCOLS)

        idf = cst[:, C_ID:C_ID + 128]
        idb = cstb[:, B_ID:B_ID + 128]
        w_in_sb = Wt[:, 0:16384].rearrange("p (k n) -> p k n", k=8)
        w_out_sb = Wt[:, 16384:24576].rearrange("p (k n) -> p k n", k=8)
        w_down_sb = Wt[:, 0:22528].rearrange("p (k n) -> p k n", k=NFC)

        V = lambda fn, r, w: P.op("dve", fn, r, w)
        A = lambda fn, r, w: P.op("act", fn, r, w)
        G = lambda fn, r, w: P.op("pool", fn, r, w)

        def MM(out, lhsT, rhs, start, stop, r, w, signal):
            P.op("pe", lambda e: e.matmul(out, lhsT=lhsT, rhs=rhs, start=start, stop=stop, skip_group_check=True), r, w, signal)

        def TR(out, in_, ident, r, w, signal):
            P.op("pe", lambda e: e.transpose(out=out, in_=in_, identity=ident), r, w, signal)

        gbc = [0]

        def gb():
            gbc[0] = (gbc[0] + 1) % 6
            return gbc[0]

        sctr = [0]

        def sc(n=1):
            if sctr[0] + n > 48:
                sctr[0] = 0
            a = sctr[0]
            sctr[0] += n
            return a

        P.dma("sp", (lambda e, _kw=dict(out=cst[:], in_=cst_d): e.dma_start(**_kw)), writes=["cst"])
        P.dma("pool", (lambda e, _kw=dict(out=cstb[:], in_=cstb_d): e.dma_start(**_kw)), writes=["cstb"])
        G((lambda e, _kw=dict(ap=zer[:], constant=0.0): e.memset(**_kw)), [], ["zer"])
        G((lambda e, _kw=dict(ap=Wp[:], constant=0.0): e.memset(**_kw)), [], ["Wp"])
        if doB:
            xv = xp_d.rearrange("(t p) d -> p t d", p=128)
            for t0 in range(0, NT, 4):
                P.dma("sp", (lambda t0: (lambda e, _kw=dict(out=x[:, t0:t0 + 4, :], in_=xv[:, t0:t0 + 4, :]): e.dma_start(**_kw)))(t0),
                      writes=[f"x{t}" for t in range(t0, t0 + 4)], sem=f"xl{t0 // 4 % 4}")
            P.dma("sp", (lambda e, _kw=dict(out=lamc[:], in_=lamc_d): e.dma_start(**_kw)), writes=["lamc"])
        P.dma("sp", (lambda e, _kw=dict(out=xs[:], in_=xs_d): e.dma_start(**_kw)), writes=["xs"])
        if doA:
            TA.reset()
            pt8i = TA.get("pt8i", (32 * T8,), I32)
            pt8f = TA.get("pt8f", (32 * T8,), F32)
            idxf = TA.get("idxf", (32 * T8,), F32)
            P.dma("sp", (lambda e, _kw=dict(out=pt8i[0:8, :], in_=pt8_d): e.dma_start(**_kw)), writes=["pt8i"])
            V((lambda e, _kw=dict(out=pt8f[0:8, :], in_=pt8i[0:8, :]): e.tensor_copy(**_kw)), ["pt8i"], ["pt8f"])
            b0 = gb()
            MM(ps[b0][:, 0:32 * T8], cst[0:8, C_E8:C_E8 + 128], pt8f[0:8, :], True, True, ["cst", "pt8f"], [f"ps{b0}"], True)
            V((lambda e, _kw=dict(out=idxf, in0=ps[b0][:, 0:32 * T8], scalar1=16.0, scalar2=cst[:, C_I16:C_I16 + 1], op0=ALU.mult, op1=ALU.add): e.tensor_scalar(**_kw)),
              [f"ps{b0}", "cst"], ["idxf"])
            V((lambda e, _kw=dict(out=idx[:], in_=idxf): e.tensor_copy(**_kw)), ["idxf"], ["idx"])

        def prenorm_tile(xap, xkey, gsel, hT, hkey, tcol):
            c0 = sc(3)
            hnj = hn[:]
            A((lambda e, _kw=dict(out=hnj, in_=xap, func=AF.Square, accum_out=stt[:, c0:c0 + 1]): e.activation(**_kw)), [xkey], ["hn", "stt"])
            A((lambda e, _kw=dict(out=stt[:, c0 + 1:c0 + 2], in_=stt[:, c0:c0 + 1], func=AF.Sqrt, scale=1.0 / D, bias=RMS_EPS): e.activation(**_kw)), ["stt"], ["stt"])
            V((lambda e, _kw=dict(out=stt[:, c0 + 2:c0 + 3], in_=stt[:, c0 + 1:c0 + 2]): e.reciprocal(**_kw)), ["stt"], ["stt"])
            V((lambda e, _kw=dict(out=hnj, in0=xap, scalar1=stt[:, c0 + 2:c0 + 3], scalar2=None, op0=ALU.mult): e.tensor_scalar(**_kw)), [xkey, "stt"], ["hn"])
            b = gb()
            for kc in range(8):
                TR(psb[b][:, kc * 128:(kc + 1) * 128], hn[:, kc * 128:(kc + 1) * 128], idb, ["hn", "cstb"], [f"ps{b}"], kc == 7)
            V((lambda e, _kw=dict(out=hT[:, :, tcol:tcol + 128], in0=psb[b].rearrange("p (k t) -> p k t", k=8),
                                        in1=gfm[:, gsel, :].unsqueeze(2).to_broadcast([128, 8, 128]), op=ALU.mult): e.tensor_tensor(**_kw)),
              [f"ps{b}", "gfm"], [hkey])

        def postnorm_add(bA, bB, xap, xkey):
            c0 = sc(5)
            tj = tmpn[:].bitcast(BF16)
            A((lambda e, _kw=dict(out=tj[:, 0:512], in_=ps[bA][:], func=AF.Square, accum_out=stt[:, c0:c0 + 1]): e.activation(**_kw)), [f"ps{bA}"], ["tmpn", "stt"])
            A((lambda e, _kw=dict(out=tj[:, 512:1024], in_=ps[bB][:], func=AF.Square, accum_out=stt[:, c0 + 1:c0 + 2]): e.activation(**_kw)), [f"ps{bB}"], ["tmpn", "stt"])
            V((lambda e, _kw=dict(out=stt[:, c0 + 2:c0 + 3], in0=stt[:, c0:c0 + 1], in1=stt[:, c0 + 1:c0 + 2], op=ALU.add): e.tensor_tensor(**_kw)), ["stt"], ["stt"])
            A((lambda e, _kw=dict(out=stt[:, c0 + 3:c0 + 4], in_=stt[:, c0 + 2:c0 + 3], func=AF.Sqrt, scale=1.0 / D, bias=RMS_EPS): e.activation(**_kw)), ["stt"], ["stt"])
            V((lambda e, _kw=dict(out=stt[:, c0 + 4:c0 + 5], in_=stt[:, c0 + 3:c0 + 4]): e.reciprocal(**_kw)), ["stt"], ["stt"])
            for n, bk in enumerate((bA, bB)):
                V((lambda e, _kw=dict(out=tmpn[:], in0=ps[bk][:], scalar=stt[:, c0 + 4:c0 + 5], in1=gpost[:, n * 512:(n + 1) * 512],
                                                            op0=ALU.mult, op1=ALU.mult): e.scalar_tensor_tensor(**_kw)), [f"ps{bk}", "stt", "gpost"], ["tmpn"])
                G((lambda e, _kw=dict(out=xap[:, n * 512:(n + 1) * 512], in0=xap[:, n * 512:(n + 1) * 512], in1=tmpn[:], op=ALU.add): e.tensor_tensor(**_kw)),
                  ["tmpn", xkey], [xkey])

        def fm_chunk(col, hT, hkey, W, evac):
            b = gb()
            for kc in range(8):
                MM(ps[b][:, 0:W], w_in_sb[:, kc, col:col + 128], hT[:, kc, 0:W], kc == 0, kc == 7, ["w_in", hkey], [f"ps{b}"], kc == 7)
            evac(b)

        def tm_cols(col, N, hT, hkey, tcol):
            b = gb()
            for kc in range(8):
                MM(ps[b][:, 0:N], hT[:, kc, tcol:tcol + 128], w_in_sb[:, kc, col:col + N], kc == 0, kc == 7, ["w_in", hkey], [f"ps{b}"], kc == 7)
            return b

        def ln_rows(vg, vgk, vn32, vnk):
            c0 = sc(10)
            V((lambda e, _kw=dict(out=stt[:, c0:c0 + 6], in_=vg): e.bn_stats(**_kw)), [vgk], ["stt"])
            V((lambda e, _kw=dict(out=stt[:, c0 + 6:c0 + 8], in_=stt[:, c0:c0 + 6]): e.bn_aggr(**_kw)), ["stt"], ["stt"])
            A((lambda e, _kw=dict(out=stt[:, c0 + 8:c0 + 9], in_=stt[:, c0 + 7:c0 + 8], func=AF.Sqrt, scale=1.0, bias=LN_EPS): e.activation(**_kw)), ["stt"], ["stt"])
            V((lambda e, _kw=dict(out=stt[:, c0 + 9:c0 + 10], in_=stt[:, c0 + 8:c0 + 9]): e.reciprocal(**_kw)), ["stt"], ["stt"])
            V((lambda e, _kw=dict(out=vn32, in0=vg, scalar1=stt[:, c0 + 6:c0 + 7], scalar2=stt[:, c0 + 9:c0 + 10], op0=ALU.subtract, op1=ALU.mult): e.tensor_scalar(**_kw)),
              [vgk, "stt"], [vnk])
            V((lambda e, _kw=dict(out=vn32, in0=vn32, in1=vbc[:, VB_LNG:VB_LNG + 256], op=ALU.mult): e.tensor_tensor(**_kw)), [vnk, "vbc"], [vnk])
            V((lambda e, _kw=dict(out=vn32, in0=vn32, in1=vbc[:, VB_LNB:VB_LNB + 256], op=ALU.add): e.tensor_tensor(**_kw)), [vnk, "vbc"], [vnk])

        def to_vnz(vn32, vnk, vnz):
            v4 = vn32.rearrange("p (a two c) -> p a two c", two=2, c=64)
            z4 = vnz.rearrange("p (a two) c -> p a two c", two=2)
            G((lambda e, _kw=dict(out=z4[:, :, 0, 0:64], in_=v4[:, :, 0, :]): e.tensor_copy(**_kw)), [vnk], ["vnz"])
            G((lambda e, _kw=dict(out=z4[:, :, 1, 64:128], in_=v4[:, :, 1, :]): e.tensor_copy(**_kw)), [vnk], ["vnz"])

        def conv_ln_silu(cy, W, ybf, ysq, mean, m2, rstd, mixT, mkey, mcol):
            G((lambda e, _kw=dict(out=ybf, in_=cy): e.tensor_copy(**_kw)), ["cy"], ["ybf"])
            G((lambda e, _kw=dict(out=ysq, in0=cy, in1=cy, op=ALU.mult): e.tensor_tensor(**_kw)), ["cy"], ["ysq"])
            b = gb()
            one = cstb[:, B_ONE:B_ONE + 128]
            for ch in range(2):
                MM(ps[b][:, 0:W], one, ybf[:, ch, :], ch == 0, ch == 1, ["cstb", "ybf"], [f"ps{b}"], False)
            for ch in range(2):
                MM(ps[b][:, 256:256 + W], one, ysq[:, ch, :], ch == 0, ch == 1, ["cstb", "ysq"], [f"ps{b}"], ch == 1)
            A((lambda e, _kw=dict(out=mean, in_=ps[b][:, 0:W]): e.copy(**_kw)), [f"ps{b}"], ["cmean"])
            G((lambda e, _kw=dict(out=m2, in0=mean, in1=mean, op=ALU.mult): e.tensor_tensor(**_kw)), ["cmean"], ["cm2"])
            V((lambda e, _kw=dict(out=m2, in0=ps[b][:, 256:256 + W], in1=m2, op=ALU.subtract): e.tensor_tensor(**_kw)), [f"ps{b}", "cm2"], ["cm2"])
            A((lambda e, _kw=dict(out=rstd, in_=m2, func=AF.Sqrt, scale=1.0, bias=LN_EPS): e.activation(**_kw)), ["cm2"], ["crstd"])
            V((lambda e, _kw=dict(out=rstd, in_=rstd): e.reciprocal(**_kw)), ["crstd"], ["crstd"])
            V((lambda e, _kw=dict(out=cy, in0=cy, in1=mean.unsqueeze(1).to_broadcast([128, 2, W]), op=ALU.subtract): e.tensor_tensor(**_kw)), ["cy", "cmean"], ["cy"])
            V((lambda e, _kw=dict(out=cy, in0=cy, in1=rstd.unsqueeze(1).to_broadcast([128, 2, W]), op=ALU.mult): e.tensor_tensor(**_kw)), ["cy", "crstd"], ["cy"])
            for ch in range(2):
                A((lambda e, _kw=dict(out=mixT[:, ch, mcol:mcol + W], in_=cy[:, ch, :], func=AF.Silu, scale=vecfm[:, ch, 1:2], bias=vecfm[:, ch, 2:3]): e.activation(**_kw)),
                  ["cy", "vecfm"], [mkey])

        def attn_finish(o0, l0, o1, l1, rk, obuf_h, okey):
            c0 = sc(3)
            V((lambda e, _kw=dict(out=stt[:, c0:c0 + 1], in_=l0): e.reciprocal(**_kw)), rk, ["stt"])
            V((lambda e, _kw=dict(out=stt[:, c0 + 1:c0 + 2], in_=l1): e.reciprocal(**_kw)), rk, ["stt"])
            V((lambda e, _kw=dict(out=stt[:, c0 + 2:c0 + 3], in0=stt[:, c0 + 1:c0 + 2], in1=nlam[:, 0:1], op=ALU.mult): e.tensor_tensor(**_kw)), ["stt", "nlam"], ["stt"])
            V((lambda e, _kw=dict(out=obuf_h, in0=o0, scalar1=stt[:, c0:c0 + 1], scalar2=None, op0=ALU.mult): e.tensor_scalar(**_kw)), rk + ["stt"], [okey])
            V((lambda e, _kw=dict(out=obuf_h, in0=o1, scalar=stt[:, c0 + 2:c0 + 3], in1=obuf_h, op0=ALU.mult, op1=ALU.add): e.scalar_tensor_tensor(**_kw)), rk + ["stt", okey], [okey])

        def head_norm_to_mix(ob, okey, osq, ycb, lam_init, mixT, mkey, mcol):
            c0 = sc(8)
            o3 = ob.rearrange("p (h e) -> p h e", h=4)
            G((lambda e, _kw=dict(out=osq, in0=ob, in1=ob, op=ALU.mult): e.tensor_tensor(**_kw)), [okey], ["osq"])
            V((lambda e, _kw=dict(out=stt[:, c0:c0 + 4], in_=osq.rearrange("p (h e) -> p h e", h=4), axis=AX.X, op=ALU.add): e.tensor_reduce(**_kw)), ["osq"], ["stt"])
            A((lambda e, _kw=dict(out=stt[:, c0 + 4:c0 + 8], in_=stt[:, c0:c0 + 4], func=AF.Sqrt, scale=1.0 / 64, bias=LN_EPS): e.activation(**_kw)), ["stt"], ["stt"])
            V((lambda e, _kw=dict(out=stt[:, c0:c0 + 4], in_=stt[:, c0 + 4:c0 + 8]): e.reciprocal(**_kw)), ["stt"], ["stt"])
            V((lambda e, _kw=dict(out=o3, in0=o3, scalar=lamc[:, 1:2], in1=stt[:, c0:c0 + 4].unsqueeze(2).to_broadcast([128, 4, 64]),
                                               op0=ALU.mult, op1=ALU.mult): e.scalar_tensor_tensor(**_kw)), [okey, "stt", "lamc"], [okey])
            V((lambda e, _kw=dict(out=ycb.rearrange("p (h e) -> p h e", h=4), in0=o3,
                                        in1=vbc[:, VB_GSUB:VB_GSUB + 64].unsqueeze(1).to_broadcast([128, 4, 64]), op=ALU.mult): e.tensor_tensor(**_kw)), [okey, "vbc"], ["ycb"])
            b = gb()
            for ch in range(2):
                TR(psb[b][:, ch * 128:(ch + 1) * 128], ycb[:, ch * 128:(ch + 1) * 128], idb, ["ycb", "cstb"], [f"ps{b}"], ch == 1)
            A((lambda e, _kw=dict(out=mixT[:, 4:6, mcol:mcol + 128], in_=psb[b][:, 0:256].rearrange("p (c t) -> p c t", c=2)): e.copy(**_kw)), [f"ps{b}"], [mkey])

        def out_proj_tile(mixT, mkeys, mcol, xap, xkey):
            bA, bB = gb(), gb()
            for n, bk in enumerate((bA, bB)):
                for kc in range(8):
                    MM(ps[bk][:], mixT[:, kc, mcol:mcol + 128], w_out_sb[:, kc, n * 512:(n + 1) * 512], kc == 0, kc == 7, mkeys + ["w_out"], [f"ps{bk}"], kc == 7)
            postnorm_add(bA, bB, xap, xkey)

        ARK = set()

        def phase_barrier(extra=()):
            P.barrier(sorted(set(P.buf.keys()) | ARK | set(extra)))

        for l in ([0] if doB else []):
            lam_init = None
            P.scope = "loads"
            phase_barrier(["w_in", "w_out", "w_down"])
            TA.reset()
            w_in_v = w_in_d[l].rearrange("(kc p) n -> p kc n", p=128)
            for hf in range(2):
                P.dma("pool", (lambda hf: (lambda e, _kw=dict(out=w_in_sb[:, hf * 4:(hf + 1) * 4, :], in_=w_in_v[:, hf * 4:(hf + 1) * 4, :]): e.dma_start(**_kw)))(hf),
                      writes=["w_in"], sem="w_in")
            P.dma("pool", (lambda e, _kw=dict(out=w_out_sb, in_=w_out_d[l].rearrange("(kc p) n -> p kc n", p=128)): e.dma_start(**_kw)), writes=["w_out"], sem="w_out")
            P.dma("sp", (lambda e, _kw=dict(out=vecfm[:], in_=vecfm_d[l]): e.dma_start(**_kw)), writes=["vecfm"])
            P.dma("sp", (lambda e, _kw=dict(out=gfm[:], in_=gfm_d[l]): e.dma_start(**_kw)), writes=["gfm"])
            P.dma("sp", (lambda e, _kw=dict(out=vbc[:], in_=vbc_d[l].partition_broadcast(128)): e.dma_start(**_kw)), writes=["vbc"])
            P.dma("sp", (lambda e, _kw=dict(out=gpost[:], in_=gpost_d[l, 0].partition_broadcast(128)): e.dma_start(**_kw)), writes=["gpost"])
            sguw_st = TA.get("sguw_st", (4, 128), F32)
            P.dma("sp", (lambda e, _kw=dict(out=sguw_st, in_=sguw_d[l].rearrange("g t s -> t g s")): e.dma_start(**_kw)), writes=["sguw_st"])
            for g in range(4):
                P.dma("sp", (lambda g: (lambda e, _kw=dict(out=BsT[(g % 2) * 64:(g % 2) * 64 + 64, g // 2, :], in_=sgub_d[l, g:g + 1, :].partition_broadcast(64)): e.dma_start(**_kw)))(g),
                      writes=["BsT"], sem="BsT")
                P.dma("sp", (lambda g: (lambda e, _kw=dict(out=Wpst[(g % 2) * 64:(g % 2) * 64 + 64, g // 2, :], in_=poolw_d[l, g]): e.dma_start(**_kw)))(g),
                      writes=["Wpst"], sem="Wpst")
            G((lambda e, _kw=dict(out=Wp[0:64, :, 0:64], in_=Wpst[0:64, :, :]): e.tensor_copy(**_kw)), ["Wpst"], ["Wp"])
            G((lambda e, _kw=dict(out=Wp[64:128, :, 64:128], in_=Wpst[64:128, :, :]): e.tensor_copy(**_kw)), ["Wpst"], ["Wp"])
            c0 = sc(6)
            ltmp = TA.get("ltmp", (64,), F32)
            V((lambda e, _kw=dict(out=ltmp[:, 0:32], in0=vbc[:, VB_LQ1:VB_LQ1 + 32], in1=vbc[:, VB_LK1:VB_LK1 + 32], op=ALU.mult): e.tensor_tensor(**_kw)), ["vbc"], ["ltmp"])
            V((lambda e, _kw=dict(out=ltmp[:, 32:64], in0=vbc[:, VB_LQ2:VB_LQ2 + 32], in1=vbc[:, VB_LK2:VB_LK2 + 32], op=ALU.mult): e.tensor_tensor(**_kw)), ["vbc"], ["ltmp"])
            V((lambda e, _kw=dict(out=stt[:, c0:c0 + 2], in_=ltmp.rearrange("p (a b) -> p a b", a=2), axis=AX.X, op=ALU.add): e.tensor_reduce(**_kw)), ["ltmp"], ["stt"])
            A((lambda e, _kw=dict(out=stt[:, c0 + 2:c0 + 4], in_=stt[:, c0:c0 + 2], func=AF.Exp): e.activation(**_kw)), ["stt"], ["stt"])
            V((lambda e, _kw=dict(out=stt[:, c0 + 4:c0 + 5], in0=stt[:, c0 + 3:c0 + 4], in1=stt[:, c0 + 2:c0 + 3], op=ALU.subtract): e.tensor_tensor(**_kw)), ["stt"], ["stt"])
            V((lambda e, _kw=dict(out=nlam[:, 0:1], in0=stt[:, c0 + 4:c0 + 5], scalar1=lamc[:, 0:1], scalar2=None, op0=ALU.add): e.tensor_scalar(**_kw)), ["stt", "lamc"], ["nlam"])
            b = gb()
            for g in range(4):
                TR(ps[b][:, g * 128:(g + 1) * 128], sguw_st[:, g, :], idf, ["sguw_st", "cst"], [f"ps{b}"], g == 3)
            V((lambda e, _kw=dict(out=WsT[:], in0=ps[b][:].rearrange("p (g t) -> p g t", g=4),
                                        in1=cst[:, C_TRI:C_TRI + 128].unsqueeze(1).to_broadcast([128, 4, 128]), op=ALU.mult): e.tensor_tensor(**_kw)), [f"ps{b}", "cst"], ["WsT"])
            sel = cstb[0:4, B_SEL:B_SEL + 128]
            b = gb()
            for g in range(4):
                MM(ps[b][0:4, g * 128:(g + 1) * 128], WsT[0:4, g, 0:4], sel, True, True, ["WsT", "cstb"], [f"ps{b}"], g == 3)
            asb = TA.get("asb", (4, 128), BF16)
            A((lambda e, _kw=dict(out=asb[0:4], in_=ps[b][0:4, :].rearrange("p (g t) -> p g t", g=4)): e.copy(**_kw)), [f"ps{b}"], ["asb"])
            b2 = gb()
            for g in range(4):
                MM(ps[b2][:, g * 128:(g + 1) * 128], asb[0:4, g, :], sel, True, True, ["asb", "cstb"], [f"ps{b2}"], g == 3)
            V((lambda e, _kw=dict(out=WsS[:], in0=ps[b2][:].rearrange("p (g t) -> p g t", g=4),
                                        in1=cst[:, C_BLK:C_BLK + 128].unsqueeze(1).to_broadcast([128, 4, 128]), op=ALU.mult): e.tensor_tensor(**_kw)), [f"ps{b2}", "cst"], ["WsS"])
            ARK.update(TA.keys)

            P.scope = "PM"
            phase_barrier()
            TA.reset()
            W = 256
            hT = TA.get("hT", (8, 512), BF16)
            kT_all = TA.get("kT_all", (2, S), BF16)
            V1 = TA.get("V1", (NT, 4, 65), BF16)
            Qbd = TA.get("Qbd", (2, 512), BF16)
            mixT = TA.get("mixT", (8, W), BF16)
            NPT = 3
            pT = [TA.get(f"pT{i}", (512,), BF16) for i in range(NPT)]
            hconv = TA.get("hconv", (2, 30 + W), F32)
            cy = TA.get("cy", (2, W), F32)
            ybf = TA.get("ybf", (2, W), BF16)
            ysq = TA.get("ysq", (2, W), BF16)
            cmean = TA.get("cmean", (W,), F32)
            cm2 = TA.get("cm2", (W,), F32)
            crstd = TA.get("crstd", (W,), F32)
            sg = TA.get("sg", (2, W), F32)
            uT = TA.get("uT", (2, W), BF16)
            vg = TA.get("vg", (256,), F32)
            vn32 = TA.get("vn32", (256,), F32)
            vnz = TA.get("vnz", (4, 128), BF16)
            sgt = TA.get("sgt", (2, 128), F32)
            xdT = TA.get("xdT", (2, 16 + W), F32)
            pA = TA.get("pA", (2, 16 + W), F32)
            pB = TA.get("pB", (2, 16 + W), F32)
            dT = TA.get("dT", (2, W), BF16)
            kvo = TA.get("kvo", (512,), F32)
            obuf = TA.get("obuf", (2, 256), F32)
            osq = TA.get("osq", (256,), F32)
            ycb = TA.get("ycb", (256,), BF16)
            tm1 = TA.get("tm1", (256,), F32)
            tm2 = TA.get("tm2", (256,), F32)
            ARK.update(TA.keys)
            G((lambda e, _kw=dict(ap=V1, constant=1.0): e.memset(**_kw)), [], ["V1"])
            G((lambda e, _kw=dict(ap=vnz, constant=0.0): e.memset(**_kw)), [], ["vnz"])
            G((lambda e, _kw=dict(ap=hconv[:, :, 0:30], constant=0.0): e.memset(**_kw)), [], ["hconv"])
            G((lambda e, _kw=dict(ap=xdT[:, :, 0:15], constant=0.0): e.memset(**_kw)), [], ["xdT"])

            def pm_prenorm(blk):
                off = (blk % 2) * 256
                for tl in range(2):
                    prenorm_tile(x[:, 2 * blk + tl, :], f"x{2 * blk + tl}", 0, hT, f"hT{blk % 2}", off + tl * 128)

            F_HOIST, F_SKEW, F_SIDE = cfg.get("hoist", 1), cfg.get("skew", 1), cfg.get("side", 0)
            if F_HOIST:
                pm_prenorm(0)
            for blk in range(NB):
                t0 = 2 * blk
                hoff = (blk % 2) * 256
                hTb = hT[:, :, hoff:hoff + 256]
                hk = f"hT{blk % 2}"
                if not F_HOIST:
                    pm_prenorm(blk)
                for ch in range(2):
                    fm_chunk(256 + ch * 128, hTb, hk, W,
                             lambda b, ch=ch: A((lambda e, _kw=dict(out=sg[:, ch, :], in_=ps[b][:, 0:W], func=AF.Sigmoid): e.activation(**_kw)), [f"ps{b}"], ["sg"]))
                for ch in range(2):
                    fm_chunk(ch * 128, hTb, hk, W,
                             lambda b, ch=ch: V((lambda e, _kw=dict(out=hconv[:, ch, 30:30 + W], in0=ps[b][:, 0:W], in1=sg[:, ch, :], op=ALU.mult): e.tensor_tensor(**_kw)),
                                                [f"ps{b}", "sg"], ["hconv"]))
                for ch in range(2):
                    fm_chunk(512 + ch * 128, hTb, hk, W,
                             lambda b, ch=ch: A((lambda e, _kw=dict(out=uT[:, ch, :], in_=ps[b][:, 0:W], func=AF.Gelu): e.activation(**_kw)), [f"ps{b}"], ["uT"]))
                for chk in range(2):
                    def ev_qp(b, chk=chk):
                        for c in range(2):
                            V((lambda e, _kw=dict(out=Qbd[:, chk, c * 256:(c + 1) * 256], in0=ps[b][:, 0:W], scalar1=cst[:, C_MC + c:C_MC + c + 1],
                                                             scalar2=None, op0=ALU.mult): e.tensor_scalar(**_kw)), [f"ps{b}", "cst"], ["Qbd"])
                    fm_chunk(1024 + chk * 128, hTb, hk, W, ev_qp)
                for ch in range(2):
                    fm_chunk(1280 + ch * 128, hTb, hk, W,
                             lambda b, ch=ch: A((lambda e, _kw=dict(out=kT_all[:, ch, blk * W:(blk + 1) * W], in_=ps[b][:, 0:W]): e.copy(**_kw)), [f"ps{b}"], ["kT_all"]))
                for ch in range(2):
                    fm_chunk(1792 + ch * 128, hTb, hk, W,
                             lambda b, ch=ch: A((lambda e, _kw=dict(out=xdT[:, ch, 15:15 + W], in_=ps[b][:, 0:W]): e.copy(**_kw)), [f"ps{b}"], ["xdT"]))
                for tl in range(2):
                    tg = t0 + tl
                    b = tm_cols(1280, 512, hT, hk, hoff + tl * 128)
                    V((lambda e, _kw=dict(out=kvo, in_=ps[b][:]): e.tensor_copy(**_kw)), [f"ps{b}"], ["kvo"])
                    P.dma("sp", (lambda tg: (lambda e, _kw=dict(out=o_kp[l, tg * 128:(tg + 1) * 128, :], in_=kvo[:, 0:256]): e.dma_start(**_kw)))(tg), reads=["kvo"], writes=["o_kp"], sem="st_kvo")
                    P.dma("sp", (lambda tg: (lambda e, _kw=dict(out=o_vp[l, tg * 128:(tg + 1) * 128, :], in_=kvo[:, 256:512]): e.dma_start(**_kw)))(tg), reads=["kvo"], writes=["o_vp"], sem="st_kvo")
                    G((lambda e, _kw=dict(out=V1[:, tg, :, 0:64], in_=kvo[:, 256:512].rearrange("p (h e) -> p h e", h=4)): e.tensor_copy(**_kw)), ["kvo"], ["V1"])
                    b = tm_cols(768, 256, hT, hk, hoff + tl * 128)
                    A((lambda e, _kw=dict(out=vg, in_=ps[b][:, 0:256], func=AF.Gelu): e.activation(**_kw)), [f"ps{b}"], ["vg"])
                    ln_rows(vg, "vg", vn32, "vn32")
                    to_vnz(vn32, "vn32", vnz)
                    b = gb()
                    for ch in range(2):
                        for j in range(2):
                            MM(ps[b][:, ch * 128:(ch + 1) * 128], vnz[:, 2 * ch + j, :], WsT[:, 2 * ch + j, :], j == 0, j == 1, ["vnz", "WsT"], [f"ps{b}"], ch == 1 and j == 1)
                    V((lambda e, _kw=dict(out=sgt, in0=ps[b][:, 0:256].rearrange("p (c t) -> p c t", c=2), in1=BsT[:], op=ALU.add): e.tensor_tensor(**_kw)), [f"ps{b}", "BsT"], ["sgt"])
                    V((lambda e, _kw=dict(out=mixT[:, 2:4, tl * 128:(tl + 1) * 128], in0=sgt, in1=uT[:, :, tl * 128:(tl + 1) * 128], op=ALU.mult): e.tensor_tensor(**_kw)),
                      ["sgt", "uT"], ["mixB"])
                    if tg == NT - 1:
                        b = tm_cols(0, 512, hT, hk, hoff + tl * 128)
                        A((lambda e, _kw=dict(out=tm1, in_=ps[b][:, 256:512], func=AF.Sigmoid): e.activation(**_kw)), [f"ps{b}"], ["tm1"])
                        V((lambda e, _kw=dict(out=tm1, in0=ps[b][:, 0:256], in1=tm1, op=ALU.mult): e.tensor_tensor(**_kw)), [f"ps{b}", "tm1"], ["tm1"])
                        P.dma("sp", (lambda e, _kw=dict(out=o_cp[l], in_=tm1[98:128, :]): e.dma_start(**_kw)), reads=["tm1"], writes=["o_cp"], sem="st_tm1")
                        b = tm_cols(1792, 256, hT, hk, hoff + tl * 128)
                        A((lambda e, _kw=dict(out=tm2, in_=ps[b][:, 0:256]): e.copy(**_kw)), [f"ps{b}"], ["tm2"])
                        P.dma("sp", (lambda e, _kw=dict(out=o_pp[l], in_=tm2[113:128, :]): e.dma_start(**_kw)), reads=["tm2"], writes=["o_pp"], sem="st_tm2")
                if F_HOIST and blk + 1 < NB:
                    pm_prenorm(blk + 1)

                def conv_taps(j0, j1):
                    for j in range(j0, j1):
                        for ch in range(2):
                            if j == 0:
                                V((lambda e, _kw=dict(out=cy[:, ch, :], in0=hconv[:, ch, 0:W], scalar1=vecfm[:, ch, 4:5], scalar2=vecfm[:, ch, 0:1],
                                                                   op0=ALU.mult, op1=ALU.add): e.tensor_scalar(**_kw)), ["hconv", "vecfm"], [f"cy{ch}", "cy"])
                            else:
                                V((lambda e, _kw=dict(out=cy[:, ch, :], in0=hconv[:, ch, j:j + W], scalar=vecfm[:, ch, 4 + j:5 + j], in1=cy[:, ch, :],
                                                                               op0=ALU.mult, op1=ALU.add): e.scalar_tensor_tensor(**_kw)), ["hconv", "vecfm", f"cy{ch}"], [f"cy{ch}"])

                def pool_dve(blk=blk):
                    V((lambda e, _kw=dict(out=pA[:, :, 1:15 + W], in0=xdT[:, :, 1:15 + W], in1=xdT[:, :, 0:14 + W], op=ALU.add): e.tensor_tensor(**_kw)), ["xdT"], ["pA"])
                    V((lambda e, _kw=dict(out=pB[:, :, 3:15 + W], in0=pA[:, :, 3:15 + W], in1=pA[:, :, 1:13 + W], op=ALU.add): e.tensor_tensor(**_kw)), ["pA"], ["pB"])
                    V((lambda e, _kw=dict(out=pA[:, 1, 7:15 + W], in0=pB[:, 1, 7:15 + W], in1=pB[:, 1, 3:11 + W], op=ALU.add): e.tensor_tensor(**_kw)), ["pB", "pA"], ["pA"])
                    V((lambda e, _kw=dict(out=pB[64:128, 1, 15:15 + W], in0=pA[64:128, 1, 15:15 + W], in1=pA[64:128, 1, 7:7 + W], op=ALU.add): e.tensor_tensor(**_kw)), ["pA", "pB"], ["pB"])
                    for ch in range(2):
                        for hf, src in ((0, pA), (1, pB)):
                            lo, hi = hf * 64, hf * 64 + 64
                            V((lambda e, _kw=dict(out=dT[lo:hi, ch, :], in0=src[lo:hi, ch, 15:15 + W], scalar=cst[lo:hi, C_INVW + ch:C_INVW + ch + 1],
                                                                                             in1=xdT[lo:hi, ch, 15:15 + W], op0=ALU.mult, op1=ALU.subtract): e.scalar_tensor_tensor(**_kw)), ["pA", "pB", "xdT", "cst"], ["dT"])
                            if blk == 0:
                                c16 = sc(16)
                                V((lambda e, _kw=dict(out=stt[lo:hi, c16:c16 + 16], in0=src[lo:hi, ch, 15:31],
                                                                                                   in1=cst[lo:hi, C_ICNT + ch * 16:C_ICNT + ch * 16 + 16], op=ALU.mult): e.tensor_tensor(**_kw)), ["pA", "pB", "cst"], ["stt"])
                                V((lambda e, _kw=dict(out=dT[lo:hi, ch, 0:16], in0=stt[lo:hi, c16:c16 + 16], in1=xdT[lo:hi, ch, 15:31], op=ALU.subtract): e.tensor_tensor(**_kw)),
                                  ["stt", "xdT", "dT"], ["dT"])
                    G((lambda e, _kw=dict(out=xdT[:, :, 0:15], in_=xdT[:, :, W:W + 15]): e.tensor_copy(**_kw)), ["xdT", "pA", "pB"], ["xdT"])

                def pool_mm():
                    for ch in range(2):
                        b = gb()
                        MM(ps[b][:, 0:W], Wp[:, ch, :], dT[:, ch, :], True, True, ["Wp", "dT"], [f"ps{b}"], True)
                        V((lambda e, _kw=dict(out=mixT[:, 6 + ch, :], in0=ps[b][:, 0:W], scalar1=vecfm[:, ch, 3:4], scalar2=None, op0=ALU.mult): e.tensor_scalar(**_kw)),
                          [f"ps{b}", "vecfm"], ["mixD"])

                def conv_finish():
                    G((lambda e, _kw=dict(out=hconv[:, :, 0:30], in_=hconv[:, :, W:W + 30]): e.tensor_copy(**_kw)), ["hconv"], ["hconv"])
                    P.op("pool", lambda e: e.nop(), ["cy0", "cy1"], ["cy"])
                    conv_ln_silu(cy, W, ybf, ysq, cmean, cm2, crstd, mixT, "mixA", 0)

                side = {0: [lambda: conv_taps(0, 10)], 1: [pool_dve, lambda: conv_taps(10, 17)], 2: [pool_mm, lambda: conv_taps(17, 24)],
                        3: [lambda: conv_taps(24, 31), conv_finish]}

                qb0, qb1 = 2 * blk, 2 * blk + 1
                items = [(h, kb) for h in range(4) for kb in range(qb1 + 1)]
                pinfo = {}

                def att_s(n, h, kb):
                    r0 = (h % 2) * 64
                    b = gb()
                    MM(ps[b][:], kT_all[r0:r0 + 64, h // 2, kb * 128:(kb + 1) * 128], Qbd[r0:r0 + 64, h // 2, :], True, True, ["kT_all", "Qbd"], [f"ps{b}"], True)
                    pi = n % NPT
                    pinfo[n] = pi
                    dd = qb1 - kb
                    A((lambda e, _kw=dict(out=pT[pi], in_=ps[b][:], func=AF.Exp, scale=ISQ, bias=cst[:, C_BT + h * 16 + dd:C_BT + h * 16 + dd + 1]): e.activation(**_kw)),
                      [f"ps{b}", "cst"], [f"pT{pi}"])
                    if kb >= qb0:
                        mo = B_MA if kb == qb0 else B_MB
                        G((lambda e, _kw=dict(out=pT[pi], in0=pT[pi], in1=cstb[:, mo:mo + 512], op=ALU.mult): e.tensor_tensor(**_kw)), [f"pT{pi}", "cstb"], [f"pT{pi}"])

                def att_pv(n, h, kb):
                    bo = 6 + (h % 2)
                    pi = pinfo[n]
                    if kb == 0:
                        MM(ps[bo][:], zer[:, 0:128], zer[:], True, True, ["zer"], [f"ps{bo}"], False)
                    for c in range(2):
                        for qi in range(2):
                            if kb <= qb0 + qi:
                                a_ = c * 2 + qi
                                last = (kb == qb0 + qi)
                                MM(ps[bo][:, a_ * 128:a_ * 128 + 65], pT[pi][:, c * 256 + qi * 128:c * 256 + qi * 128 + 128], V1[:, kb, h, :], False, last,
                                   [f"pT{pi}", "V1"], [f"ps{bo}"], last and c == 1)
                    if kb == qb1:
                        if F_SIDE != 3:
                            for fn_ in side[h]:
                                fn_()
                        for qi in range(2):
                            a0, a1 = qi, 2 + qi
                            attn_finish(ps[bo][:, a0 * 128:a0 * 128 + 64], ps[bo][:, a0 * 128 + 64:a0 * 128 + 65],
                                        ps[bo][:, a1 * 128:a1 * 128 + 64], ps[bo][:, a1 * 128 + 64:a1 * 128 + 65], [f"ps{bo}"],
                                        obuf[:, qi, h * 64:(h + 1) * 64], f"obuf{qi}")
                        if F_SIDE == 3:
                            for fn_ in side[h]:
                                fn_()

                post_side = []
                if not F_SIDE:
                    for h_ in range(4):
                        for fn_ in side[h_]:
                            fn_()
                        side[h_] = []
                elif F_SIDE == 3:
                    side = {0: [lambda: conv_taps(0, 12)], 1: [pool_dve, lambda: conv_taps(12, 22)], 2: [lambda: conv_taps(22, 31)], 3: []}
                    post_side = [pool_mm, conv_finish]
                elif F_SIDE == 2:
                    side = {0: [lambda: conv_taps(0, 10)], 1: [pool_dve, lambda: conv_taps(10, 17)], 2: [lambda: conv_taps(17, 24)],
                            3: [lambda: conv_taps(24, 31)]}
                    post_side = [pool_mm, conv_finish]
                if F_SKEW:
                    for n in range(len(items) + 1):
                        if n < len(items):
                            att_s(n, *items[n])
                        if n >= 1:
                            att_pv(n - 1, *items[n - 1])
                else:
                    for n in range(len(items)):
                        att_s(n, *items[n])
                        att_pv(n, *items[n])
                for fn_ in post_side:
                    fn_()
                for qi in range(2):
                    head_norm_to_mix(obuf[:, qi, :], f"obuf{qi}", osq, ycb, lam_init, mixT, "mixC", qi * 128)
                for tl in range(2):
                    out_proj_tile(mixT, ["mixA", "mixB", "mixC", "mixD"], tl * 128, x[:, t0 + tl, :], f"x{t0 + tl}")

            P.scope = "S2"
            phase_barrier()
            TA.reset()
            W = 128
            hTs = TA.get("hT", (8, 128), BF16)
            mixT = TA.get("mixT", (8, W), BF16)
            sg = TA.get("sg", (2, W), F32)
            hs32 = TA.get("hs32", (2, 32, 34), F32)
            cy = TA.get("cy", (2, W), F32)
            ybf = TA.get("ybf", (2, W), BF16)
            ysq = TA.get("ysq", (2, W), BF16)
            cmean = TA.get("cmean", (W,), F32)
            cm2 = TA.get("cm2", (W,), F32)
            crstd = TA.get("crstd", (W,), F32)
            uT = TA.get("uT", (2, W), BF16)
            vg = TA.get("vg", (256,), F32)
            vn32 = TA.get("vn32", (256,), F32)
            vnz = TA.get("vnz", (4, 128), BF16)
            sgt = TA.get("sgt", (2, 128), F32)
            xds = TA.get("xds", (2, 32, 19), F32)
            sA = TA.get("pA", (2, 32, 19), F32)
            sB = TA.get("pB", (2, 32, 19), F32)
            dT = TA.get("dT", (2, W), BF16)
            R = TA.get("R", (8, 65), F32)
            R8 = TA.get("R8", (NCORES, 8 * 65), F32)
            obuf = TA.get("obuf", (256,), F32)
            osq = TA.get("osq", (256,), F32)
            ycb = TA.get("ycb", (256,), BF16)
            tm1 = TA.get("tm1", (256,), F32)
            tm2 = TA.get("tm2", (256,), F32)
            ARK.update(TA.keys)
            G((lambda e, _kw=dict(ap=vnz, constant=0.0): e.memset(**_kw)), [], ["vnz"])
            for ch in range(2):
                P.dma("sp", (lambda ch: (lambda e, _kw=dict(out=hs32[:, ch, :, 0:30], in_=sconv_fm_d[l, ch * 128:(ch + 1) * 128]): e.dma_start(**_kw)))(ch), writes=["hs32"], sem="hs32")
                P.dma("sp", (lambda ch: (lambda e, _kw=dict(out=xds[:, ch, :, 0:15], in_=spool_fm_d[l, ch * 128:(ch + 1) * 128]): e.dma_start(**_kw)))(ch), writes=["xds"], sem="xds")
            P.dma("sp", (lambda e, _kw=dict(out=o_cs[l, :, 0:26, :], in_=sconv_raw_d[l, :, 4:30, :]): e.dma_start(**_kw)), writes=["o_cs"], sem="st_d2d")
            P.dma("sp", (lambda e, _kw=dict(out=o_ps[l, :, 0:11, :], in_=spool_raw_d[l, :, 4:15, :]): e.dma_start(**_kw)), writes=["o_ps"], sem="st_d2d")
            pv_ = part_d.rearrange("c (b t a) e -> b t c (a e)", t=4, a=8)
            for t in range(4):
                P.dma("sp", (lambda t: (lambda e, _kw=dict(out=R8[32 * t:32 * t + 32, :, :], in_=pv_[:, t, :, :]): e.dma_start(**_kw)))(t),
                      writes=["R8"], sem="R8")
            Rf = R.rearrange("p a e -> p (a e)")
            V((lambda e, _kw=dict(out=Rf, in0=R8[:, 0, :], in1=R8[:, 1, :], op=ALU.add): e.tensor_tensor(**_kw)), ["R8"], ["R"])
            for c_ in range(2, NCORES):
                V((lambda e, _kw=dict(out=Rf, in0=Rf, in1=R8[:, c_, :], op=ALU.add): e.tensor_tensor(**_kw)), ["R8", "R"], ["R"])
            prenorm_tile(xs[:], "xs", 0, hTs, "hT", 0)
            bt = lambda ap: ap.rearrange("p (t b) -> p b t", t=4)
            for ch in range(2):
                fm_chunk(256 + ch * 128, hTs, "hT", W,
                         lambda b, ch=ch: A((lambda e, _kw=dict(out=sg[:, ch, :], in_=ps[b][:, 0:W], func=AF.Sigmoid): e.activation(**_kw)), [f"ps{b}"], ["sg"]))
            for ch in range(2):
                fm_chunk(ch * 128, hTs, "hT", W,
                         lambda b, ch=ch: V((lambda e, _kw=dict(out=hs32[:, ch, :, 30:34], in0=bt(ps[b][:, 0:W]), in1=bt(sg[:, ch, :]), op=ALU.mult): e.tensor_tensor(**_kw)),
                                            [f"ps{b}", "sg"], ["hs32"]))
            for ch in range(2):
                fm_chunk(512 + ch * 128, hTs, "hT", W,
                         lambda b, ch=ch: A((lambda e, _kw=dict(out=uT[:, ch, :], in_=ps[b][:, 0:W], func=AF.Gelu): e.activation(**_kw)), [f"ps{b}"], ["uT"]))
            for ch in range(2):
                fm_chunk(1792 + ch * 128, hTs, "hT", W,
                         lambda b, ch=ch: A((lambda e, _kw=dict(out=xds[:, ch, :, 15:19], in_=bt(ps[b][:, 0:W])): e.copy(**_kw)), [f"ps{b}"], ["xds"]))
            b = tm_cols(0, 512, hTs, "hT", 0)
            A((lambda e, _kw=dict(out=tm1, in_=ps[b][:, 256:512], func=AF.Sigmoid): e.activation(**_kw)), [f"ps{b}"], ["tm1"])
            V((lambda e, _kw=dict(out=tm1, in0=ps[b][:, 0:256], in1=tm1, op=ALU.mult): e.tensor_tensor(**_kw)), [f"ps{b}", "tm1"], ["tm1"])
            b = tm_cols(1792, 256, hTs, "hT", 0)
            A((lambda e, _kw=dict(out=tm2, in_=ps[b][:, 0:256]): e.copy(**_kw)), [f"ps{b}"], ["tm2"])
            b = tm_cols(768, 256, hTs, "hT", 0)
            A((lambda e, _kw=dict(out=vg, in_=ps[b][:, 0:256], func=AF.Gelu): e.activation(**_kw)), [f"ps{b}"], ["vg"])
            ln_rows(vg, "vg", vn32, "vn32")
            for t in range(4):
                P.dma("sp", (lambda t: (lambda e, _kw=dict(out=o_cs[l, :, 26 + t, :], in_=tm1[32 * t:32 * t + 32, :]): e.dma_start(**_kw)))(t), reads=["tm1"], writes=["o_cs"], sem="st_tm1")
                P.dma("sp", (lambda t: (lambda e, _kw=dict(out=o_ps[l, :, 11 + t, :], in_=tm2[32 * t:32 * t + 32, :]): e.dma_start(**_kw)))(t), reads=["tm2"], writes=["o_ps"], sem="st_tm2")
            P.dma("sp", (lambda e, _kw=dict(out=o_gs[l], in_=vn32): e.dma_start(**_kw)), reads=["vn32"], writes=["o_gs"], sem="st_vn32")
            to_vnz(vn32, "vn32", vnz)
            b = gb()
            for ch in range(2):
                for j in range(2):
                    MM(ps[b][:, ch * 128:(ch + 1) * 128], vnz[:, 2 * ch + j, :], WsS[:, 2 * ch + j, :], j == 0, j == 1, ["vnz", "WsS"], [f"ps{b}"], ch == 1 and j == 1)
            V((lambda e, _kw=dict(out=sgt.rearrange("p c (t b) -> p c t b", t=4), in0=ps[b][:, 0:256].rearrange("p (c t b) -> p c t b", c=2, t=4),
                                             in1=BsT[:, :, 0:4].unsqueeze(3).to_broadcast([128, 2, 4, 32]), op=ALU.add): e.tensor_tensor(**_kw)), [f"ps{b}", "BsT"], ["sgt"])
            V((lambda e, _kw=dict(out=mixT[:, 2:4, :], in0=sgt, in1=uT, op=ALU.mult): e.tensor_tensor(**_kw)), ["sgt", "uT"], ["mixB"])
            cy4 = [cy[:, ch, :].rearrange("p (t b) -> p b t", t=4) for ch in range(2)]
            for j in range(31):
                for ch in range(2):
                    if j == 0:
                        V((lambda e, _kw=dict(out=cy4[ch], in0=hs32[:, ch, :, 0:4], scalar1=vecfm[:, ch, 4:5], scalar2=vecfm[:, ch, 0:1], op0=ALU.mult, op1=ALU.add): e.tensor_scalar(**_kw)),
                          ["hs32", "vecfm"], [f"cy{ch}", "cy"])
                    else:
                        V((lambda e, _kw=dict(out=cy4[ch], in0=hs32[:, ch, :, j:j + 4], scalar=vecfm[:, ch, 4 + j:5 + j], in1=cy4[ch], op0=ALU.mult, op1=ALU.add): e.scalar_tensor_tensor(**_kw)),
                          ["hs32", "vecfm", f"cy{ch}"], [f"cy{ch}"])
            P.op("pool", lambda e: e.nop(), ["cy0", "cy1"], ["cy"])
            conv_ln_silu(cy, W, ybf, ysq, cmean, cm2, crstd, mixT, "mixA", 0)
            V((lambda e, _kw=dict(out=sA[:, :, :, 1:19], in0=xds[:, :, :, 1:19], in1=xds[:, :, :, 0:18], op=ALU.add): e.tensor_tensor(**_kw)), ["xds"], ["pA"])
            V((lambda e, _kw=dict(out=sB[:, :, :, 3:19], in0=sA[:, :, :, 3:19], in1=sA[:, :, :, 1:17], op=ALU.add): e.tensor_tensor(**_kw)), ["pA"], ["pB"])
            V((lambda e, _kw=dict(out=sA[:, 1, :, 7:19], in0=sB[:, 1, :, 7:19], in1=sB[:, 1, :, 3:15], op=ALU.add): e.tensor_tensor(**_kw)), ["pB", "pA"], ["pA"])
            V((lambda e, _kw=dict(out=sB[64:128, 1, :, 15:19], in0=sA[64:128, 1, :, 15:19], in1=sA[64:128, 1, :, 7:11], op=ALU.add): e.tensor_tensor(**_kw)), ["pA", "pB"], ["pB"])
            for ch in range(2):
                for hf, src in ((0, sA), (1, sB)):
                    lo, hi = hf * 64, hf * 64 + 64
                    V((lambda e, _kw=dict(out=dT[lo:hi, ch, :].rearrange("p (t b) -> p b t", t=4), in0=src[lo:hi, ch, :, 15:19],
                                                                                     scalar=cst[lo:hi, C_INVW + ch:C_INVW + ch + 1], in1=xds[lo:hi, ch, :, 15:19],
                                                                                     op0=ALU.mult, op1=ALU.subtract): e.scalar_tensor_tensor(**_kw)), ["pA", "pB", "xds", "cst"], ["dT"])
            for ch in range(2):
                b = gb()
                MM(ps[b][:, 0:W], Wp[:, ch, :], dT[:, ch, :], True, True, ["Wp", "dT"], [f"ps{b}"], True)
                V((lambda e, _kw=dict(out=mixT[:, 6 + ch, :], in0=ps[b][:, 0:W], scalar1=vecfm[:, ch, 3:4], scalar2=None, op0=ALU.mult): e.tensor_scalar(**_kw)), [f"ps{b}", "vecfm"], ["mixD"])
            for h in range(4):
                attn_finish(R[:, 2 * h, 0:64], R[:, 2 * h, 64:65], R[:, 2 * h + 1, 0:64], R[:, 2 * h + 1, 64:65], ["R"], obuf[:, h * 64:(h + 1) * 64], "obuf0")
            head_norm_to_mix(obuf, "obuf0", osq, ycb, lam_init, mixT, "mixC", 0)
            if DBG and l == 0:
                dbgt = TA.get("dbgt", (8, 128), F32)
                ARK.update(TA.keys)
                G((lambda e, _kw=dict(out=dbgt, in_=mixT): e.tensor_copy(**_kw)), ["mixA", "mixB", "mixC", "mixD"], ["dbgt"])
                P.dma("sp", (lambda e, _kw=dict(out=o_dbg, in_=dbgt): e.dma_start(**_kw)), reads=["dbgt"], writes=["o_dbg"], sem="st_dbg")
            out_proj_tile(mixT, ["mixA", "mixB", "mixC", "mixD"], 0, xs[:], "xs")

            P.scope = "F"
            phase_barrier(["w_in", "w_out", "w_down"])
            TA.reset()
            hT = TA.get("hT", (8, 512), BF16)
            hTs = TA.get("hTs", (8, 128), BF16)
            NW = 3
            wgu = [TA.get(f"wgu{i}", (8, 256), BF16) for i in range(NW)]
            sgb = [TA.get(f"sgb{i}", (512,), F32) for i in range(2)]
            actT = TA.get("actT", (NFC, 512), BF16)
            actTs = TA.get("actTs", (NFC, 128), BF16)
            ARK.update(TA.keys)
            wd_v = w_down_d[l].rearrange("(fc p) n -> p fc n", p=128)
            for hf in range(2):
                P.dma("pool", (lambda hf: (lambda e, _kw=dict(out=w_down_sb[:, hf * 11:(hf + 1) * 11, :], in_=wd_v[:, hf * 11:(hf + 1) * 11, :]): e.dma_start(**_kw)))(hf),
                      writes=["w_down"], sem="w_down")
            P.dma("sp", (lambda e, _kw=dict(out=gpost[:], in_=gpost_d[l, 1].partition_broadcast(128)): e.dma_start(**_kw)), writes=["gpost"])
            prenorm_tile(xs[:], "xs", 1, hTs, "hTs", 0)
            wcount = 0
            for fb in range(NF):
                last_blk = (fb == NF - 1)
                for tl in range(4):
                    prenorm_tile(x[:, 4 * fb + tl, :], f"x{4 * fb + tl}", 1, hT, "hT", tl * 128)
                for f_ in range(NFC):
                    wi = wcount % NW
                    wcount += 1
                    P.dma("pool", (lambda wi, f_: (lambda e, _kw=dict(out=wgu[wi], in_=w_gu_d[l, f_]): e.dma_start(**_kw)))(wi, f_), writes=[f"wgu{wi}"], sem=f"wgu{wi}")
                    bg, bu = gb(), gb()
                    for kc in range(8):
                        MM(ps[bg][:], wgu[wi][:, kc, 0:128], hT[:, kc, :], kc == 0, kc == 7, [f"wgu{wi}", "hT"], [f"ps{bg}"], kc == 7)
                    for kc in range(8):
                        MM(ps[bu][:], wgu[wi][:, kc, 128:256], hT[:, kc, :], kc == 0, kc == 7, [f"wgu{wi}", "hT"], [f"ps{bu}"], kc == 7)
                    si = f_ % 2
                    A((lambda e, _kw=dict(out=sgb[si], in_=ps[bg][:], func=AF.Silu): e.activation(**_kw)), [f"ps{bg}"], [f"sgb{si}"])
                    V((lambda e, _kw=dict(out=actT[:, f_, :], in0=ps[bu][:], in1=sgb[si], op=ALU.mult): e.tensor_tensor(**_kw)), [f"ps{bu}", f"sgb{si}"], ["actT"])
                    if last_blk:
                        bg = gb()
                        for kc in range(8):
                            MM(ps[bg][:, 0:128], wgu[wi][:, kc, 0:128], hTs[:, kc, :], kc == 0, kc == 7, [f"wgu{wi}", "hTs"], [f"ps{bg}"], False)
                        for kc in range(8):
                            MM(ps[bg][:, 128:256], wgu[wi][:, kc, 128:256], hTs[:, kc, :], kc == 0, kc == 7, [f"wgu{wi}", "hTs"], [f"ps{bg}"], kc == 7)
                        A((lambda e, _kw=dict(out=sgb[si][:, 0:128], in_=ps[bg][:, 0:128], func=AF.Silu): e.activation(**_kw)), [f"ps{bg}"], [f"sgb{si}"])
                        V((lambda e, _kw=dict(out=actTs[:, f_, :], in0=ps[bg][:, 128:256], in1=sgb[si][:, 0:128], op=ALU.mult): e.tensor_tensor(**_kw)),
                          [f"ps{bg}", f"sgb{si}"], ["actTs"])

                def down_tile(aT, akey, mcol, xap, xkey):
                    bA, bB = gb(), gb()
                    for n, bk in enumerate((bA, bB)):
                        for f2 in range(NFC):
                            MM(ps[bk][:], aT[:, f2, mcol:mcol + 128], w_down_sb[:, f2, n * 512:(n + 1) * 512], f2 == 0, f2 == NFC - 1, [akey, "w_down"], [f"ps{bk}"], f2 == NFC - 1)
                    postnorm_add(bA, bB, xap, xkey)
                for tl in range(4):
                    down_tile(actT, "actT", tl * 128, x[:, 4 * fb + tl, :], f"x{4 * fb + tl}")
                if last_blk:
                    down_tile(actTs, "actTs", 0, xs[:], "xs")

        P.scope = "final"
        if doB:
            yv = o_yp.rearrange("(t p) d -> p t d", p=128)
            for t0 in range(0, NT, 4):
                P.dma("sp", (lambda t0: (lambda e, _kw=dict(out=yv[:, t0:t0 + 4, :], in_=x[:, t0:t0 + 4, :]): e.dma_start(**_kw)))(t0),
                      reads=[f"x{t}" for t in range(t0, t0 + 4)], writes=["o_yp"], sem="st_x")
            P.dma("sp", (lambda e, _kw=dict(out=o_ys, in_=xs[:]): e.dma_start(**_kw)), reads=["xs"], writes=["o_ys"], sem="st_xs")
        if doA:
            P.scope = "S1"
            phase_barrier(["w_in", "w_out", "w_down"])
            P.dma("pool", (lambda e, _kw=dict(out=w_in_sb[:, :, 1024:1792], in_=w_in_a_d.rearrange("(kc p) n -> p kc n", p=128)): e.dma_start(**_kw)), writes=["w_in"], sem="w_in")
            P.dma("sp", (lambda e, _kw=dict(out=gfm[:], in_=gfm_a_d): e.dma_start(**_kw)), writes=["gfm"])
            phase_barrier()
            TA.reset()
            hTs = TA.get("hT", (8, 128), BF16)
            Qblk = TA.get("Qblk", (2, 32, 32), BF16)
            kTs = TA.get("kTs", (2, 128), BF16)
            Vnew = TA.get("Vnew", (258,), BF16)
            kvo = TA.get("kvo", (512,), F32)
            NKV, NKT, NSX = 8, 3, 4
            kvs = [TA.get(f"kvs{i}", (514,), BF16) for i in range(NKV)]
            ktT = [TA.get(f"ktT{i}", (256,), BF16) for i in range(NKT)]
            Sx = [TA.get(f"Sx{i}", (32,), F32) for i in range(NSX)]
            pTs = [TA.get(f"pTs{i}", (32,), BF16) for i in range(NSX)]
            msk = TA.get("msk", (256,), F32)
            stg = [TA.get(f"stg{i}", (66,), F32) for i in range(2)]
            ARK.update(TA.keys)
            G((lambda e, _kw=dict(ap=Qblk, constant=0.0): e.memset(**_kw)), [], ["Qblk"])
            G((lambda e, _kw=dict(ap=Vnew[:, 256:258], constant=0.125): e.memset(**_kw)), [], ["Vnew"])
            for i in range(NKV):
                G((lambda e, _kw=dict(ap=kvs[i][:, 512:514], constant=1.0): e.memset(**_kw)), [], [f"kvs{i}"])
            prenorm_tile(xs[:], "xs", 0, hTs, "hT", 0)
            for chk in range(2):
                def ev_q(b, chk=chk):
                    pv = ps[b][:, 0:128].rearrange("p (t b) -> p b t", t=4)
                    for h2 in range(2):
                        for c in range(2):
                            h = 2 * chk + h2
                            V((lambda e, _kw=dict(out=Qblk[:, chk, :, :].rearrange("p b (t x) -> p b t x", t=4)[:, :, :, h * 2 + c],
                                                                          in0=pv, scalar1=cst[:, C_MH2C + h2 * 2 + c:C_MH2C + h2 * 2 + c + 1], scalar2=None, op0=ALU.mult): e.tensor_scalar(**_kw)),
                              [f"ps{b}", "cst"], ["Qblk"])
                fm_chunk(1024 + chk * 128, hTs, "hT", 128, ev_q)
            for chk in range(2):
                fm_chunk(1280 + chk * 128, hTs, "hT", 128,
                         lambda b, chk=chk: A((lambda e, _kw=dict(out=kTs[:, chk, :], in_=ps[b][:, 0:128]): e.copy(**_kw)), [f"ps{b}"], ["kTs"]))
            b = tm_cols(1280, 512, hTs, "hT", 0)
            V((lambda e, _kw=dict(out=kvo, in_=ps[b][:]): e.tensor_copy(**_kw)), [f"ps{b}"], ["kvo"])
            P.dma("sp", (lambda e, _kw=dict(out=o_ks, in_=kvo[:, 0:256]): e.dma_start(**_kw)), reads=["kvo"], writes=["o_ks"], sem="st_kvo")
            P.dma("sp", (lambda e, _kw=dict(out=o_vs, in_=kvo[:, 256:512]): e.dma_start(**_kw)), reads=["kvo"], writes=["o_vs"], sem="st_kvo")
            G((lambda e, _kw=dict(out=Vnew[:, 0:256], in0=kvo[:, 256:512], scalar1=0.125, scalar2=None, op0=ALU.mult): e.tensor_scalar(**_kw)), ["kvo"], ["Vnew"])
            cin_ap = o_part
            kvl = kv_d
            tiles = []
            tcount = 0
            for bb in range(32):
                for t in range(T8 + 1):
                    td = dict(bb=bb, t=t, new=(t == T8), n=len(tiles))
                    if not td["new"]:
                        td["sl"] = tcount % NKV
                        td["kr"] = tcount % NKT
                        tcount += 1
                    tiles.append(td)

            def stage_a(td):
                if td["new"]:
                    return
                sl_, kr, col = td["sl"], td["kr"], td["bb"] * T8 + td["t"]
                P.dma("pool", (lambda sl_, col: (lambda e, _kw=dict(out=kvs[sl_][:, 0:512], out_offset=None, in_=kvl,
                                                                                 in_offset=bass.IndirectOffsetOnAxis(ap=idx[:, col:col + 1], axis=0)): e.indirect_dma_start(**_kw)))(sl_, col),
                      reads=["idx"], writes=[f"kvs{sl_}"], sem=f"kvs{sl_}")
                b1 = gb()
                for hh in range(2):
                    TR(psb[b1][:, hh * 128:(hh + 1) * 128], kvs[sl_][:, hh * 128:(hh + 1) * 128], idb, [f"kvs{sl_}", "cstb"], [f"ps{b1}"], hh == 1)
                A((lambda e, _kw=dict(out=ktT[kr], in_=psb[b1][:, 0:256]): e.copy(**_kw)), [f"ps{b1}"], [f"ktT{kr}"])

            def stage_b(td):
                bb, t = td["bb"], td["t"]
                if not td["new"]:
                    kr = td["kr"]
                    lh = [ktT[kr][:, 0:128], ktT[kr][:, 128:256]]
                    lk = [f"ktT{kr}"]
                    bias_ap = cst[:, C_BIAS + t * 32:C_BIAS + (t + 1) * 32]
                    bkey = "cst"
                else:
                    lh = [kTs[:, 0, :], kTs[:, 1, :]]
                    lk = ["kTs"]
                    bias_ap = cstb[:, B_BN + bb * 32:B_BN + (bb + 1) * 32]
                    bkey = "cstb"
                b2 = gb()
                for hh in range(2):
                    MM(ps[b2][:, 0:32], lh[hh], Qblk[:, hh, bb, :], hh == 0, hh == 1, lk + ["Qblk"], [f"ps{b2}"], hh == 1)
                si = td["n"] % NSX
                V((lambda e, _kw=dict(out=Sx[si], in0=ps[b2][:, 0:32], scalar=ISQ, in1=bias_ap, op0=ALU.mult, op1=ALU.add): e.scalar_tensor_tensor(**_kw)),
                  [f"ps{b2}", bkey], [f"Sx{si}"])
                A((lambda e, _kw=dict(out=pTs[si], in_=Sx[si], func=AF.Exp): e.activation(**_kw)), [f"Sx{si}"], [f"pTs{si}"])

            def stage_c(td):
                bb, t, new = td["bb"], td["t"], td["new"]
                bo_s = 6 + (bb % 2)
                si = td["n"] % NSX
                if not new:
                    rhs_v = kvs[td["sl"]][:, 256:513]
                    rk = [f"kvs{td['sl']}"]
                else:
                    rhs_v = Vnew[:, 0:257]
                    rk = ["Vnew"]
                MM(ps[bo_s][0:32, 0:257], pTs[si], rhs_v, t == 0, new, [f"pTs{si}"] + rk, [f"ps{bo_s}"], new)
                if not new:
                    return
                sg_i = bb % 2
                V((lambda e, _kw=dict(out=msk[0:32, :], in0=ps[bo_s][0:32, 0:256], in1=cst[0:32, C_HM:C_HM + 256], op=ALU.mult): e.tensor_tensor(**_kw)),
                  [f"ps{bo_s}", "cst"], ["msk"])
                V((lambda e, _kw=dict(out=stg[sg_i][0:32, 0:64], in_=msk[0:32, :].rearrange("p (h e) -> p e h", h=4), axis=AX.X, op=ALU.add): e.tensor_reduce(**_kw)),
                  ["msk"], [f"stg{sg_i}"])
                A((lambda e, _kw=dict(out=stg[sg_i][0:32, 64:65], in_=ps[bo_s][0:32, 256:257]): e.copy(**_kw)), [f"ps{bo_s}"], [f"stg{sg_i}"])
                P.dma("sp", (lambda sg_i, bb: (lambda e, _kw=dict(out=cin_ap[bb * 32:(bb + 1) * 32, :], in_=stg[sg_i][0:32, 0:65]): e.dma_start(**_kw)))(sg_i, bb),
                      reads=[f"stg{sg_i}"], writes=["o_part"], sem=f"st_stg{sg_i}")

            NTL = len(tiles)
            for i in range(NTL + 3):
                if i < NTL:
                    stage_a(tiles[i])
                if 0 <= i - 1 < NTL:
                    stage_b(tiles[i - 1])
                if 0 <= i - 3 < NTL:
                    stage_c(tiles[i - 3])

        P.finish(out_keys)
        nsem = len(P.dsem) + len(P.esem)
        assert nsem < 140, nsem
    return nc


_CACHE = {}


def run(inputs, cfg):
    L, S = cfg["L"], cfg["S"]
    inp = {k: np.asarray(v) for k, v in inputs.items()}
    st = host_static(inp, cfg)
    state = {"xp": [inp["x_prompt"][c] for c in range(NCORES)],
             "xs": inp["x_sample"].transpose(1, 0, 2).reshape(128, D), "part": None}
    acc = {k: [None] * L for k in ("kp", "vp", "cp", "pp", "ks", "vs", "cs", "ps", "gs")}
    for stage in range(L + 1):
        doA, doB = stage < L, stage >= 1
        key = (tuple(sorted(cfg.items())), doA, doB)
        if key not in _CACHE:
            _CACHE[key] = build(cfg, stage)
        nc = _CACHE[key]
        maps = stage_maps(inp, cfg, stage, st, state)
        res = run_bass_kernel_spmd(nc, maps, core_ids=list(range(NCORES)))
        R_ = res.results
        del maps
        cat = lambda k: np.stack([np.asarray(R_[c][k]) for c in range(NCORES)], axis=0)
        if doB:
            lb = stage - 1
            acc["kp"][lb] = cat("o_kp")[:, 0]
            acc["vp"][lb] = cat("o_vp")[:, 0]
            acc["cp"][lb] = cat("o_cp")[:, 0]
            acc["pp"][lb] = cat("o_pp")[:, 0]
            r0 = R_[0]
            acc["cs"][lb] = np.asarray(r0["o_cs"])[0]
            acc["ps"][lb] = np.asarray(r0["o_ps"])[0]
            acc["gs"][lb] = np.asarray(r0["o_gs"])[0].reshape(4, 32, 256).transpose(1, 0, 2)
            state["xp"] = [np.asarray(R_[c]["o_yp"]) for c in range(NCORES)]
            state["xs"] = np.asarray(r0["o_ys"])
        if doA:
            la = stage
            r0 = R_[0]
            acc["ks"][la] = np.asarray(r0["o_ks"]).reshape(4, 32, 256).transpose(1, 0, 2)
            acc["vs"][la] = np.asarray(r0["o_vs"]).reshape(4, 32, 256).transpose(1, 0, 2)
            state["part"] = cat("o_part")
    y_p = np.stack(state["xp"], axis=0)
    y_s = state["xs"].reshape(4, 32, D).transpose(1, 0, 2)
    stk = lambda k: np.stack(acc[k], axis=0)
    k_p = stk("kp").reshape(L, NCORES, S, 4, 2, 32)
    v_p = stk("vp").reshape(L, NCORES, S, 4, 64)
    k_s = stk("ks").reshape(L, 32, 4, 4, 2, 32)
    v_s = stk("vs").reshape(L, 32, 4, 4, 64)
    outs = (y_p, y_s, k_p, v_p, stk("cp"), stk("pp"), k_s, v_s, stk("cs"), stk("ps"), stk("gs"))
    return tuple(np.ascontiguousarray(o, dtype=np.float32) for o in outs)


def kernel(**inputs):
    cfg = make_cfg()
    return run(inputs, cfg)
```
